# Optimizing a Trainium2 kernel written in Bass

```python
import jax, jax.numpy as jnp
from jax import lax
import numpy as np

D_MODEL = 2048
BATCH = 4
SEQ = 8192
DEPTH = 1

MEM_LEN = 256
NORM_EPS = 1e-6
D_FF = 5632
ATTN_HEADS = 16
ATTN_KV_HEADS = 4
ATTN_GROUP = ATTN_HEADS // ATTN_KV_HEADS
HEAD_DIM = 64
WINDOW = 128
BLOCK = 128
ATTN_WIDTH = ATTN_HEADS * HEAD_DIM
KV_WIDTH = ATTN_KV_HEADS * HEAD_DIM
RWKV_HEADS = 16
RWKV_HEAD_SIZE = 64
RWKV_WIDTH = RWKV_HEADS * RWKV_HEAD_SIZE
DECAY_RANK = 64
AAA_RANK = 64
GATE_RANK = 160
GN_EPS = 64e-5
RWKV_COLS = 3 * RWKV_WIDTH + DECAY_RANK + AAA_RANK + GATE_RANK
GATE_COLS = 2 * D_MODEL
IN_COLS = ATTN_WIDTH + 2 * KV_WIDTH + RWKV_COLS + GATE_COLS
CROSS_HEADS = 4
CROSS_HEAD_DIM = 128
CROSS_WIDTH = CROSS_HEADS * CROSS_HEAD_DIM
NEG_INF = -1e30

kernel_name = "hybrid_swa_sink_rwkv7_gated_macaron"


def rms_norm(x, g):
    xf = x.astype(jnp.float32)
    y = xf * lax.rsqrt(jnp.mean(xf * xf, axis=-1, keepdims=True) + NORM_EPS)
    return (y * g.astype(jnp.float32)).astype(x.dtype)


def swiglu(h, w_gate, w_up, w_down):
    return (jax.nn.silu(h @ w_gate) * (h @ w_up)) @ w_down


def token_shift(p, mu):
    p_prev = jnp.pad(p, ((0, 0), (1, 0), (0, 0)))[:, :-1]
    return p + (p_prev - p) * mu


def sliding_window_attention(q, k, v, sinks):
    B, T = q.shape[0], q.shape[1]
    nb = T // BLOCK
    qb = q.reshape(B, nb, BLOCK, ATTN_KV_HEADS, ATTN_GROUP, HEAD_DIM)

    def band(t):
        cur = t.reshape(B, nb, BLOCK, ATTN_KV_HEADS, HEAD_DIM)
        prev = jnp.concatenate([jnp.zeros_like(cur[:, :1]), cur[:, :-1]], axis=1)
        return jnp.concatenate([prev, cur], axis=2)

    kb, vb = band(k), band(v)
    s = jnp.einsum('bnqhgd,bnkhd->bnhgqk', qb, kb).astype(jnp.float32) * (HEAD_DIM ** -0.5)
    qi = jnp.arange(BLOCK)[:, None]
    ki = jnp.arange(2 * BLOCK)[None, :]
    dist = qi + BLOCK - ki
    blk = jnp.arange(nb)[:, None, None]
    valid = (dist >= 0) & (dist < WINDOW) & (blk * BLOCK + ki - BLOCK >= 0)
    s = jnp.where(valid[None, :, None, None], s, NEG_INF)
    sink = sinks.astype(jnp.float32).reshape(1, 1, ATTN_KV_HEADS, ATTN_GROUP, 1, 1)
    m = jnp.maximum(jnp.max(s, axis=-1, keepdims=True), sink)
    e = jnp.exp(s - m)
    p = e / (jnp.sum(e, axis=-1, keepdims=True) + jnp.exp(sink - m))
    o = jnp.einsum('bnhgqk,bnkhd->bnqhgd', p.astype(v.dtype), vb)
    return o.reshape(B, T, ATTN_WIDTH)


def rwkv7_scan(r, w, k, v, a, b):
    B, T, H, N = r.shape

    def step(S, inp):
        r_t, w_t, k_t, v_t, a_t, b_t = inp
        sa = jnp.einsum('bhij,bhj->bhi', S, a_t)
        S = S * w_t[:, :, None, :] + sa[..., None] * b_t[:, :, None, :] + v_t[..., None] * k_t[:, :, None, :]
        return S, jnp.einsum('bhij,bhj->bhi', S, r_t)

    xs = tuple(jnp.moveaxis(t, 1, 0) for t in (r, w, k, v, a, b))
    _, ys = lax.scan(step, jnp.zeros((B, H, N, N), jnp.float32), xs)
    return jnp.moveaxis(ys, 0, 1)


def rwkv7_time_mix(p, mu, w0, w2, a0, a2, g2, k_k, k_a, r_k, ln_w, ln_b):
    B, T = p.shape[0], p.shape[1]
    f32 = jnp.float32
    xs = token_shift(p, mu).astype(f32)
    c = np.cumsum([RWKV_WIDTH, RWKV_WIDTH, RWKV_WIDTH, DECAY_RANK, AAA_RANK]).tolist()
    r, k, v, xw, xa, xg = jnp.split(xs, c, axis=-1)
    w_log = -jax.nn.softplus(-(w0.astype(f32) + jnp.tanh(xw) @ w2.astype(f32))) - 0.5
    a = jax.nn.sigmoid(a0.astype(f32) + xa @ a2.astype(f32))
    g = jax.nn.sigmoid(xg) @ g2.astype(f32)
    hs = lambda t: t.reshape(B, T, RWKV_HEADS, RWKV_HEAD_SIZE)
    r, k, v, a, decay = hs(r), hs(k), hs(v), hs(a), hs(jnp.exp(-jnp.exp(w_log)))
    k_k, k_a, r_k = k_k.astype(f32), k_a.astype(f32), r_k.astype(f32)
    kk = k * k_k
    kk = kk / jnp.maximum(jnp.sqrt(jnp.sum(kk * kk, axis=-1, keepdims=True)), 1e-12)
    k = k * (1.0 + (a - 1.0) * k_a)
    y = rwkv7_scan(r, decay, k, v, -kk, kk * a)
    mean = jnp.mean(y, axis=-1, keepdims=True)
    var = jnp.mean(jnp.square(y - mean), axis=-1, keepdims=True)
    y = ((y - mean) * lax.rsqrt(var + GN_EPS)).reshape(B, T, RWKV_WIDTH)
    y = y * ln_w.astype(f32) + ln_b.astype(f32)
    bonus = jnp.sum(r * k * r_k, axis=-1, keepdims=True) * v
    y = (y + bonus.reshape(B, T, RWKV_WIDTH)) * g
    return y.astype(p.dtype)


def gated_parallel_mixer(x, norm_g, w_in, sinks, mu, w0, w2, a0, a2, g2, k_k, k_a, r_k,
                         ln_w, ln_b, w_proj_attn, w_proj_rwkv, w_out):
    h = rms_norm(x, norm_g)
    p = h @ w_in
    c = np.cumsum([ATTN_WIDTH, KV_WIDTH, KV_WIDTH, RWKV_COLS]).tolist()
    q, k, v, p_rwkv, gates = jnp.split(p, c, axis=-1)
    y_attn = sliding_window_attention(q, k, v, sinks)
    y_rwkv = rwkv7_time_mix(p_rwkv, mu, w0, w2, a0, a2, g2, k_k, k_a, r_k, ln_w, ln_b)
    gate_attn, gate_rwkv = jnp.split(jax.nn.sigmoid(gates), 2, axis=-1)
    merged = gate_attn * (y_attn @ w_proj_attn) + gate_rwkv * (y_rwkv @ w_proj_rwkv)
    return x + merged @ w_out


def memory_cross_attention(x, mem, norm_g, mem_norm_g, w_q, w_kv, w_o):
    B, T = x.shape[0], x.shape[1]
    h = rms_norm(x, norm_g)
    mn = rms_norm(mem, mem_norm_g)
    q = (h @ w_q).reshape(B, T, CROSS_HEADS, CROSS_HEAD_DIM)
    km, vm = jnp.split(mn @ w_kv, 2, axis=-1)
    km = km.reshape(B, -1, CROSS_HEADS, CROSS_HEAD_DIM)
    vm = vm.reshape(B, -1, CROSS_HEADS, CROSS_HEAD_DIM)
    s = jnp.einsum('bthd,bmhd->bhtm', q, km).astype(jnp.float32) * (CROSS_HEAD_DIM ** -0.5)
    pr = jax.nn.softmax(s, axis=-1).astype(vm.dtype)
    o = jnp.einsum('bhtm,bmhd->bthd', pr, vm).reshape(B, T, CROSS_WIDTH)
    return x + o @ w_o


def setup_inputs(seed: int = 0) -> dict:
    key = jax.random.key(seed)
    ks = iter(jax.random.split(key, 40))
    L = DEPTH
    nrm = lambda shape, scale: jax.random.normal(next(ks), shape, jnp.float32) * scale
    gain = lambda shape: 1.0 + nrm(shape, 0.02)
    return {
        "x": nrm((BATCH, SEQ, D_MODEL), 1.0),
        "mem": nrm((BATCH, MEM_LEN, D_MODEL), 1.0),
        "ffn1_norm": gain((L, D_MODEL)),
        "ffn1_w_gate": nrm((L, D_MODEL, D_FF), D_MODEL ** -0.5),
        "ffn1_w_up": nrm((L, D_MODEL, D_FF), D_MODEL ** -0.5),
        "ffn1_w_down": nrm((L, D_FF, D_MODEL), D_FF ** -0.5),
        "mix_norm": gain((L, D_MODEL)),
        "w_in": nrm((L, D_MODEL, IN_COLS), D_MODEL ** -0.5),
        "attn_sinks": nrm((L, ATTN_HEADS), 0.5),
        "rwkv_mu": jax.random.uniform(next(ks), (L, RWKV_COLS), jnp.float32),
        "rwkv_w0": nrm((L, RWKV_WIDTH), 0.5),
        "rwkv_w2": nrm((L, DECAY_RANK, RWKV_WIDTH), DECAY_RANK ** -0.5),
        "rwkv_a0": nrm((L, RWKV_WIDTH), 0.1),
        "rwkv_a2": nrm((L, AAA_RANK, RWKV_WIDTH), AAA_RANK ** -0.5),
        "rwkv_g2": nrm((L, GATE_RANK, RWKV_WIDTH), GATE_RANK ** -0.5),
        "rwkv_k_k": 0.85 + nrm((L, RWKV_HEADS, RWKV_HEAD_SIZE), 0.02),
        "rwkv_k_a": gain((L, RWKV_HEADS, RWKV_HEAD_SIZE)),
        "rwkv_r_k": nrm((L, RWKV_HEADS, RWKV_HEAD_SIZE), 0.1),
        "rwkv_ln_w": gain((L, RWKV_WIDTH)),
        "rwkv_ln_b": nrm((L, RWKV_WIDTH), 0.02),
        "w_proj_attn": nrm((L, ATTN_WIDTH, D_MODEL), ATTN_WIDTH ** -0.5),
        "w_proj_rwkv": nrm((L, RWKV_WIDTH, D_MODEL), RWKV_WIDTH ** -0.5),
        "w_out": nrm((L, D_MODEL, D_MODEL), D_MODEL ** -0.5),
        "cross_norm": gain((L, D_MODEL)),
        "mem_norm": gain((L, D_MODEL)),
        "w_cross_q": nrm((L, D_MODEL, CROSS_WIDTH), D_MODEL ** -0.5),
        "w_cross_kv": nrm((L, D_MODEL, 2 * CROSS_WIDTH), D_MODEL ** -0.5),
        "w_cross_o": nrm((L, CROSS_WIDTH, D_MODEL), CROSS_WIDTH ** -0.5),
        "ffn2_norm": gain((L, D_MODEL)),
        "ffn2_w_gate": nrm((L, D_MODEL, D_FF), D_MODEL ** -0.5),
        "ffn2_w_up": nrm((L, D_MODEL, D_FF), D_MODEL ** -0.5),
        "ffn2_w_down": nrm((L, D_FF, D_MODEL), D_FF ** -0.5),
        "final_norm": gain((D_MODEL,)),
    }


def reference(x, mem, ffn1_norm, ffn1_w_gate, ffn1_w_up, ffn1_w_down, mix_norm, w_in, attn_sinks,
              rwkv_mu, rwkv_w0, rwkv_w2, rwkv_a0, rwkv_a2, rwkv_g2, rwkv_k_k, rwkv_k_a, rwkv_r_k,
              rwkv_ln_w, rwkv_ln_b, w_proj_attn, w_proj_rwkv, w_out, cross_norm, mem_norm,
              w_cross_q, w_cross_kv, w_cross_o, ffn2_norm, ffn2_w_gate, ffn2_w_up, ffn2_w_down,
              final_norm):
    for l in range(DEPTH):
        x = x + 0.5 * swiglu(rms_norm(x, ffn1_norm[l]), ffn1_w_gate[l], ffn1_w_up[l], ffn1_w_down[l])
        x = gated_parallel_mixer(x, mix_norm[l], w_in[l], attn_sinks[l], rwkv_mu[l], rwkv_w0[l],
                                 rwkv_w2[l], rwkv_a0[l], rwkv_a2[l], rwkv_g2[l], rwkv_k_k[l],
                                 rwkv_k_a[l], rwkv_r_k[l], rwkv_ln_w[l], rwkv_ln_b[l],
                                 w_proj_attn[l], w_proj_rwkv[l], w_out[l])
        x = memory_cross_attention(x, mem, cross_norm[l], mem_norm[l], w_cross_q[l],
                                   w_cross_kv[l], w_cross_o[l])
        x = x + 0.5 * swiglu(rms_norm(x, ffn2_norm[l]), ffn2_w_gate[l], ffn2_w_up[l], ffn2_w_down[l])
    return rms_norm(x, final_norm)
```

```python
import contextlib
import types
import numpy as np
import concourse.bass as bass
import concourse.mybir as mybir
from concourse.bass_utils import run_bass_kernel_spmd

F32 = mybir.dt.float32
BF16 = mybir.dt.bfloat16
AF = mybir.ActivationFunctionType
ALU = mybir.AluOpType

D = 2048; DFF = 5632; TT = 512; NKC = 16
SEQ = 8192; BATCH = 4; MEM = 256
INC = 8992
CDEC = 0.6065306597126334
import os as _os
SAME_ENGINE_SYNC = _os.environ.get("KSES", "1") == "1"

ENGS = ("pe", "act", "dve", "pool", "sp")


class Prog:
    def __init__(self):
        self.ops = []
        self.eng_ops = {e: [] for e in ENGS}
        self.last_w = {}
        self.readers = {}
        self.epoch = 0

    def op(self, eng, fn, reads=(), writes=(), dma=None):
        if fn is not None and fn.__closure__:
            cells = tuple(types.CellType(c.cell_contents) for c in fn.__closure__)
            fn = types.FunctionType(fn.__code__, fn.__globals__, fn.__name__, fn.__defaults__, cells)
        oid = len(self.ops)
        deps = set()
        for k in reads:
            w = self.last_w.get(k)
            if w is not None:
                deps.add(w)
        for k in writes:
            w = self.last_w.get(k)
            if w is not None:
                deps.add(w)
            for r in self.readers.get(k, ()):
                deps.add(r)
        for k in writes:
            self.last_w[k] = oid
            self.readers[k] = []
        ws = set(writes)
        for k in reads:
            if k not in ws:
                self.readers.setdefault(k, []).append(oid)
        deps.discard(oid)
        o = dict(id=oid, eng=eng, fn=fn, deps=deps, dma=dma, epoch=self.epoch, signal=(dma is not None))
        self.ops.append(o)
        self.eng_ops[eng].append(o)
        return oid

    def barrier(self, engs=("pe", "act", "dve", "pool")):
        lasts = {}
        for e in engs:
            for o in reversed(self.eng_ops[e]):
                if o["fn"] is not None and o["dma"] is None:
                    lasts[e] = o["id"]
                    break
        for e in engs:
            d = set(v for k, v in lasts.items() if k != e)
            o = dict(id=len(self.ops), eng=e, fn=None, deps=d, dma=None, epoch=self.epoch, signal=False)
            self.ops.append(o)
            self.eng_ops[e].append(o)

    def finalize(self, nc, stack, handles, final_dma_chans):
        ops = self.ops
        for o in ops:
            for d in o["deps"]:
                od = ops[d]
                if od["dma"] is not None:
                    continue
                if od["eng"] != o["eng"]:
                    od["signal"] = True
                elif SAME_ENGINE_SYNC and o["eng"] in ("act", "dve", "pool"):
                    od["signal"] = True
        sems = {}
        counts = {}

        def getsem(key):
            if key not in sems:
                sems[key] = stack.enter_context(nc.semaphore("s_%s_%s" % key))
                counts[key] = 0
            return sems[key]

        for o in ops:
            if not o["signal"]:
                continue
            if o["dma"] is not None:
                key = ("d" + str(o["dma"]), o["epoch"])
                getsem(key)
                counts[key] += 16
                o["sig"] = (key, counts[key], 16)
            else:
                key = (o["eng"], o["epoch"])
                getsem(key)
                counts[key] += 1
                o["sig"] = (key, counts[key], 1)
        self.nsems = len(sems)
        final_waits = []
        for ch in final_dma_chans:
            for key in sems:
                if key[0] == "d" + str(ch):
                    final_waits.append((key, counts[key]))

        def emit(engname, e):
            waited = {}
            for o in self.eng_ops[engname]:
                need = {}
                for d in o["deps"]:
                    od = ops[d]
                    if "sig" not in od:
                        continue
                    if od["dma"] is None and od["eng"] == engname and not (
                            SAME_ENGINE_SYNC and engname in ("act", "dve", "pool")):
                        continue
                    key, val, _ = od["sig"]
                    if val > need.get(key, 0):
                        need[key] = val
                for key, val in need.items():
                    if waited.get(key, 0) >= val:
                        continue
                    e.wait_ge(sems[key], val)
                    waited[key] = val
                if o["fn"] is None:
                    continue
                inst = o["fn"](e)
                if o["signal"]:
                    key, val, inc = o["sig"]
                    inst.then_inc(sems[key], inc)
            if engname == "sp":
                for key, val in final_waits:
                    e.wait_ge(sems[key], val)

        with nc.Block() as block:
            @block.tensor
            def _(e):
                emit("pe", e)

            @block.scalar
            def _(e):
                emit("act", e)

            @block.vector
            def _(e):
                emit("dve", e)

            @block.gpsimd
            def _(e):
                emit("pool", e)

            @block.sync
            def _(e):
                emit("sp", e)


Q_HEAD_ORDER = []
for _c in range(4):
    Q_HEAD_ORDER += [_c, 4 + _c]
for _c in range(4):
    Q_HEAD_ORDER += [8 + _c, 12 + _c]
Q_PERM = np.concatenate([np.arange(h * 64, (h + 1) * 64) for h in Q_HEAD_ORDER])


def _cols(v):
    v = np.asarray(v, np.float32).reshape(-1)
    n = (len(v) + 127) // 128 * 128
    p = np.zeros(n, np.float32)
    p[:len(v)] = v
    return p.reshape(-1, 128).T


class Vecs:
    def __init__(self):
        self.parts = []
        self.off = {}
        self.n = 0

    def add(self, name, arr):
        arr = np.asarray(arr, np.float32)
        assert arr.shape[0] == 128
        self.off[name] = self.n
        self.parts.append(arr)
        self.n += arr.shape[1]

    def build(self):
        return np.ascontiguousarray(np.concatenate(self.parts, axis=1))


def build_vecs(inp):
    V = Vecs()
    for nm in ("ffn1_norm", "mix_norm", "cross_norm", "ffn2_norm", "final_norm", "mem_norm"):
        V.add(nm, _cols(inp[nm]))
    mu = np.asarray(inp["rwkv_mu"], np.float32).reshape(-1)
    V.add("mu_rkv", _cols(mu[0:3072]))
    V.add("mu_xw", _cols(mu[3072:3136]))
    V.add("mu_xa", _cols(mu[3136:3200]))
    V.add("mu_xg0", _cols(mu[3200:3328]))
    V.add("mu_xg1", _cols(mu[3328:3360]))
    for nm in ("rwkv_w0", "rwkv_a0", "rwkv_k_k", "rwkv_k_a", "rwkv_r_k", "rwkv_ln_w", "rwkv_ln_b"):
        V.add(nm, _cols(inp[nm]))
    sk = np.asarray(inp["attn_sinks"], np.float32).reshape(-1)
    V.add("sinks", np.repeat(sk[None, :], 128, axis=0))
    return V


def build_consts(half=1):
    c = {}
    c["ident"] = np.eye(128, dtype=np.float32)
    c["ones"] = np.ones((128, 128), np.float32)
    bo = np.zeros((128, 128), np.float32)
    bo[:64, :64] = 1.0
    bo[64:, 64:] = 1.0
    c["bones"] = bo
    ki = np.arange(128)[:, None]
    qi = np.arange(128)[None, :]
    c["mcur"] = np.tile((qi >= ki).astype(np.float32), (1, 4))
    c["mprev"] = np.tile((qi < ki).astype(np.float32), (1, 4))
    c["flag"] = np.full((128, 1), float(half), np.float32)
    id64 = np.zeros((128, 512), np.float32)
    id64[:64] = np.tile(np.eye(64, dtype=np.float32), (1, 8))
    c["id64"] = id64
    s = np.arange(64)[:, None]
    t = np.arange(64)[None, :]
    strict = (t > s).astype(np.float32)
    incl = (t >= s).astype(np.float32)
    m = np.block([[strict, incl], [strict, incl]])
    c["mlt"] = np.tile(m, (1, 4))
    mlow = np.zeros((128, 512), np.float32)
    mlow[:64] = np.tile((s > t).astype(np.float32), (1, 8))
    c["mlow"] = mlow
    order = ["ident", "ones", "bones", "mcur", "mprev", "id64", "mlt", "mlow", "flag"]
    off = {}
    o = 0
    for k in order:
        off[k] = o
        o += c[k].shape[1]
    allc = np.concatenate([c[k] for k in order], axis=1)
    return np.ascontiguousarray(allc), off


NCB = 128 * 3 + 512 * 3


WNAMES = [
    ("ffn1_wg", D, DFF), ("ffn1_wu", D, DFF), ("ffn1_wd", DFF, D),
    ("w_in", D, INC), ("w_pa", 1024, D), ("w_pr", 1024, D), ("w_out", D, D),
    ("w_cq", D, 512), ("w_ckv", D, 1024), ("w_co", 512, D),
    ("ffn2_wg", D, DFF), ("ffn2_wu", D, DFF), ("ffn2_wd", DFF, D),
    ("w2", 64, 1024), ("a2", 64, 1024), ("g2", 160, 1024),
]


def build_program(ntiles, vec_off, nvec, c_off, ncst, debug=None, npre=0):
    nc = bass.Bass("TRN2", target_bir_lowering=False)
    ntok = ntiles * TT
    P = Prog()
    stack = contextlib.ExitStack()
    with stack:
        xT_d = nc.dram_tensor("xT", [D, ntok], F32, kind="ExternalInput").ap()
        memT_d = nc.dram_tensor("memT", [D, MEM], F32, kind="ExternalInput").ap()
        vecs_d = nc.dram_tensor("vecs", [128, nvec], F32, kind="ExternalInput").ap()
        cst_d = nc.dram_tensor("consts", [128, ncst], F32, kind="ExternalInput").ap()
        out_d = nc.dram_tensor("outT", [D, (ntiles - npre) * TT], F32, kind="ExternalOutput").ap()
        wf = {}
        wb = {}
        for nm, k, n in WNAMES:
            wf[nm] = nc.dram_tensor(nm, [k, n], F32, kind="ExternalInput").ap()
            wb[nm] = nc.dram_tensor(nm + "_bf", [k, n], BF16, kind="Internal").ap()
        dbg_d = {}
        if debug:
            for nm, shp in debug.items():
                dbg_d[nm] = nc.dram_tensor("dbg_" + nm, list(shp), F32, kind="ExternalOutput").ap()

        def sb(name, shape, dt):
            return stack.enter_context(nc.sbuf_tensor(name, list(shape), dt))

        xT = sb("xT_s", [128, NKC, TT], F32)
        hT = sb("hT_s", [128, NKC, TT], BF16)
        NWB = 3
        wbuf = [sb("wb%d" % i, [128, 8, 512], BF16) for i in range(NWB)]
        vecs = sb("vecs_s", [128, nvec], F32)
        cstf = sb("cstf", [128, ncst - NCB], F32)
        cstb = sb("cstb", [128, NCB], BF16)
        omka = sb("omka", [128, 8], F32)
        esink = sb("esink", [128, 16], F32)
        w2b = sb("w2b", [64, 1024], BF16)
        a2b = sb("a2b", [64, 1024], BF16)
        g2b = sb("g2b", [128, 2, 1024], BF16)
        kmT = sb("kmT", [128, 4, MEM], BF16)
        vm = sb("vm", [128, 2, 512], BF16)
        rstd = sb("rstd", [128, TT], F32)
        carry = sb("carry", [128, 32], F32)
        S0T = sb("S0T", [64, 8, 2, 64], F32)
        S0Tb = sb("S0Tb", [64, 8, 2, 64], BF16)
        ARB = sb("ARB", [64, 8, 2, 64], BF16)
        BKB = sb("BKB", [64, 8, 2, 64], BF16)
        WtotB = sb("WtotB", [64, 8], F32)
        Whl = sb("Whl", [128, 2, 8], BF16)
        kTh = sb("kTh", [128, 2, 128 + TT], BF16)
        vtok = sb("vtok", [128, 5, 256], BF16)
        NACT = 22
        act = sb("act", [128, NACT, TT], BF16)
        _al = act[:, 16:22, :].rearrange("p a b -> p (a b)")
        _alf = _al[:, 0:2056].bitcast(F32)
        Pst = [_alf[:, 0:TT + 1], _alf[:, TT + 1:2 * TT + 2]]
        gbuf = _al[:, 2056:2056 + 512]
        Bt = [sb("B0", [128, TT], BF16), sb("B1", [128, TT], BF16)]
        qT = sb("qT", [128, 8, TT], BF16)
        yr = sb("yr", [128, 8, TT], BF16)
        oT = yr
        xs_small = sb("xs_small", [128, 4, TT], BF16)
        xr = sb("xr", [128, TT], F32)
        xk = sb("xk", [128, TT], F32)
        xv = sb("xv", [128, TT], F32)
        Tt = [sb("T%d" % i, [128, TT], F32) for i in range(7)]
        den = Tt[5]
        ET = [Tt[i][:, :].bitcast(BF16).rearrange("p (a b) -> p a b", a=2) for i in range(2)]
        ysb = Tt[3]
        sqb = Bt[1]
        AR = sb("AR", [128, 8, 2, 64], BF16)
        BK = sb("BK", [128, 8, 2, 64], BF16)
        TM = sb("TM", [128, 8, 3, 64], BF16)
        TMt = sb("TMt", [64, 8, 3, 128], BF16)
        LT = sb("LT", [64, 16, 128], BF16)
        LTk = sb("LTk", [64, 16, 128], BF16)
        Wtot = sb("Wtot", [128, 8], F32)
        inv = {nm: sb("inv_" + nm, [64, 16, 64], BF16) for nm in ("L0", "N0", "L1", "N1", "Q", "P")}
        Zs = sb("Zs", [64, 2, 64], BF16)
        UV = sb("UV", [64, 2, 64], BF16)

        print("SBUF bytes remaining per partition:", nc.sbuf_bytes_remaining)
        ps = [stack.enter_context(nc.psum_tensor("ps%d" % i, [128, 512], F32)) for i in range(8)]
        psctr = [0]
        YB = 7

        def nextps():
            i = psctr[0] % 7
            psctr[0] += 1
            return i

        def vcol(name, j=0, rows=128):
            o = vec_off[name] + j
            return vecs[0:rows, o:o + 1]

        def cb(name, w, rows=128, c0=0):
            o = c_off[name] + c0
            return cstb[0:rows, o:o + w]

        def cf(name, w, rows=128, c0=0):
            o = c_off[name] + c0 - NCB
            return cstf[0:rows, o:o + w]

        for nm, k, n in WNAMES:
            step = 512
            for r0 in range(0, k, step):
                r1 = min(k, r0 + step)
                P.op("pool", lambda e, nm=nm, r0=r0, r1=r1: e.dma_start(out=wb[nm][r0:r1, :], in_=wf[nm][r0:r1, :]),
                     writes=[("wbf", nm)], dma="wc_" + nm)
        P.op("sp", lambda e: e.dma_start(out=vecs[:], in_=vecs_d[:, :]), writes=["vecs"], dma="misc1")
        xflat = xT[:, :, :].rearrange("p a b -> p (a b)")
        P.op("sp", lambda e: e.dma_start(out=xflat[:, 0:ncst], in_=cst_d[:, :]), writes=["xT"], dma="misc2")
        P.op("dve", lambda e: e.tensor_copy(out=cstb[:], in_=xflat[:, 0:NCB]), reads=["xT"], writes=["cst"])
        P.op("dve", lambda e: e.tensor_copy(out=cstf[:], in_=xflat[:, NCB:ncst]), reads=["xT"], writes=["cstf"])
        P.op("dve", lambda e: e.tensor_scalar(out=omka[:], in0=vecs[:, vec_off["rwkv_k_a"]:vec_off["rwkv_k_a"] + 8],
                                              scalar1=-1.0, scalar2=1.0, op0=ALU.mult, op1=ALU.add),
             reads=["vecs"], writes=["omka"])
        P.op("act", lambda e: e.activation(out=esink[:], in_=vecs[:, vec_off["sinks"]:vec_off["sinks"] + 16], func=AF.Exp),
             reads=["vecs"], writes=["esink"])
        P.op("sp", lambda e: e.dma_start(out=w2b[:], in_=wb["w2"][:, :]), reads=[("wbf", "w2")], writes=["w2b"], dma="misc3")
        P.op("sp", lambda e: e.dma_start(out=a2b[:], in_=wb["a2"][:, :]), reads=[("wbf", "a2")], writes=["a2b"], dma="misc4")
        P.op("sp", lambda e: e.dma_start(out=g2b[:, 0, :], in_=wb["g2"][0:128, :]), reads=[("wbf", "g2")], writes=["g2b"], dma="misc5")
        P.op("sp", lambda e: e.dma_start(out=g2b[0:32, 1, :], in_=wb["g2"][128:160, :]), reads=[("wbf", "g2")], writes=["g2b1"], dma="misc6")
        for t_ in (carry, S0T, S0Tb, kTh, vtok):
            P.op("pool", lambda e, t_=t_: e.memset(t_[:], 0.0), writes=["init_" + t_.name])
        INITK = ["init_" + t_.name for t_ in (carry, S0T, S0Tb, kTh, vtok)]

        wslot = [0]

        def load_panel(wname, kc0, nk, c0, ncol):
            s = wslot[0] % NWB
            wslot[0] += 1
            src = wb[wname][kc0 * 128:(kc0 + nk) * 128, c0:c0 + ncol].rearrange("(k p) n -> p k n", p=128)
            P.op("sp", lambda e, s=s, src=src, nk=nk, ncol=ncol: e.dma_start(out=wbuf[s][:, 0:nk, 0:ncol], in_=src),
                 reads=[("wbf", wname)], writes=[("wb", s)], dma="w%d" % s)
            return s

        def proj(wname, nkc_total, rhs_fn, rhs_keys, chunks, evac, pan_cols=512, ncols=TT, kc_base=0):
            groups = []
            cur = []
            for idx, (c0, w) in enumerate(chunks):
                if cur and (len(cur) == 4 or c0 + w - chunks[cur[0]][0] > pan_cols):
                    groups.append(cur)
                    cur = []
                cur.append(idx)
            if cur:
                groups.append(cur)
            for g in groups:
                gc0 = chunks[g[0]][0]
                gc1 = chunks[g[-1]][0] + chunks[g[-1]][1]
                banks = [nextps() for _ in g]
                for kp0 in range(0, nkc_total, 8):
                    nk = min(8, nkc_total - kp0)
                    s = load_panel(wname, kc_base + kp0, nk, gc0, gc1 - gc0)

                    def mmfn(e, s=s, kp0=kp0, nk=nk, g=g, banks=banks, gc0=gc0):
                        inst = None
                        for kk in range(nk):
                            kc = kp0 + kk
                            for idx, b in zip(g, banks):
                                c0, w = chunks[idx]
                                inst = e.matmul(ps[b][0:w, 0:ncols], lhsT=wbuf[s][:, kk, c0 - gc0:c0 - gc0 + w], rhs=rhs_fn(kc),
                                                start=(kc == 0), stop=(kc == nkc_total - 1))
                        return inst
                    P.op("pe", mmfn, reads=[("wb", s)] + list(rhs_keys), writes=[("ps", b) for b in banks])
                for idx, b in zip(g, banks):
                    evac(idx, b, chunks[idx][1])

        def rmsnorm(gname, src=None, src_key="xT", nrows=TT, dst=None, dst_key="hT"):
            src = xT if src is None else src
            dst = hT if dst is None else dst
            n = nrows
            P.op("act", lambda e: e.activation(out=dst[:, :, 0:n], in_=src[:, :, 0:n], func=AF.Square),
                 reads=[src_key], writes=[dst_key])
            b = nextps()

            def mm(e):
                inst = None
                for kc in range(NKC):
                    inst = e.matmul(ps[b][:, 0:n], lhsT=cb("ones", 128), rhs=dst[:, kc, 0:n], start=(kc == 0), stop=(kc == NKC - 1))
                return inst
            P.op("pe", mm, reads=[dst_key, "cst"], writes=[("ps", b)])
            P.op("act", lambda e: e.activation(out=rstd[:, 0:n], in_=ps[b][:, 0:n], func=AF.Ln, bias=1e-6, scale=1.0 / D),
                 reads=[("ps", b)], writes=["rstd"])
            P.op("act", lambda e: e.activation(out=rstd[:, 0:n], in_=rstd[:, 0:n], func=AF.Exp, scale=-0.5),
                 reads=["rstd"], writes=["rstd"])

            def sc(e):
                inst = None
                for kc in range(NKC):
                    inst = e.scalar_tensor_tensor(out=dst[:, kc, 0:n], in0=src[:, kc, 0:n], scalar=vcol(gname, kc), in1=rstd[:, 0:n],
                                                  op0=ALU.mult, op1=ALU.mult)
                return inst
            P.op("dve", sc, reads=[src_key, "rstd", "vecs"], writes=[dst_key])

        def ffn(pref, gname):
            rmsnorm(gname)
            hr = lambda kc: hT[:, kc, :]

            def ev_gate(idx, b, w):
                P.op("act", lambda e: e.activation(out=act[:, idx, :], in_=ps[b][:, :], func=AF.Silu),
                     reads=[("ps", b)], writes=[("act", idx)])

            def ev_up(idx, b, w):
                P.op("dve", lambda e: e.tensor_tensor(out=act[:, idx, :], in0=ps[b][:, :], in1=act[:, idx, :], op=ALU.mult),
                     reads=[("ps", b), ("act", idx)], writes=[("act", idx)])

            def ev_down(idx, b, w):
                P.op("dve", lambda e: e.scalar_tensor_tensor(out=xT[:, idx, :], in0=ps[b][:, :], scalar=0.5, in1=xT[:, idx, :],
                                                             op0=ALU.mult, op1=ALU.add),
                     reads=[("ps", b), "xT"], writes=["xT"])
            for hf in range(2):
                for g0 in range(0, NACT, 4):
                    n = min(4, NACT - g0)
                    sub = [((hf * NACT + g0 + i) * 128, 128) for i in range(n)]
                    proj(pref + "_wg", NKC, hr, ["hT"], sub, lambda i, b, w, g0=g0: ev_gate(g0 + i, b, w))
                    proj(pref + "_wu", NKC, hr, ["hT"], sub, lambda i, b, w, g0=g0: ev_up(g0 + i, b, w))
                proj(pref + "_wd", NACT, lambda kc: act[:, kc, :], [("act", j) for j in range(NACT)],
                     [(m * 128, 128) for m in range(16)], ev_down, kc_base=hf * NACT)

        def mem_kv():
            P.op("sp", lambda e: e.dma_start(out=xT[:, :, 0:MEM], in_=memT_d.rearrange("(k p) m -> p k m", p=128)),
                 writes=["xT"], dma="xin")
            rmsnorm("mem_norm", nrows=MEM)
            def ev_k(idx, b, w):
                P.op("act", lambda e: e.activation(out=kmT[:, idx, :], in_=ps[b][:, 0:MEM], func=AF.Copy),
                     reads=[("ps", b)], writes=["kmT"])
            proj("w_ckv", NKC, lambda kc: hT[:, kc, 0:MEM], ["hT"], [(h * 128, 128) for h in range(4)], ev_k, ncols=MEM)
            for mb in range(2):
                b = nextps()
                for kp0 in (0, 8):
                    s = load_panel("w_ckv", kp0, 8, 512, 512)

                    def mm(e, s=s, kp0=kp0, b=b, mb=mb):
                        inst = None
                        for kk in range(8):
                            kc = kp0 + kk
                            inst = e.matmul(ps[b][:, :], lhsT=hT[:, kc, mb * 128:(mb + 1) * 128], rhs=wbuf[s][:, kk, :],
                                            start=(kc == 0), stop=(kc == 15))
                        return inst
                    P.op("pe", mm, reads=[("wb", s), "hT"], writes=[("ps", b)])
                P.op("act", lambda e, b=b, mb=mb: e.activation(out=vm[:, mb, :], in_=ps[b][:, :], func=AF.Copy),
                     reads=[("ps", b)], writes=["vm"])

        RW0 = 1536

        def token_shift_evac(b, width, carry_col, mu_ap, dst_ap, dst_key, func=None, pidx=[0]):
            i = pidx[0] % 2
            pidx[0] += 1
            Pb = Pst[i]
            w = width
            P.op("pool", lambda e: e.tensor_copy(out=Pb[0:w, 0:1], in_=carry[0:w, carry_col:carry_col + 1]),
                 reads=["carry"] + INITK, writes=[("Pst", i)])
            P.op("act", lambda e: e.activation(out=Pb[0:w, 1:TT + 1], in_=ps[b][0:w, :], func=AF.Copy),
                 reads=[("ps", b)], writes=[("Pst", i)])
            P.op("pool", lambda e: e.tensor_copy(out=carry[0:w, carry_col:carry_col + 1], in_=Pb[0:w, TT:TT + 1]),
                 reads=[("Pst", i)], writes=["carry"])
            P.op("dve", lambda e: e.tensor_tensor(out=Tt[6][0:w, :], in0=Pb[0:w, 0:TT], in1=Pb[0:w, 1:TT + 1], op=ALU.subtract),
                 reads=[("Pst", i)], writes=["T6"])
            if func is None:
                P.op("dve", lambda e: e.scalar_tensor_tensor(out=dst_ap, in0=Tt[6][0:w, :], scalar=mu_ap, in1=Pb[0:w, 1:TT + 1],
                                                             op0=ALU.mult, op1=ALU.add),
                     reads=["T6", ("Pst", i), "vecs"], writes=[dst_key])
            else:
                P.op("dve", lambda e: e.scalar_tensor_tensor(out=Tt[6][0:w, :], in0=Tt[6][0:w, :], scalar=mu_ap, in1=Pb[0:w, 1:TT + 1],
                                                             op0=ALU.mult, op1=ALU.add),
                     reads=["T6", ("Pst", i), "vecs"], writes=["T6"])
                P.op("act", lambda e: e.activation(out=dst_ap, in_=Tt[6][0:w, :], func=func),
                     reads=["T6"], writes=[dst_key])

        def attention(tile_idx, pre=False):
            hrhs = lambda kc: hT[:, kc, :]
            if pre and tile_idx != npre - 1:
                return
            def ev_q(idx, b, w):
                P.op("act", lambda e: e.activation(out=qT[:, idx, :], in_=ps[b][:, :], func=AF.Copy),
                     reads=[("ps", b)], writes=[("qT", idx)])
            if not pre:
                proj("w_in", NKC, hrhs, ["hT"], [(c * 128, 128) for c in range(8)], ev_q)

            def ev_k(idx, b, w):
                P.op("act", lambda e: e.activation(out=kTh[:, idx, 128:128 + TT], in_=ps[b][:, :], func=AF.Copy),
                     reads=[("ps", b)] + INITK, writes=["kTh"])
            proj("w_in", NKC, hrhs, ["hT"], [(1024 + c * 128, 128) for c in range(2)], ev_k)
            for tb in range(4):
                b = nextps()
                for kp0 in (0, 8):
                    s = load_panel("w_in", kp0, 8, 1280, 256)

                    def mm(e, s=s, kp0=kp0, b=b, tb=tb):
                        inst = None
                        for kk in range(8):
                            kc = kp0 + kk
                            inst = e.matmul(ps[b][:, 0:256], lhsT=hT[:, kc, tb * 128:(tb + 1) * 128], rhs=wbuf[s][:, kk, 0:256],
                                            start=(kc == 0), stop=(kc == 15))
                        return inst
                    P.op("pe", mm, reads=[("wb", s), "hT"], writes=[("ps", b)])
                P.op("act", lambda e, b=b, tb=tb: e.activation(out=vtok[:, tb + 1, :], in_=ps[b][:, 0:256], func=AF.Copy),
                     reads=[("ps", b)] + INITK, writes=["vtok"])
            ei = 0
            for g in range(4 if not pre else 0):
                kch, kbase = g // 2, (g % 2) * 64
                qc0 = 0 if g < 2 else 4
                for n in range(4):
                    first = (npre == 0 and tile_idx == 0 and n == 0)
                    firstown = (npre > 0 and tile_idx == npre and n == 0)
                    Eb = ET[ei % 2]
                    ekey = "T%d" % (ei % 2)
                    ei += 1
                    bprev, bcur = nextps(), nextps()
                    qrhs = qT[kbase:kbase + 64, qc0:qc0 + 4, n * 128:(n + 1) * 128]

                    def mm(e, kch=kch, kbase=kbase, n=n, bprev=bprev, bcur=bcur, qrhs=qrhs, first=first):
                        inst = None
                        if not first:
                            inst = e.matmul(ps[bprev][:, :], lhsT=kTh[kbase:kbase + 64, kch, n * 128:(n + 1) * 128], rhs=qrhs,
                                            start=True, stop=True)
                        inst = e.matmul(ps[bcur][:, :], lhsT=kTh[kbase:kbase + 64, kch, (n + 1) * 128:(n + 2) * 128], rhs=qrhs,
                                        start=True, stop=True)
                        return inst
                    P.op("pe", mm, reads=["kTh"] + [("qT", qc0 + i) for i in range(4)], writes=[("ps", bprev), ("ps", bcur)])
                    if not first:
                        P.op("act", lambda e, Eb=Eb, bprev=bprev: e.activation(out=Eb[:, 0, :], in_=ps[bprev][:, :], func=AF.Exp, scale=0.125),
                             reads=[("ps", bprev)], writes=[ekey])
                    P.op("act", lambda e, Eb=Eb, bcur=bcur: e.activation(out=Eb[:, 1, :], in_=ps[bcur][:, :], func=AF.Exp, scale=0.125),
                         reads=[("ps", bcur)], writes=[ekey])
                    if not first:
                        if firstown:
                            P.op("dve", lambda e, Eb=Eb: e.scalar_tensor_tensor(out=Eb[:, 0, :], in0=Eb[:, 0, :], scalar=cf("flag", 1), in1=cb("mprev", 512),
                                                                                 op0=ALU.mult, op1=ALU.mult),
                                 reads=[ekey, "cst", "cstf"], writes=[ekey])
                        else:
                            P.op("pool", lambda e, Eb=Eb: e.tensor_tensor(out=Eb[:, 0, :], in0=Eb[:, 0, :], in1=cb("mprev", 512), op=ALU.mult),
                                 reads=[ekey, "cst"], writes=[ekey])
                    P.op("pool", lambda e, Eb=Eb: e.tensor_tensor(out=Eb[:, 1, :], in0=Eb[:, 1, :], in1=cb("mcur", 512), op=ALU.mult),
                         reads=[ekey, "cst"], writes=[ekey])
                    bo, bd = nextps(), nextps()

                    def mm2(e, Eb=Eb, g=g, n=n, bo=bo, bd=bd, kbase=kbase, first=first):
                        inst = None
                        for hh in range(4):
                            osl = ps[bo][kbase:kbase + 64, hh * 128:(hh + 1) * 128]
                            dsl = ps[bd][kbase:kbase + 64, hh * 128:(hh + 1) * 128]
                            if not first:
                                e.matmul(osl, lhsT=vtok[:, n, g * 64:(g + 1) * 64], rhs=Eb[:, 0, hh * 128:(hh + 1) * 128], start=True, stop=False)
                            e.matmul(osl, lhsT=vtok[:, n + 1, g * 64:(g + 1) * 64], rhs=Eb[:, 1, hh * 128:(hh + 1) * 128], start=first, stop=True)
                            if not first:
                                e.matmul(dsl, lhsT=cb("ones", 64), rhs=Eb[:, 0, hh * 128:(hh + 1) * 128], start=True, stop=False)
                            inst = e.matmul(dsl, lhsT=cb("ones", 64), rhs=Eb[:, 1, hh * 128:(hh + 1) * 128], start=first, stop=True)
                        return inst
                    P.op("pe", mm2, reads=[ekey, "vtok", "cst"], writes=[("ps", bo), ("ps", bd)])
                    def dn(e, bd=bd, kbase=kbase, g=g):
                        inst = None
                        for hh in range(4):
                            h = 4 * g + hh
                            inst = e.tensor_scalar(out=den[kbase:kbase + 64, hh * 128:(hh + 1) * 128],
                                                   in0=ps[bd][kbase:kbase + 64, hh * 128:(hh + 1) * 128],
                                                   scalar1=esink[kbase:kbase + 64, h:h + 1], scalar2=None, op0=ALU.add)
                        return inst
                    P.op("dve", dn, reads=[("ps", bd), "esink"], writes=["T5"])
                    P.op("dve", lambda e, kbase=kbase: e.reciprocal(out=den[kbase:kbase + 64, :], in_=den[kbase:kbase + 64, :]),
                         reads=["T5"], writes=["T5"])

                    def yo(e, bo=bo, kbase=kbase, qc0=qc0, n=n):
                        return e.tensor_tensor(out=qT[kbase:kbase + 64, qc0:qc0 + 4, n * 128:(n + 1) * 128],
                                               in0=ps[bo][kbase:kbase + 64, :].rearrange("p (h q) -> p h q", h=4),
                                               in1=den[kbase:kbase + 64, :].rearrange("p (h q) -> p h q", h=4), op=ALU.mult)
                    P.op("dve", yo, reads=[("ps", bo), "T5"], writes=[("qT", qc0 + i) for i in range(4)])
            P.op("pool", lambda e: e.tensor_copy(out=kTh[:, :, 0:128], in_=kTh[:, :, TT:TT + 128]), reads=["kTh"], writes=["kTh"])
            P.op("pool", lambda e: e.tensor_copy(out=vtok[:, 0, :], in_=vtok[:, 4, :]), reads=["vtok"], writes=["vtok"])

        def rwkv(tile_idx, pre=False):
            hrhs = lambda kc: hT[:, kc, :]
            small = [(RW0 + 3072, 64, "mu_xw", AF.Tanh, 0, 24), (RW0 + 3136, 64, "mu_xa", AF.Copy, 1, 25),
                     (RW0 + 3200, 128, "mu_xg0", AF.Sigmoid, 2, 26), (RW0 + 3328, 32, "mu_xg1", AF.Sigmoid, 3, 27)]

            def ev_small(idx, b, w):
                c0, ww, mun, fn, slot, ccol = small[idx]
                token_shift_evac(b, ww, ccol, vcol(mun, 0, ww), xs_small[0:ww, slot, :], ("xs_small", slot), func=fn)
            proj("w_in", NKC, hrhs, ["hT"], [(s_[0], s_[1]) for s_ in small], ev_small)

            KSUB = int(os.environ.get("KSUB", "9"))
            if KSUB < 2:
                return
            for ch in range(8):
                dsts = [(xr, "xr"), (xk, "xk"), (xv, "xv")]

                def ev_rkv(idx, b, w, ch=ch):
                    token_shift_evac(b, 128, idx * 8 + ch, vcol("mu_rkv", idx * 8 + ch), dsts[idx][0][:, :], dsts[idx][1])
                for idx in range(3):
                    proj("w_in", NKC, hrhs, ["hT"], [(RW0 + idx * 1024 + ch * 128, 128)],
                         lambda i, b, w, idx=idx: ev_rkv(idx, b, w))
                T = Tt
                csl = slice(ch * 128, (ch + 1) * 128)
                b1 = nextps()
                P.op("pe", lambda e, b1=b1: e.matmul(ps[b1][:, :], lhsT=w2b[:, csl], rhs=xs_small[0:64, 0, :], start=True, stop=True),
                     reads=["w2b", ("xs_small", 0)], writes=[("ps", b1)])
                P.op("act", lambda e, b1=b1: e.activation(out=T[0][:, :], in_=ps[b1][:, :], func=AF.Sigmoid, bias=vcol("rwkv_w0", ch)),
                     reads=[("ps", b1), "vecs"], writes=["T0"])
                P.op("dve", lambda e: e.tensor_tensor_scan(out=T[1][:, :], data0=T[0][:, :], data1=T[0][:, :], initial=0.0,
                                                           op0=ALU.add, op1=ALU.bypass),
                     reads=["T0"], writes=["T1"])
                P.op("dve", lambda e: e.tensor_copy(out=T[2][:, 0:64], in_=T[1][:, 0:64]), reads=["T1"], writes=["T2"])

                def lwf(e):
                    base = T[1][:, 63:63 + 448].rearrange("p (c s) -> p c s", s=64)[:, :, 0:1].to_broadcast([128, 7, 64])
                    return e.tensor_tensor(out=T[2][:, 64:512].rearrange("p (c s) -> p c s", s=64),
                                           in0=T[1][:, 64:512].rearrange("p (c s) -> p c s", s=64), in1=base, op=ALU.subtract)
                P.op("dve", lwf, reads=["T1", "T2"], writes=["T2"])
                P.op("dve", lambda e: e.tensor_tensor(out=T[1][:, :], in0=T[2][:, :], in1=T[0][:, :], op=ALU.subtract),
                     reads=["T2", "T0", "T1"], writes=["T1"])

                def lrem(e):
                    last = T[2][:, :].rearrange("p (c s) -> p c s", s=64)[:, :, 63:64].to_broadcast([128, 8, 64])
                    return e.tensor_tensor(out=T[0][:, :].rearrange("p (c s) -> p c s", s=64), in0=last,
                                           in1=T[2][:, :].rearrange("p (c s) -> p c s", s=64), op=ALU.subtract)
                P.op("dve", lrem, reads=["T2", "T1", "T0"], writes=["T0"])
                P.op("act", lambda e: e.activation(out=Wtot[:, :], in_=T[2][:, :].rearrange("p (c s) -> p c s", s=64)[:, :, 63],
                                                   func=AF.Exp, scale=-CDEC), reads=["T2"], writes=["Wtot"])
                P.op("act", lambda e: e.activation(out=T[3][:, :], in_=T[2][:, :], func=AF.Exp, scale=-CDEC), reads=["T2"], writes=["T3"])
                P.op("act", lambda e: e.activation(out=T[2][:, :], in_=T[2][:, :], func=AF.Exp, scale=CDEC), reads=["T2", "T3", "Wtot"], writes=["T2"])
                P.op("act", lambda e: e.activation(out=T[1][:, :], in_=T[1][:, :], func=AF.Exp, scale=-CDEC), reads=["T1"], writes=["T1"])
                P.op("act", lambda e: e.activation(out=T[0][:, :], in_=T[0][:, :], func=AF.Exp, scale=-CDEC), reads=["T0"], writes=["T0"])
                P.op("dve", lambda e: e.tensor_scalar(out=T[4][:, :], in0=xk[:, :], scalar1=vcol("rwkv_k_k", ch), scalar2=None, op0=ALU.mult),
                     reads=["xk", "vecs"], writes=["T4"])
                P.op("act", lambda e: e.activation(out=Bt[0][:, :], in_=T[4][:, :], func=AF.Square), reads=["T4"], writes=["B0"])
                b2 = nextps()
                P.op("pe", lambda e, b2=b2: e.matmul(ps[b2][:, :], lhsT=cb("bones", 128), rhs=Bt[0][:, :], start=True, stop=True),
                     reads=["B0", "cst"], writes=[("ps", b2)])
                P.op("dve", lambda e, b2=b2: e.tensor_scalar(out=T[5][:, :], in0=ps[b2][:, :], scalar1=1e-24, scalar2=None, op0=ALU.max),
                     reads=[("ps", b2)], writes=["T5"])
                P.op("act", lambda e: e.activation(out=T[5][:, :], in_=T[5][:, :], func=AF.Ln), reads=["T5"], writes=["T5"])
                P.op("act", lambda e: e.activation(out=T[5][:, :], in_=T[5][:, :], func=AF.Exp, scale=-0.5), reads=["T5"], writes=["T5"])
                P.op("dve", lambda e: e.tensor_tensor(out=T[4][:, :], in0=T[4][:, :], in1=T[5][:, :], op=ALU.mult),
                     reads=["T4", "T5"], writes=["T4"])
                b3 = nextps()
                P.op("pe", lambda e, b3=b3: e.matmul(ps[b3][:, :], lhsT=a2b[:, csl], rhs=xs_small[0:64, 1, :], start=True, stop=True),
                     reads=["a2b", ("xs_small", 1)], writes=[("ps", b3)])
                P.op("act", lambda e, b3=b3: e.activation(out=T[5][:, :], in_=ps[b3][:, :], func=AF.Sigmoid, bias=vcol("rwkv_a0", ch)),
                     reads=[("ps", b3), "vecs", "T5", "T4"], writes=["T5"])
                P.op("dve", lambda e: e.tensor_scalar(out=T[6][:, :], in0=T[5][:, :], scalar1=vcol("rwkv_k_a", ch), scalar2=omka[:, ch:ch + 1],
                                                      op0=ALU.mult, op1=ALU.add), reads=["T5", "vecs", "omka"], writes=["T6"])
                P.op("dve", lambda e: e.tensor_tensor(out=T[6][:, :], in0=T[6][:, :], in1=xk[:, :], op=ALU.mult), reads=["T6", "xk"], writes=["T6"])
                P.op("dve", lambda e: e.tensor_tensor(out=T[5][:, :], in0=T[5][:, :], in1=T[4][:, :], op=ALU.mult), reads=["T5", "T4"], writes=["T5"])
                if not pre:
                    b4 = nextps()

                    def gmm(e, b4=b4):
                        e.matmul(ps[b4][:, :], lhsT=g2b[:, 0, csl], rhs=xs_small[:, 2, :], start=True, stop=False)
                        return e.matmul(ps[b4][:, :], lhsT=g2b[0:32, 1, csl], rhs=xs_small[0:32, 3, :], start=False, stop=True)
                    P.op("pe", gmm, reads=["g2b", "g2b1", ("xs_small", 2), ("xs_small", 3)], writes=[("ps", b4)])
                    P.op("act", lambda e, b4=b4: e.activation(out=gbuf[:, :], in_=ps[b4][:, :], func=AF.Copy), reads=[("ps", b4)], writes=["gbuf"])
                v3 = lambda t_: t_[:, :].rearrange("p (c s) -> p c s", s=64)
                P.op("dve", lambda e: e.scalar_tensor_tensor(out=AR[:, :, 0, :], in0=v3(T[4]), scalar=-1.0, in1=v3(T[1]), op0=ALU.mult, op1=ALU.mult),
                     reads=["T4", "T1"], writes=["AR"])
                P.op("pool", lambda e: e.tensor_tensor(out=AR[:, :, 1, :], in0=v3(xr), in1=v3(T[3]), op=ALU.mult), reads=["xr", "T3"], writes=["AR1"])
                P.op("dve", lambda e: e.tensor_tensor(out=BK[:, :, 0, :], in0=v3(T[5]), in1=v3(T[2]), op=ALU.mult), reads=["T5", "T2"], writes=["BK"])
                P.op("pool", lambda e: e.tensor_tensor(out=BK[:, :, 1, :], in0=v3(T[6]), in1=v3(T[2]), op=ALU.mult), reads=["T6", "T2"], writes=["BK1"])
                P.op("dve", lambda e: e.tensor_tensor(out=TM[:, :, 1, :], in0=v3(T[5]), in1=v3(T[0]), op=ALU.mult), reads=["T5", "T0"], writes=["TM1"])
                P.op("pool", lambda e: e.tensor_tensor(out=TM[:, :, 2, :], in0=v3(T[6]), in1=v3(T[0]), op=ALU.mult), reads=["T6", "T0"], writes=["TM2"])
                P.op("act", lambda e: e.activation(out=TM[:, :, 0, :], in_=v3(xv), func=AF.Copy), reads=["xv"], writes=["TM0"])
                selB = cstb[:, c_off["ident"] + 64:c_off["ident"] + 128]
                for (X, XB, kx, kxb) in ((AR, ARB, ("AR", "AR1"), "ARB"), (BK, BKB, ("BK", "BK1"), "BKB")):
                    for hf4 in range(2):
                        bx = nextps()
                        P.op("pe", lambda e, X=X, hf4=hf4, bx=bx: e.matmul(
                            ps[bx][0:64, :], lhsT=selB, rhs=X[:, hf4 * 4:(hf4 + 1) * 4, :, :].rearrange("p c a s -> p (c a s)"),
                            start=True, stop=True), reads=list(kx) + ["cst"], writes=[("ps", bx)])
                        if hf4 == 0:
                            P.op("act", lambda e, XB=XB, bx=bx: e.activation(out=XB[:, 0:4, :, :].rearrange("p c a s -> p (c a s)"),
                                                                          in_=ps[bx][0:64, :], func=AF.Copy),
                                 reads=[("ps", bx)], writes=[kxb])
                        else:
                            P.op("dve", lambda e, XB=XB, bx=bx: e.tensor_copy(out=XB[:, 4:8, :, :].rearrange("p c a s -> p (c a s)"),
                                                                           in_=ps[bx][0:64, :]),
                                 reads=[("ps", bx)], writes=[kxb + "h"])
                P.op("dve", lambda e: e.tensor_copy(out=Whl[:, 0, :], in_=Wtot[:, :]), reads=["Wtot"], writes=["Whl"])
                P.op("dve", lambda e: e.tensor_tensor(out=Whl[:, 1, :], in0=Wtot[:, :], in1=Whl[:, 0, :], op=ALU.subtract),
                     reads=["Wtot", "Whl"], writes=["Whl"])
                bw = nextps()

                def wmm(e, bw=bw):
                    e.matmul(ps[bw][0:64, 0:8], lhsT=selB, rhs=Whl[:, 0, :], start=True, stop=False)
                    return e.matmul(ps[bw][0:64, 0:8], lhsT=selB, rhs=Whl[:, 1, :], start=False, stop=True)
                P.op("pe", wmm, reads=["Whl", "cst"], writes=[("ps", bw)])
                P.op("dve", lambda e, bw=bw: e.tensor_copy(out=WtotB[:, :], in_=ps[bw][0:64, 0:8]), reads=[("ps", bw)], writes=["WtotB"])
                ARx = (AR, ARB)
                BKx = (BK, BKB)
                XKEYS = ["AR", "AR1", "BK", "BK1", "ARB", "ARBh", "BKB", "BKBh"]
                if not pre:
                    P.op("dve", lambda e: e.tensor_tensor(out=T[4][:, :], in0=xr[:, :], in1=T[6][:, :], op=ALU.mult), reads=["xr", "T6", "T4", "AR"], writes=["T4"])
                    P.op("dve", lambda e: e.tensor_scalar(out=Bt[0][:, :], in0=T[4][:, :], scalar1=vcol("rwkv_r_k", ch), scalar2=None, op0=ALU.mult),
                         reads=["T4", "vecs"], writes=["B0"])
                    b5 = nextps()
                    P.op("pe", lambda e, b5=b5: e.matmul(ps[b5][:, :], lhsT=cb("bones", 128), rhs=Bt[0][:, :], start=True, stop=True),
                         reads=["B0", "cst"], writes=[("ps", b5)])
                    P.op("dve", lambda e, b5=b5: e.tensor_tensor(out=T[4][:, :], in0=ps[b5][:, :], in1=xv[:, :], op=ALU.mult),
                         reads=[("ps", b5), "xv"], writes=["T4"])
                if KSUB < 3:
                    continue
                K3 = int(os.environ.get("K3", "7"))
                for c2 in range(8):
                    if not (K3 & 1):
                        break
                    bt_ = nextps()

                    def trf(e, c2=c2, bt_=bt_):
                        inst = None
                        for k3 in range(3):
                            inst = e.matmul(ps[bt_][0:64, k3 * 128:(k3 + 1) * 128], lhsT=TM[:, c2, k3, :], rhs=cb("ident", 128), start=True, stop=True)
                        return inst
                    P.op("pe", trf, reads=["TM0", "TM1", "TM2", "cst"], writes=[("ps", bt_)])
                    P.op("dve", lambda e, c2=c2, bt_=bt_: e.tensor_copy(out=TMt[:, c2, :, :].rearrange("p k f -> p (k f)"), in_=ps[bt_][0:64, 0:384]),
                         reads=[("ps", bt_)], writes=["TMt"])
                for (dstL, kdst, slot) in ((LT, "LT", 0), (LTk, "LTk", 1)):
                    for q4 in range(4):
                        if not (K3 & 2):
                            break
                        bl = nextps()

                        def lmm(e, q4=q4, bl=bl, slot=slot):
                            inst = None
                            for ii in range(4):
                                pi = q4 * 4 + ii
                                c, hd = pi // 2, pi % 2
                                inst = e.matmul(ps[bl][0:64, ii * 128:(ii + 1) * 128], lhsT=BKx[hd][0:64, c, slot, :],
                                                rhs=ARx[hd][0:64, c, :, :].rearrange("p a s -> p (a s)"), start=True, stop=True)
                            return inst
                        P.op("pe", lmm, reads=XKEYS, writes=[("ps", bl)])
                        if not (K3 & 4):
                            continue
                        P.op("dve", lambda e, q4=q4, bl=bl, dstL=dstL: e.tensor_tensor(
                            out=dstL[:, q4 * 4:(q4 + 1) * 4, :].rearrange("p a s -> p (a s)"),
                            in0=ps[bl][0:64, :], in1=cf("mlt", 512, rows=64), op=ALU.mult),
                            reads=[("ps", bl), "cstf"], writes=[kdst])
                if KSUB < 4:
                    continue
                f2 = lambda t_, hlf: t_[:, hlf * 8:(hlf + 1) * 8, :].rearrange("p a s -> p (a s)")
                for hlf in range(2):
                    bl = nextps()

                    def l0mm(e, hlf=hlf, bl=bl):
                        inst = None
                        for ii in range(8):
                            pi = hlf * 8 + ii
                            c, hd = pi // 2, pi % 2
                            inst = e.matmul(ps[bl][0:64, ii * 64:(ii + 1) * 64], lhsT=ARx[hd][0:64, c, 0, :], rhs=BKx[hd][0:64, c, 0, :], start=True, stop=True)
                        return inst
                    P.op("pe", l0mm, reads=XKEYS, writes=[("ps", bl)])
                    P.op("dve", lambda e, hlf=hlf, bl=bl: e.tensor_tensor(out=f2(inv["L0"], hlf), in0=ps[bl][0:64, :],
                                                                          in1=cf("mlow", 512, rows=64), op=ALU.mult),
                         reads=[("ps", bl), "cstf"], writes=["inv_L0"])
                P.op("pool", lambda e: e.tensor_copy(out=inv["N0"][:, :, :], in_=LT[:, :, 0:64]), reads=["LT"], writes=["inv_N0"])
                for hlf in range(2):
                    P.op("pool", lambda e, hlf=hlf: e.tensor_tensor(out=f2(inv["Q"], hlf), in0=f2(inv["N0"], hlf), in1=cb("id64", 512, rows=64), op=ALU.add),
                         reads=["inv_N0", "cst"], writes=["inv_Q"])
                    P.op("pool", lambda e, hlf=hlf: e.tensor_tensor(out=f2(inv["P"], hlf), in0=f2(inv["L0"], hlf), in1=cb("id64", 512, rows=64), op=ALU.add),
                         reads=["inv_L0", "cst"], writes=["inv_P"])
                cur, nxt = "0", "1"
                Qm, Pm = inv["Q"], inv["P"]
                for lvl in range(5):
                    Lc, Nc, Ln_, Nn = inv["L" + cur], inv["N" + cur], inv["L" + nxt], inv["N" + nxt]
                    kLc, kNc, kLn, kNn = "inv_L" + cur, "inv_N" + cur, "inv_L" + nxt, "inv_N" + nxt

                    def mm8(lt, rh, hlf, bq):
                        def f(e):
                            inst = None
                            for ii in range(8):
                                pi = hlf * 8 + ii
                                inst = e.matmul(ps[bq][0:64, ii * 64:(ii + 1) * 64], lhsT=lt[:, pi, :], rhs=rh[:, pi, :], start=True, stop=True)
                            return inst
                        return f
                    for (dst, kdst, lt, rh) in ((Ln_, kLn, Nc, Lc), (Nn, kNn, Lc, Nc)):
                        for hlf in range(2):
                            bq = nextps()
                            P.op("pe", mm8(lt, rh, hlf, bq), reads=[kLc, kNc], writes=[("ps", bq)])
                            if hlf == 0:
                                P.op("act", lambda e, hlf=hlf, bq=bq, dst=dst: e.activation(out=f2(dst, hlf), in_=ps[bq][0:64, :], func=AF.Copy),
                                     reads=[("ps", bq)], writes=[kdst])
                            else:
                                P.op("dve", lambda e, hlf=hlf, bq=bq, dst=dst: e.tensor_copy(out=f2(dst, hlf), in_=ps[bq][0:64, :]),
                                     reads=[("ps", bq)], writes=[kdst])
                    pend = []
                    for (dst, kdst, lt, rh, krh) in ((Qm, "inv_Q", Pm, Nn, kNn), (Pm, "inv_P", Qm, Ln_, kLn)):
                        if lvl == 4 and dst is Pm:
                            continue
                        for hlf in range(2):
                            bq = nextps()
                            P.op("pe", mm8(lt, rh, hlf, bq), reads=["inv_Q", "inv_P", krh], writes=[("ps", bq)])
                            pend.append((dst, kdst, hlf, bq))
                    for (dst, kdst, hlf, bq) in pend:
                        P.op("dve", lambda e, hlf=hlf, bq=bq, dst=dst: e.tensor_tensor(out=f2(dst, hlf), in0=ps[bq][0:64, :], in1=f2(dst, hlf), op=ALU.add),
                             reads=[("ps", bq), kdst], writes=[kdst])
                    cur, nxt = nxt, cur
                if KSUB < 5:
                    continue
                TTm = Qm
                kTT = "inv_Q"
                sk = ("S0T", ch)
                for c in range(8):
                    bz = nextps()

                    def zmm(e, c=c, bz=bz):
                        inst = None
                        for hd in range(2):
                            o = ps[bz][0:64, hd * 64:(hd + 1) * 64]
                            e.matmul(o, lhsT=ARx[hd][0:64, c, 0, :], rhs=S0Tb[:, ch, hd, :], start=True, stop=False)
                            inst = e.matmul(o, lhsT=LTk[:, c * 2 + hd, 0:64], rhs=TMt[:, c, 0, hd * 64:(hd + 1) * 64], start=False, stop=True)
                        return inst
                    P.op("pe", zmm, reads=XKEYS + [sk, "LTk", "TMt"] + INITK, writes=[("ps", bz)])
                    P.op("act", lambda e, bz=bz: e.activation(out=Zs[:, :, :].rearrange("p a s -> p (a s)"), in_=ps[bz][0:64, 0:128], func=AF.Copy),
                         reads=[("ps", bz)], writes=["Zs"])
                    bu = nextps()

                    def umm(e, c=c, bu=bu):
                        inst = None
                        for hd in range(2):
                            inst = e.matmul(ps[bu][0:64, hd * 64:(hd + 1) * 64], lhsT=TTm[:, c * 2 + hd, :], rhs=Zs[:, hd, :], start=True, stop=True)
                        return inst
                    P.op("pe", umm, reads=[kTT, "Zs"], writes=[("ps", bu)])
                    P.op("act", lambda e, bu=bu: e.activation(out=UV[:, :, :].rearrange("p a s -> p (a s)"), in_=ps[bu][0:64, 0:128], func=AF.Copy),
                         reads=[("ps", bu)], writes=["UV"])

                    def ymm(e, c=c):
                        inst = None
                        for hd in range(2):
                            rows = slice(hd * 64, hd * 64 + 64)
                            o = ps[YB][rows, c * 64:(c + 1) * 64]
                            e.matmul(o, lhsT=S0Tb[:, ch, hd, :], rhs=ARx[hd][0:64, c, 1, :], start=True, stop=False)
                            e.matmul(o, lhsT=UV[:, hd, :], rhs=LT[:, c * 2 + hd, 64:128], start=False, stop=False)
                            inst = e.matmul(o, lhsT=TMt[:, c, 0, hd * 64:(hd + 1) * 64], rhs=LTk[:, c * 2 + hd, 64:128], start=False, stop=True)
                        return inst
                    if not pre:
                        P.op("pe", ymm, reads=XKEYS + [sk, "UV", "LT", "LTk", "TMt"], writes=[("ps", YB)])
                    bs = nextps()

                    def smm(e, c=c, bs=bs):
                        inst = None
                        for hd in range(2):
                            o = ps[bs][0:64, hd * 64:(hd + 1) * 64]
                            e.matmul(o, lhsT=TMt[:, c, 1, hd * 64:(hd + 1) * 64], rhs=UV[:, hd, :], start=True, stop=False)
                            inst = e.matmul(o, lhsT=TMt[:, c, 2, hd * 64:(hd + 1) * 64], rhs=TMt[:, c, 0, hd * 64:(hd + 1) * 64], start=False, stop=True)
                        return inst
                    P.op("pe", smm, reads=["TMt", "UV"], writes=[("ps", bs)])

                    def supd(e, c=c, bs=bs):
                        e.scalar_tensor_tensor(out=S0T[:, ch, 0, :], in0=S0T[:, ch, 0, :], scalar=Wtot[0:64, c:c + 1], in1=ps[bs][0:64, 0:64],
                                               op0=ALU.mult, op1=ALU.add)
                        return e.scalar_tensor_tensor(out=S0T[:, ch, 1, :], in0=S0T[:, ch, 1, :], scalar=WtotB[:, c:c + 1], in1=ps[bs][0:64, 64:128],
                                                      op0=ALU.mult, op1=ALU.add)
                    P.op("dve", supd, reads=[("ps", bs), "Wtot", "WtotB"] + INITK, writes=[("S0Tf", ch)])
                    P.op("act", lambda e: e.activation(out=S0Tb[:, ch, :, :].rearrange("p a s -> p (a s)"),
                                                       in_=S0T[:, ch, :, :].rearrange("p a s -> p (a s)"), func=AF.Copy),
                         reads=[("S0Tf", ch)], writes=[sk])
                if KSUB < 6 or pre:
                    continue
                P.op("act", lambda e: e.activation(out=ysb[:, :], in_=ps[YB][:, :], func=AF.Copy), reads=[("ps", YB), "T3"], writes=["T3"])
                P.op("dve", lambda e: e.tensor_copy(out=sqb[:, :], in_=ysb[:, :]), reads=["T3"], writes=["sqb"])
                bm = nextps()
                P.op("pe", lambda e, bm=bm: e.matmul(ps[bm][:, :], lhsT=cb("bones", 128), rhs=sqb[:, :], start=True, stop=True),
                     reads=["sqb", "cst"], writes=[("ps", bm)])
                P.op("dve", lambda e, bm=bm: e.scalar_tensor_tensor(out=ysb[:, :], in0=ps[bm][:, :], scalar=-1.0 / 64, in1=ysb[:, :], op0=ALU.mult, op1=ALU.add),
                     reads=[("ps", bm), "T3"], writes=["T3"])
                P.op("act", lambda e: e.activation(out=sqb[:, :], in_=ysb[:, :], func=AF.Square), reads=["T3", "sqb"], writes=["sqb"])
                bv = nextps()
                P.op("pe", lambda e, bv=bv: e.matmul(ps[bv][:, :], lhsT=cb("bones", 128), rhs=sqb[:, :], start=True, stop=True),
                     reads=["sqb", "cst"], writes=[("ps", bv)])
                P.op("act", lambda e, bv=bv: e.activation(out=T[5][:, :], in_=ps[bv][:, :], func=AF.Ln, bias=64e-5, scale=1.0 / 64),
                     reads=[("ps", bv), "T5"], writes=["T5"])
                P.op("act", lambda e: e.activation(out=T[5][:, :], in_=T[5][:, :], func=AF.Exp, scale=-0.5), reads=["T5"], writes=["T5"])
                P.op("dve", lambda e: e.tensor_tensor(out=ysb[:, :], in0=ysb[:, :], in1=T[5][:, :], op=ALU.mult), reads=["T3", "T5"], writes=["T3"])
                P.op("dve", lambda e: e.tensor_scalar(out=ysb[:, :], in0=ysb[:, :], scalar1=vcol("rwkv_ln_w", ch), scalar2=vcol("rwkv_ln_b", ch),
                                                      op0=ALU.mult, op1=ALU.add), reads=["T3", "vecs"], writes=["T3"])
                P.op("dve", lambda e: e.tensor_tensor(out=ysb[:, :], in0=ysb[:, :], in1=T[4][:, :], op=ALU.add), reads=["T3", "T4"], writes=["T3"])
                P.op("dve", lambda e: e.tensor_tensor(out=yr[:, ch, :], in0=ysb[:, :], in1=gbuf[:, :], op=ALU.mult), reads=["T3", "gbuf"], writes=[("yr", ch)])

        def mixer(tile_idx, STG=9, pre=False):
            rmsnorm("mix_norm")
            attention(tile_idx, pre)
            if STG < 3:
                return
            rwkv(tile_idx, pre)
            if STG < 4 or pre:
                return
            for m in range(16):
                hold = {}

                def ev_pa(idx, b, w):
                    hold["pa"] = b

                def ev_pr(idx, b, w):
                    hold["pr"] = b

                def ev_ga(idx, b, w):
                    hold["ga"] = b

                def ev_gr(idx, b, w):
                    hold["gr"] = b
                proj("w_pa", 8, lambda kc: qT[:, kc, :], [("qT", i) for i in range(8)], [(m * 128, 128)], ev_pa)
                proj("w_pr", 8, lambda kc: yr[:, kc, :], [("yr", i) for i in range(8)], [(m * 128, 128)], ev_pr)
                proj("w_in", NKC, lambda kc: hT[:, kc, :], ["hT"], [(4896 + m * 128, 128)], ev_ga)
                proj("w_in", NKC, lambda kc: hT[:, kc, :], ["hT"], [(6944 + m * 128, 128)], ev_gr)
                pa, pr, ga, gr = hold["pa"], hold["pr"], hold["ga"], hold["gr"]
                P.op("act", lambda e, ga=ga: e.activation(out=Tt[0][:, :], in_=ps[ga][:, :], func=AF.Sigmoid), reads=[("ps", ga)], writes=["T0"])
                P.op("act", lambda e, gr=gr: e.activation(out=Tt[1][:, :], in_=ps[gr][:, :], func=AF.Sigmoid), reads=[("ps", gr)], writes=["T1"])
                P.op("dve", lambda e, pa=pa: e.tensor_tensor(out=Tt[0][:, :], in0=ps[pa][:, :], in1=Tt[0][:, :], op=ALU.mult), reads=[("ps", pa), "T0"], writes=["T0"])
                P.op("dve", lambda e, pr=pr: e.tensor_tensor(out=Tt[1][:, :], in0=ps[pr][:, :], in1=Tt[1][:, :], op=ALU.mult), reads=[("ps", pr), "T1"], writes=["T1"])
                P.op("dve", lambda e, m=m: e.tensor_tensor(out=act[:, m, :], in0=Tt[0][:, :], in1=Tt[1][:, :], op=ALU.add), reads=["T0", "T1"], writes=[("act", m)])

            def ev_out(idx, b, w):
                P.op("dve", lambda e: e.tensor_tensor(out=xT[:, idx, :], in0=ps[b][:, :], in1=xT[:, idx, :], op=ALU.add),
                     reads=[("ps", b), "xT"], writes=["xT"])
            proj("w_out", NKC, lambda kc: act[:, kc, :], [("act", j) for j in range(16)], [(m * 128, 128) for m in range(16)], ev_out)

        def cross():
            rmsnorm("cross_norm")

            def ev_q(idx, b, w):
                P.op("act", lambda e: e.activation(out=qT[:, idx, :], in_=ps[b][:, :], func=AF.Copy), reads=[("ps", b)], writes=[("qT", idx)])
            proj("w_cq", NKC, lambda kc: hT[:, kc, :], ["hT"], [(h * 128, 128) for h in range(4)], ev_q)
            sc = float(128 ** -0.5)
            for h in range(4):
                Eb = ET[h % 2]
                ekey = "T%d" % (h % 2)
                b0, b1 = nextps(), nextps()

                def smm(e, h=h, b0=b0, b1=b1):
                    e.matmul(ps[b0][:, :], lhsT=kmT[:, h, 0:128], rhs=qT[:, h, :], start=True, stop=True)
                    return e.matmul(ps[b1][:, :], lhsT=kmT[:, h, 128:256], rhs=qT[:, h, :], start=True, stop=True)
                P.op("pe", smm, reads=["kmT", ("qT", h)], writes=[("ps", b0), ("ps", b1)])
                P.op("act", lambda e, Eb=Eb, b0=b0: e.activation(out=Eb[:, 0, :], in_=ps[b0][:, :], func=AF.Exp, scale=sc), reads=[("ps", b0)], writes=[ekey])
                P.op("act", lambda e, Eb=Eb, b1=b1: e.activation(out=Eb[:, 1, :], in_=ps[b1][:, :], func=AF.Exp, scale=sc), reads=[("ps", b1)], writes=[ekey])
                bo, bd = nextps(), nextps()

                def omm(e, h=h, Eb=Eb, bo=bo, bd=bd):
                    e.matmul(ps[bo][:, :], lhsT=vm[:, 0, h * 128:(h + 1) * 128], rhs=Eb[:, 0, :], start=True, stop=False)
                    e.matmul(ps[bo][:, :], lhsT=vm[:, 1, h * 128:(h + 1) * 128], rhs=Eb[:, 1, :], start=False, stop=True)
                    e.matmul(ps[bd][:, :], lhsT=cb("ones", 128), rhs=Eb[:, 0, :], start=True, stop=False)
                    return e.matmul(ps[bd][:, :], lhsT=cb("ones", 128), rhs=Eb[:, 1, :], start=False, stop=True)
                P.op("pe", omm, reads=[ekey, "vm", "cst"], writes=[("ps", bo), ("ps", bd)])
                P.op("dve", lambda e, bd=bd: e.reciprocal(out=den[:, :], in_=ps[bd][:, :]), reads=[("ps", bd)], writes=["T5"])
                P.op("dve", lambda e, bo=bo, h=h: e.tensor_tensor(out=oT[:, h, :], in0=ps[bo][:, :], in1=den[:, :], op=ALU.mult),
                     reads=[("ps", bo), "T5"], writes=[("yr", h)])

            def ev_o(idx, b, w):
                P.op("dve", lambda e: e.tensor_tensor(out=xT[:, idx, :], in0=ps[b][:, :], in1=xT[:, idx, :], op=ALU.add),
                     reads=[("ps", b), "xT"], writes=["xT"])
            proj("w_co", 4, lambda kc: oT[:, kc, :], [("yr", h) for h in range(4)], [(m * 128, 128) for m in range(16)], ev_o)

        def final_norm_store(t):
            P.op("act", lambda e: e.activation(out=hT[:, :, :], in_=xT[:, :, :], func=AF.Square), reads=["xT"], writes=["hT"])
            b = nextps()

            def mm(e):
                inst = None
                for kc in range(NKC):
                    inst = e.matmul(ps[b][:, :], lhsT=cb("ones", 128), rhs=hT[:, kc, :], start=(kc == 0), stop=(kc == NKC - 1))
                return inst
            P.op("pe", mm, reads=["hT", "cst"], writes=[("ps", b)])
            P.op("act", lambda e: e.activation(out=rstd[:, :], in_=ps[b][:, :], func=AF.Ln, bias=1e-6, scale=1.0 / D), reads=[("ps", b)], writes=["rstd"])
            P.op("act", lambda e: e.activation(out=rstd[:, :], in_=rstd[:, :], func=AF.Exp, scale=-0.5), reads=["rstd"], writes=["rstd"])

            def sc(e):
                inst = None
                for kc in range(NKC):
                    inst = e.scalar_tensor_tensor(out=xT[:, kc, :], in0=xT[:, kc, :], scalar=vcol("final_norm", kc), in1=rstd[:, :],
                                                  op0=ALU.mult, op1=ALU.mult)
                return inst
            P.op("dve", sc, reads=["xT", "rstd", "vecs"], writes=["xT"])
            P.op("pool", lambda e: e.dma_start(out=out_d[:, t * TT:(t + 1) * TT].rearrange("(k p) n -> p k n", p=128), in_=xT[:, :, :]),
                 reads=["xT"], writes=["outd"], dma="out")

        import os
        STG = int(os.environ.get("KSTAGE", "9"))
        mem_kv()
        P.barrier()
        for t in range(ntiles):
            P.epoch = 1 + t // 4
            P.op("pool", lambda e, t=t: e.dma_start(out=xT[:, :, :], in_=xT_d[:, t * TT:(t + 1) * TT].rearrange("(k p) n -> p k n", p=128)),
                 writes=["xT"], dma="xin")
            if STG >= 1:
                ffn("ffn1", "ffn1_norm")
            P.barrier()
            pre = t < npre
            if STG >= 2:
                mixer(t, STG, pre)
            P.barrier()
            if pre:
                continue
            if STG >= 5:
                cross()
            if STG >= 6:
                ffn("ffn2", "ffn2_norm")
            final_norm_store(t - npre)
            P.barrier()
        P.finalize(nc, stack, None, ["out"])
    return nc


_CACHE = {}


def kernel(**inp):
    import os
    x = np.asarray(inp["x"], np.float32)
    B, T, _ = x.shape
    ntiles = T // TT
    npre = ntiles // 2
    TH = npre * TT
    V = build_vecs(inp)
    vecs = V.build()
    consts2 = [build_consts(h) for h in (0, 1)]
    c_off = consts2[0][1]
    key = (ntiles, vecs.shape[1])
    if key not in _CACHE:
        _CACHE[key] = build_program(ntiles, V.off, vecs.shape[1], c_off, consts2[0][0].shape[1], npre=npre)
    nc = _CACHE[key]
    f = lambda a: np.ascontiguousarray(np.asarray(a, np.float32))
    w_in = f(inp["w_in"][0]).copy()
    w_in[:, 0:1024] = w_in[:, Q_PERM]
    shared = {
        "vecs": vecs,
        "ffn1_wg": f(inp["ffn1_w_gate"][0]), "ffn1_wu": f(inp["ffn1_w_up"][0]), "ffn1_wd": f(inp["ffn1_w_down"][0]),
        "w_in": w_in, "w_pa": f(np.asarray(inp["w_proj_attn"][0])[Q_PERM, :]), "w_pr": f(inp["w_proj_rwkv"][0]),
        "w_out": f(inp["w_out"][0]), "w_cq": f(inp["w_cross_q"][0]), "w_ckv": f(inp["w_cross_kv"][0]),
        "w_co": f(inp["w_cross_o"][0]),
        "ffn2_wg": f(inp["ffn2_w_gate"][0]), "ffn2_wu": f(inp["ffn2_w_up"][0]), "ffn2_wd": f(inp["ffn2_w_down"][0]),
        "w2": f(inp["rwkv_w2"][0]), "a2": f(inp["rwkv_a2"][0]), "g2": f(inp["rwkv_g2"][0]),
    }
    mem = np.asarray(inp["mem"], np.float32)
    ncores = int(os.environ.get("KCORES", "8"))
    in_maps = []
    for c in range(ncores):
        b, half = (c // 2) % B, c % 2
        m = dict(shared)
        xt = np.zeros((D, T), np.float32)
        if half == 0:
            xt[:, TH:] = x[b, 0:TH].T
        else:
            xt[:, :] = x[b].T
        m["xT"] = xt
        m["memT"] = np.ascontiguousarray(mem[b].T)
        m["consts"] = consts2[half][0]
        in_maps.append(m)
    if os.environ.get("KTRACE"):
        res = run_bass_kernel_spmd(nc, in_maps, core_ids=list(range(ncores)), trace=True)
        print("EXEC_TIME_NS", res.exec_time_ns)
    else:
        res = run_bass_kernel_spmd(nc, in_maps, core_ids=list(range(ncores)))
    out = np.zeros((B, T, D), np.float32)
    for c in range(ncores):
        b, half = (c // 2) % B, c % 2
        out[b, half * TH:(half + 1) * TH] = res.results[c]["outT"].T
    return out
```

```python
import contextlib
import types
import numpy as np
import concourse.bass as bass
import concourse.mybir as mybir
from concourse.bass_utils import run_bass_kernel_spmd

F32 = mybir.dt.float32
BF16 = mybir.dt.bfloat16
AF = mybir.ActivationFunctionType
ALU = mybir.AluOpType

D = 2048; DFF = 5632; TT = 512; NKC = 16
SEQ = 8192; BATCH = 4; MEM = 256
INC = 8992
CDEC = 0.6065306597126334
import os as _os
SAME_ENGINE_SYNC = _os.environ.get("KSES", "1") == "1"

ENGS = ("pe", "act", "dve", "pool", "sp")


class Prog:
    def __init__(self):
        self.ops = []
        self.eng_ops = {e: [] for e in ENGS}
        self.last_w = {}
        self.readers = {}
        self.epoch = 0

    def op(self, eng, fn, reads=(), writes=(), dma=None):
        if fn is not None and fn.__closure__:
            cells = tuple(types.CellType(c.cell_contents) for c in fn.__closure__)
            fn = types.FunctionType(fn.__code__, fn.__globals__, fn.__name__, fn.__defaults__, cells)
        oid = len(self.ops)
        deps = set()
        for k in reads:
            w = self.last_w.get(k)
            if w is not None:
                deps.add(w)
        for k in writes:
            w = self.last_w.get(k)
            if w is not None:
                deps.add(w)
            for r in self.readers.get(k, ()):
                deps.add(r)
        for k in writes:
            self.last_w[k] = oid
            self.readers[k] = []
        ws = set(writes)
        for k in reads:
            if k not in ws:
                self.readers.setdefault(k, []).append(oid)
        deps.discard(oid)
        o = dict(id=oid, eng=eng, fn=fn, deps=deps, dma=dma, epoch=self.epoch, signal=(dma is not None))
        self.ops.append(o)
        self.eng_ops[eng].append(o)
        return oid

    def barrier(self, engs=("pe", "act", "dve", "pool")):
        lasts = {}
        for e in engs:
            for o in reversed(self.eng_ops[e]):
                if o["fn"] is not None and o["dma"] is None:
                    lasts[e] = o["id"]
                    break
        for e in engs:
            d = set(v for k, v in lasts.items() if k != e)
            o = dict(id=len(self.ops), eng=e, fn=None, deps=d, dma=None, epoch=self.epoch, signal=False)
            self.ops.append(o)
            self.eng_ops[e].append(o)

    def finalize(self, nc, stack, handles, final_dma_chans):
        ops = self.ops
        for o in ops:
            for d in o["deps"]:
                od = ops[d]
                if od["dma"] is not None:
                    continue
                if od["eng"] != o["eng"]:
                    od["signal"] = True
                elif SAME_ENGINE_SYNC and o["eng"] in ("act", "dve", "pool"):
                    od["signal"] = True
        sems = {}
        counts = {}

        def getsem(key):
            if key not in sems:
                sems[key] = stack.enter_context(nc.semaphore("s_%s_%s" % key))
                counts[key] = 0
            return sems[key]

        for o in ops:
            if not o["signal"]:
                continue
            if o["dma"] is not None:
                key = ("d" + str(o["dma"]), o["epoch"])
                getsem(key)
                counts[key] += 16
                o["sig"] = (key, counts[key], 16)
            else:
                key = (o["eng"], o["epoch"])
                getsem(key)
                counts[key] += 1
                o["sig"] = (key, counts[key], 1)
        self.nsems = len(sems)
        final_waits = []
        for ch in final_dma_chans:
            for key in sems:
                if key[0] == "d" + str(ch):
                    final_waits.append((key, counts[key]))

        def emit(engname, e):
            waited = {}
            for o in self.eng_ops[engname]:
                need = {}
                for d in o["deps"]:
                    od = ops[d]
                    if "sig" not in od:
                        continue
                    if od["dma"] is None and od["eng"] == engname and not (
                            SAME_ENGINE_SYNC and engname in ("act", "dve", "pool")):
                        continue
                    key, val, _ = od["sig"]
                    if val > need.get(key, 0):
                        need[key] = val
                for key, val in need.items():
                    if waited.get(key, 0) >= val:
                        continue
                    e.wait_ge(sems[key], val)
                    waited[key] = val
                if o["fn"] is None:
                    continue
                inst = o["fn"](e)
                if o["signal"]:
                    key, val, inc = o["sig"]
                    inst.then_inc(sems[key], inc)
            if engname == "sp":
                for key, val in final_waits:
                    e.wait_ge(sems[key], val)

        with nc.Block() as block:
            @block.tensor
            def _(e):
                emit("pe", e)

            @block.scalar
            def _(e):
                emit("act", e)

            @block.vector
            def _(e):
                emit("dve", e)

            @block.gpsimd
            def _(e):
                emit("pool", e)

            @block.sync
            def _(e):
                emit("sp", e)


Q_HEAD_ORDER = []
for _c in range(4):
    Q_HEAD_ORDER += [_c, 4 + _c]
for _c in range(4):
    Q_HEAD_ORDER += [8 + _c, 12 + _c]
Q_PERM = np.concatenate([np.arange(h * 64, (h + 1) * 64) for h in Q_HEAD_ORDER])


def _cols(v):
    v = np.asarray(v, np.float32).reshape(-1)
    n = (len(v) + 127) // 128 * 128
    p = np.zeros(n, np.float32)
    p[:len(v)] = v
    return p.reshape(-1, 128).T


class Vecs:
    def __init__(self):
        self.parts = []
        self.off = {}
        self.n = 0

    def add(self, name, arr):
        arr = np.asarray(arr, np.float32)
        assert arr.shape[0] == 128
        self.off[name] = self.n
        self.parts.append(arr)
        self.n += arr.shape[1]

    def build(self):
        return np.ascontiguousarray(np.concatenate(self.parts, axis=1))


def build_vecs(inp):
    V = Vecs()
    for nm in ("ffn1_norm", "mix_norm", "cross_norm", "ffn2_norm", "final_norm", "mem_norm"):
        V.add(nm, _cols(inp[nm]))
    mu = np.asarray(inp["rwkv_mu"], np.float32).reshape(-1)
    V.add("mu_rkv", _cols(mu[0:3072]))
    V.add("mu_xw", _cols(mu[3072:3136]))
    V.add("mu_xa", _cols(mu[3136:3200]))
    V.add("mu_xg0", _cols(mu[3200:3328]))
    V.add("mu_xg1", _cols(mu[3328:3360]))
    for nm in ("rwkv_w0", "rwkv_a0", "rwkv_k_k", "rwkv_k_a", "rwkv_r_k", "rwkv_ln_w", "rwkv_ln_b"):
        V.add(nm, _cols(inp[nm]))
    sk = np.asarray(inp["attn_sinks"], np.float32).reshape(-1)
    V.add("sinks", np.repeat(sk[None, :], 128, axis=0))
    return V


def build_consts(half=1):
    c = {}
    c["ident"] = np.eye(128, dtype=np.float32)
    c["ones"] = np.ones((128, 128), np.float32)
    bo = np.zeros((128, 128), np.float32)
    bo[:64, :64] = 1.0
    bo[64:, 64:] = 1.0
    c["bones"] = bo
    ki = np.arange(128)[:, None]
    qi = np.arange(128)[None, :]
    c["mcur"] = np.tile((qi >= ki).astype(np.float32), (1, 4))
    c["mprev"] = np.tile((qi < ki).astype(np.float32), (1, 4))
    c["flag"] = np.full((128, 1), float(half), np.float32)
    id64 = np.zeros((128, 512), np.float32)
    id64[:64] = np.tile(np.eye(64, dtype=np.float32), (1, 8))
    c["id64"] = id64
    s = np.arange(64)[:, None]
    t = np.arange(64)[None, :]
    strict = (t > s).astype(np.float32)
    incl = (t >= s).astype(np.float32)
    m = np.block([[strict, incl], [strict, incl]])
    c["mlt"] = np.tile(m, (1, 4))
    mlow = np.zeros((128, 512), np.float32)
    mlow[:64] = np.tile((s > t).astype(np.float32), (1, 8))
    c["mlow"] = mlow
    order = ["ident", "ones", "bones", "mcur", "mprev", "id64", "mlt", "mlow", "flag"]
    off = {}
    o = 0
    for k in order:
        off[k] = o
        o += c[k].shape[1]
    allc = np.concatenate([c[k] for k in order], axis=1)
    return np.ascontiguousarray(allc), off


NCB = 128 * 3 + 512 * 3


WNAMES = [
    ("ffn1_wg", D, DFF), ("ffn1_wu", D, DFF), ("ffn1_wd", DFF, D),
    ("w_in", D, INC), ("w_pa", 1024, D), ("w_pr", 1024, D), ("w_out", D, D),
    ("w_cq", D, 512), ("w_ckv", D, 1024), ("w_co", 512, D),
    ("ffn2_wg", D, DFF), ("ffn2_wu", D, DFF), ("ffn2_wd", DFF, D),
    ("w2", 64, 1024), ("a2", 64, 1024), ("g2", 160, 1024),
]


def build_program(ntiles, vec_off, nvec, c_off, ncst, debug=None, npre=0):
    nc = bass.Bass("TRN2", target_bir_lowering=False)
    ntok = ntiles * TT
    P = Prog()
    stack = contextlib.ExitStack()
    with stack:
        xT_d = nc.dram_tensor("xT", [D, ntok], F32, kind="ExternalInput").ap()
        memT_d = nc.dram_tensor("memT", [D, MEM], F32, kind="ExternalInput").ap()
        vecs_d = nc.dram_tensor("vecs", [128, nvec], F32, kind="ExternalInput").ap()
        cst_d = nc.dram_tensor("consts", [128, ncst], F32, kind="ExternalInput").ap()
        out_d = nc.dram_tensor("outT", [D, (ntiles - npre) * TT], F32, kind="ExternalOutput").ap()
        wf = {}
        wb = {}
        for nm, k, n in WNAMES:
            wf[nm] = nc.dram_tensor(nm, [k, n], F32, kind="ExternalInput").ap()
            wb[nm] = nc.dram_tensor(nm + "_bf", [k, n], BF16, kind="Internal").ap()
        dbg_d = {}
        if debug:
            for nm, shp in debug.items():
                dbg_d[nm] = nc.dram_tensor("dbg_" + nm, list(shp), F32, kind="ExternalOutput").ap()

        def sb(name, shape, dt):
            return stack.enter_context(nc.sbuf_tensor(name, list(shape), dt))

        xT = sb("xT_s", [128, NKC, TT], F32)
        hT = sb("hT_s", [128, NKC, TT], BF16)
        NWB = 3
        wbuf = [sb("wb%d" % i, [128, 8, 512], BF16) for i in range(NWB)]
        vecs = sb("vecs_s", [128, nvec], F32)
        cstf = sb("cstf", [128, ncst - NCB], F32)
        cstb = sb("cstb", [128, NCB], BF16)
        omka = sb("omka", [128, 8], F32)
        esink = sb("esink", [128, 16], F32)
        w2b = sb("w2b", [64, 1024], BF16)
        a2b = sb("a2b", [64, 1024], BF16)
        g2b = sb("g2b", [128, 2, 1024], BF16)
        kmT = sb("kmT", [128, 4, MEM], BF16)
        vm = sb("vm", [128, 2, 512], BF16)
        rstd = sb("rstd", [128, TT], F32)
        carry = sb("carry", [128, 32], F32)
        S0T = sb("S0T", [64, 8, 2, 64], F32)
        S0Tb = sb("S0Tb", [64, 8, 2, 64], BF16)
        ARB = sb("ARB", [64, 8, 2, 64], BF16)
        BKB = sb("BKB", [64, 8, 2, 64], BF16)
        WtotB = sb("WtotB", [64, 8], F32)
        Whl = sb("Whl", [128, 2, 8], BF16)
        kTh = sb("kTh", [128, 2, 128 + TT], BF16)
        vtok = sb("vtok", [128, 5, 256], BF16)
        NACT = 22
        act = sb("act", [128, NACT, TT], BF16)
        _al = act[:, 16:22, :].rearrange("p a b -> p (a b)")
        _alf = _al[:, 0:2056].bitcast(F32)
        Pst = [_alf[:, 0:TT + 1], _alf[:, TT + 1:2 * TT + 2]]
        gbuf = _al[:, 2056:2056 + 512]
        Bt = [sb("B0", [128, TT], BF16), sb("B1", [128, TT], BF16)]
        qT = sb("qT", [128, 8, TT], BF16)
        yr = sb("yr", [128, 8, TT], BF16)
        oT = yr
        xs_small = sb("xs_small", [128, 4, TT], BF16)
        xr = sb("xr", [128, TT], F32)
        xk = sb("xk", [128, TT], F32)
        xv = sb("xv", [128, TT], F32)
        Tt = [sb("T%d" % i, [128, TT], F32) for i in range(7)]
        den = Tt[5]
        ET = [Tt[i][:, :].bitcast(BF16).rearrange("p (a b) -> p a b", a=2) for i in range(2)]
        ysb = Tt[3]
        sqb = Bt[1]
        AR = sb("AR", [128, 8, 2, 64], BF16)
        BK = sb("BK", [128, 8, 2, 64], BF16)
        TM = sb("TM", [128, 8, 3, 64], BF16)
        TMt = sb("TMt", [64, 8, 3, 128], BF16)
        LT = sb("LT", [64, 16, 128], BF16)
        LTk = sb("LTk", [64, 16, 128], BF16)
        Wtot = sb("Wtot", [128, 8], F32)
        inv = {nm: sb("inv_" + nm, [64, 16, 64], BF16) for nm in ("L0", "N0", "L1", "N1", "Q", "P")}
        Zs = sb("Zs", [64, 2, 64], BF16)
        UV = sb("UV", [64, 2, 64], BF16)

        print("SBUF bytes remaining per partition:", nc.sbuf_bytes_remaining)
        ps = [stack.enter_context(nc.psum_tensor("ps%d" % i, [128, 512], F32)) for i in range(8)]
        psctr = [0]
        YB = 7

        def nextps():
            i = psctr[0] % 7
            psctr[0] += 1
            return i

        def vcol(name, j=0, rows=128):
            o = vec_off[name] + j
            return vecs[0:rows, o:o + 1]

        def cb(name, w, rows=128, c0=0):
            o = c_off[name] + c0
            return cstb[0:rows, o:o + w]

        def cf(name, w, rows=128, c0=0):
            o = c_off[name] + c0 - NCB
            return cstf[0:rows, o:o + w]

        for nm, k, n in WNAMES:
            step = 512
            for r0 in range(0, k, step):
                r1 = min(k, r0 + step)
                P.op("pool", lambda e, nm=nm, r0=r0, r1=r1: e.dma_start(out=wb[nm][r0:r1, :], in_=wf[nm][r0:r1, :]),
                     writes=[("wbf", nm)], dma="wc_" + nm)
        P.op("sp", lambda e: e.dma_start(out=vecs[:], in_=vecs_d[:, :]), writes=["vecs"], dma="misc1")
        xflat = xT[:, :, :].rearrange("p a b -> p (a b)")
        P.op("sp", lambda e: e.dma_start(out=xflat[:, 0:ncst], in_=cst_d[:, :]), writes=["xT"], dma="misc2")
        P.op("dve", lambda e: e.tensor_copy(out=cstb[:], in_=xflat[:, 0:NCB]), reads=["xT"], writes=["cst"])
        P.op("dve", lambda e: e.tensor_copy(out=cstf[:], in_=xflat[:, NCB:ncst]), reads=["xT"], writes=["cstf"])
        P.op("dve", lambda e: e.tensor_scalar(out=omka[:], in0=vecs[:, vec_off["rwkv_k_a"]:vec_off["rwkv_k_a"] + 8],
                                              scalar1=-1.0, scalar2=1.0, op0=ALU.mult, op1=ALU.add),
             reads=["vecs"], writes=["omka"])
        P.op("act", lambda e: e.activation(out=esink[:], in_=vecs[:, vec_off["sinks"]:vec_off["sinks"] + 16], func=AF.Exp),
             reads=["vecs"], writes=["esink"])
        P.op("sp", lambda e: e.dma_start(out=w2b[:], in_=wb["w2"][:, :]), reads=[("wbf", "w2")], writes=["w2b"], dma="misc3")
        P.op("sp", lambda e: e.dma_start(out=a2b[:], in_=wb["a2"][:, :]), reads=[("wbf", "a2")], writes=["a2b"], dma="misc4")
        P.op("sp", lambda e: e.dma_start(out=g2b[:, 0, :], in_=wb["g2"][0:128, :]), reads=[("wbf", "g2")], writes=["g2b"], dma="misc5")
        P.op("sp", lambda e: e.dma_start(out=g2b[0:32, 1, :], in_=wb["g2"][128:160, :]), reads=[("wbf", "g2")], writes=["g2b1"], dma="misc6")
        for t_ in (carry, S0T, S0Tb, kTh, vtok):
            P.op("pool", lambda e, t_=t_: e.memset(t_[:], 0.0), writes=["init_" + t_.name])
        INITK = ["init_" + t_.name for t_ in (carry, S0T, S0Tb, kTh, vtok)]

        wslot = [0]

        def load_panel(wname, kc0, nk, c0, ncol):
            s = wslot[0] % NWB
            wslot[0] += 1
            src = wb[wname][kc0 * 128:(kc0 + nk) * 128, c0:c0 + ncol].rearrange("(k p) n -> p k n", p=128)
            P.op("sp", lambda e, s=s, src=src, nk=nk, ncol=ncol: e.dma_start(out=wbuf[s][:, 0:nk, 0:ncol], in_=src),
                 reads=[("wbf", wname)], writes=[("wb", s)], dma="w%d" % s)
            return s

        def proj(wname, nkc_total, rhs_fn, rhs_keys, chunks, evac, pan_cols=512, ncols=TT, kc_base=0):
            groups = []
            cur = []
            for idx, (c0, w) in enumerate(chunks):
                if cur and (len(cur) == 4 or c0 + w - chunks[cur[0]][0] > pan_cols):
                    groups.append(cur)
                    cur = []
                cur.append(idx)
            if cur:
                groups.append(cur)
            for g in groups:
                gc0 = chunks[g[0]][0]
                gc1 = chunks[g[-1]][0] + chunks[g[-1]][1]
                banks = [nextps() for _ in g]
                for kp0 in range(0, nkc_total, 8):
                    nk = min(8, nkc_total - kp0)
                    s = load_panel(wname, kc_base + kp0, nk, gc0, gc1 - gc0)

                    def mmfn(e, s=s, kp0=kp0, nk=nk, g=g, banks=banks, gc0=gc0):
                        inst = None
                        for kk in range(nk):
                            kc = kp0 + kk
                            for idx, b in zip(g, banks):
                                c0, w = chunks[idx]
                                inst = e.matmul(ps[b][0:w, 0:ncols], lhsT=wbuf[s][:, kk, c0 - gc0:c0 - gc0 + w], rhs=rhs_fn(kc),
                                                start=(kc == 0), stop=(kc == nkc_total - 1))
                        return inst
                    P.op("pe", mmfn, reads=[("wb", s)] + list(rhs_keys), writes=[("ps", b) for b in banks])
                for idx, b in zip(g, banks):
                    evac(idx, b, chunks[idx][1])

        def rmsnorm(gname, src=None, src_key="xT", nrows=TT, dst=None, dst_key="hT"):
            src = xT if src is None else src
            dst = hT if dst is None else dst
            n = nrows
            P.op("act", lambda e: e.activation(out=dst[:, :, 0:n], in_=src[:, :, 0:n], func=AF.Square),
                 reads=[src_key], writes=[dst_key])
            b = nextps()

            def mm(e):
                inst = None
                for kc in range(NKC):
                    inst = e.matmul(ps[b][:, 0:n], lhsT=cb("ones", 128), rhs=dst[:, kc, 0:n], start=(kc == 0), stop=(kc == NKC - 1))
                return inst
            P.op("pe", mm, reads=[dst_key, "cst"], writes=[("ps", b)])
            P.op("act", lambda e: e.activation(out=rstd[:, 0:n], in_=ps[b][:, 0:n], func=AF.Ln, bias=1e-6, scale=1.0 / D),
                 reads=[("ps", b)], writes=["rstd"])
            P.op("act", lambda e: e.activation(out=rstd[:, 0:n], in_=rstd[:, 0:n], func=AF.Exp, scale=-0.5),
                 reads=["rstd"], writes=["rstd"])

            def sc(e):
                inst = None
                for kc in range(NKC):
                    inst = e.scalar_tensor_tensor(out=dst[:, kc, 0:n], in0=src[:, kc, 0:n], scalar=vcol(gname, kc), in1=rstd[:, 0:n],
                                                  op0=ALU.mult, op1=ALU.mult)
                return inst
            P.op("dve", sc, reads=[src_key, "rstd", "vecs"], writes=[dst_key])

        def ffn(pref, gname):
            rmsnorm(gname)
            hr = lambda kc: hT[:, kc, :]

            def ev_gate(idx, b, w):
                P.op("act", lambda e: e.activation(out=act[:, idx, :], in_=ps[b][:, :], func=AF.Silu),
                     reads=[("ps", b)], writes=[("act", idx)])

            def ev_up(idx, b, w):
                P.op("dve", lambda e: e.tensor_tensor(out=act[:, idx, :], in0=ps[b][:, :], in1=act[:, idx, :], op=ALU.mult),
                     reads=[("ps", b), ("act", idx)], writes=[("act", idx)])

            def ev_down(idx, b, w):
                P.op("dve", lambda e: e.scalar_tensor_tensor(out=xT[:, idx, :], in0=ps[b][:, :], scalar=0.5, in1=xT[:, idx, :],
                                                             op0=ALU.mult, op1=ALU.add),
                     reads=[("ps", b), "xT"], writes=["xT"])
            for hf in range(2):
                for g0 in range(0, NACT, 4):
                    n = min(4, NACT - g0)
                    sub = [((hf * NACT + g0 + i) * 128, 128) for i in range(n)]
                    proj(pref + "_wg", NKC, hr, ["hT"], sub, lambda i, b, w, g0=g0: ev_gate(g0 + i, b, w))
                    proj(pref + "_wu", NKC, hr, ["hT"], sub, lambda i, b, w, g0=g0: ev_up(g0 + i, b, w))
                proj(pref + "_wd", NACT, lambda kc: act[:, kc, :], [("act", j) for j in range(NACT)],
                     [(m * 128, 128) for m in range(16)], ev_down, kc_base=hf * NACT)

        def mem_kv():
            P.op("sp", lambda e: e.dma_start(out=xT[:, :, 0:MEM], in_=memT_d.rearrange("(k p) m -> p k m", p=128)),
                 writes=["xT"], dma="xin")
            rmsnorm("mem_norm", nrows=MEM)
            def ev_k(idx, b, w):
                P.op("act", lambda e: e.activation(out=kmT[:, idx, :], in_=ps[b][:, 0:MEM], func=AF.Copy),
                     reads=[("ps", b)], writes=["kmT"])
            proj("w_ckv", NKC, lambda kc: hT[:, kc, 0:MEM], ["hT"], [(h * 128, 128) for h in range(4)], ev_k, ncols=MEM)
            for mb in range(2):
                b = nextps()
                for kp0 in (0, 8):
                    s = load_panel("w_ckv", kp0, 8, 512, 512)

                    def mm(e, s=s, kp0=kp0, b=b, mb=mb):
                        inst = None
                        for kk in range(8):
                            kc = kp0 + kk
                            inst = e.matmul(ps[b][:, :], lhsT=hT[:, kc, mb * 128:(mb + 1) * 128], rhs=wbuf[s][:, kk, :],
                                            start=(kc == 0), stop=(kc == 15))
                        return inst
                    P.op("pe", mm, reads=[("wb", s), "hT"], writes=[("ps", b)])
                P.op("act", lambda e, b=b, mb=mb: e.activation(out=vm[:, mb, :], in_=ps[b][:, :], func=AF.Copy),
                     reads=[("ps", b)], writes=["vm"])

        RW0 = 1536

        def token_shift_evac(b, width, carry_col, mu_ap, dst_ap, dst_key, func=None, pidx=[0]):
            i = pidx[0] % 2
            pidx[0] += 1
            Pb = Pst[i]
            w = width
            P.op("pool", lambda e: e.tensor_copy(out=Pb[0:w, 0:1], in_=carry[0:w, carry_col:carry_col + 1]),
                 reads=["carry"] + INITK, writes=[("Pst", i)])
            P.op("act", lambda e: e.activation(out=Pb[0:w, 1:TT + 1], in_=ps[b][0:w, :], func=AF.Copy),
                 reads=[("ps", b)], writes=[("Pst", i)])
            P.op("pool", lambda e: e.tensor_copy(out=carry[0:w, carry_col:carry_col + 1], in_=Pb[0:w, TT:TT + 1]),
                 reads=[("Pst", i)], writes=["carry"])
            P.op("dve", lambda e: e.tensor_tensor(out=Tt[6][0:w, :], in0=Pb[0:w, 0:TT], in1=Pb[0:w, 1:TT + 1], op=ALU.subtract),
                 reads=[("Pst", i)], writes=["T6"])
            if func is None:
                P.op("dve", lambda e: e.scalar_tensor_tensor(out=dst_ap, in0=Tt[6][0:w, :], scalar=mu_ap, in1=Pb[0:w, 1:TT + 1],
                                                             op0=ALU.mult, op1=ALU.add),
                     reads=["T6", ("Pst", i), "vecs"], writes=[dst_key])
            else:
                P.op("dve", lambda e: e.scalar_tensor_tensor(out=Tt[6][0:w, :], in0=Tt[6][0:w, :], scalar=mu_ap, in1=Pb[0:w, 1:TT + 1],
                                                             op0=ALU.mult, op1=ALU.add),
                     reads=["T6", ("Pst", i), "vecs"], writes=["T6"])
                P.op("act", lambda e: e.activation(out=dst_ap, in_=Tt[6][0:w, :], func=func),
                     reads=["T6"], writes=[dst_key])

        def attention(tile_idx, pre=False):
            hrhs = lambda kc: hT[:, kc, :]
            if pre and tile_idx != npre - 1:
                return
            def ev_q(idx, b, w):
                P.op("act", lambda e: e.activation(out=qT[:, idx, :], in_=ps[b][:, :], func=AF.Copy),
                     reads=[("ps", b)], writes=[("qT", idx)])
            if not pre:
                proj("w_in", NKC, hrhs, ["hT"], [(c * 128, 128) for c in range(8)], ev_q)

            def ev_k(idx, b, w):
                P.op("act", lambda e: e.activation(out=kTh[:, idx, 128:128 + TT], in_=ps[b][:, :], func=AF.Copy),
                     reads=[("ps", b)] + INITK, writes=["kTh"])
            proj("w_in", NKC, hrhs, ["hT"], [(1024 + c * 128, 128) for c in range(2)], ev_k)
            for tb in range(4):
                b = nextps()
                for kp0 in (0, 8):
                    s = load_panel("w_in", kp0, 8, 1280, 256)

                    def mm(e, s=s, kp0=kp0, b=b, tb=tb):
                        inst = None
                        for kk in range(8):
                            kc = kp0 + kk
                            inst = e.matmul(ps[b][:, 0:256], lhsT=hT[:, kc, tb * 128:(tb + 1) * 128], rhs=wbuf[s][:, kk, 0:256],
                                            start=(kc == 0), stop=(kc == 15))
                        return inst
                    P.op("pe", mm, reads=[("wb", s), "hT"], writes=[("ps", b)])
                P.op("act", lambda e, b=b, tb=tb: e.activation(out=vtok[:, tb + 1, :], in_=ps[b][:, 0:256], func=AF.Copy),
                     reads=[("ps", b)] + INITK, writes=["vtok"])
            ei = 0
            for g in range(4 if not pre else 0):
                kch, kbase = g // 2, (g % 2) * 64
                qc0 = 0 if g < 2 else 4
                for n in range(4):
                    first = (npre == 0 and tile_idx == 0 and n == 0)
                    firstown = (npre > 0 and tile_idx == npre and n == 0)
                    Eb = ET[ei % 2]
                    ekey = "T%d" % (ei % 2)
                    ei += 1
                    bprev, bcur = nextps(), nextps()
                    qrhs = qT[kbase:kbase + 64, qc0:qc0 + 4, n * 128:(n + 1) * 128]

                    def mm(e, kch=kch, kbase=kbase, n=n, bprev=bprev, bcur=bcur, qrhs=qrhs, first=first):
                        inst = None
                        if not first:
                            inst = e.matmul(ps[bprev][:, :], lhsT=kTh[kbase:kbase + 64, kch, n * 128:(n + 1) * 128], rhs=qrhs,
                                            start=True, stop=True)
                        inst = e.matmul(ps[bcur][:, :], lhsT=kTh[kbase:kbase + 64, kch, (n + 1) * 128:(n + 2) * 128], rhs=qrhs,
                                        start=True, stop=True)
                        return inst
                    P.op("pe", mm, reads=["kTh"] + [("qT", qc0 + i) for i in range(4)], writes=[("ps", bprev), ("ps", bcur)])
                    if not first:
                        P.op("act", lambda e, Eb=Eb, bprev=bprev: e.activation(out=Eb[:, 0, :], in_=ps[bprev][:, :], func=AF.Exp, scale=0.125),
                             reads=[("ps", bprev)], writes=[ekey])
                    P.op("act", lambda e, Eb=Eb, bcur=bcur: e.activation(out=Eb[:, 1, :], in_=ps[bcur][:, :], func=AF.Exp, scale=0.125),
                         reads=[("ps", bcur)], writes=[ekey])
                    if not first:
                        if firstown:
                            P.op("dve", lambda e, Eb=Eb: e.scalar_tensor_tensor(out=Eb[:, 0, :], in0=Eb[:, 0, :], scalar=cf("flag", 1), in1=cb("mprev", 512),
                                                                                 op0=ALU.mult, op1=ALU.mult),
                                 reads=[ekey, "cst", "cstf"], writes=[ekey])
                        else:
                            P.op("pool", lambda e, Eb=Eb: e.tensor_tensor(out=Eb[:, 0, :], in0=Eb[:, 0, :], in1=cb("mprev", 512), op=ALU.mult),
                                 reads=[ekey, "cst"], writes=[ekey])
                    P.op("pool", lambda e, Eb=Eb: e.tensor_tensor(out=Eb[:, 1, :], in0=Eb[:, 1, :], in1=cb("mcur", 512), op=ALU.mult),
                         reads=[ekey, "cst"], writes=[ekey])
                    bo, bd = nextps(), nextps()

                    def mm2(e, Eb=Eb, g=g, n=n, bo=bo, bd=bd, kbase=kbase, first=first):
                        inst = None
                        for hh in range(4):
                            osl = ps[bo][kbase:kbase + 64, hh * 128:(hh + 1) * 128]
                            dsl = ps[bd][kbase:kbase + 64, hh * 128:(hh + 1) * 128]
                            if not first:
                                e.matmul(osl, lhsT=vtok[:, n, g * 64:(g + 1) * 64], rhs=Eb[:, 0, hh * 128:(hh + 1) * 128], start=True, stop=False)
                            e.matmul(osl, lhsT=vtok[:, n + 1, g * 64:(g + 1) * 64], rhs=Eb[:, 1, hh * 128:(hh + 1) * 128], start=first, stop=True)
                            if not first:
                                e.matmul(dsl, lhsT=cb("ones", 64), rhs=Eb[:, 0, hh * 128:(hh + 1) * 128], start=True, stop=False)
                            inst = e.matmul(dsl, lhsT=cb("ones", 64), rhs=Eb[:, 1, hh * 128:(hh + 1) * 128], start=first, stop=True)
                        return inst
                    P.op("pe", mm2, reads=[ekey, "vtok", "cst"], writes=[("ps", bo), ("ps", bd)])
                    def dn(e, bd=bd, kbase=kbase, g=g):
                        inst = None
                        for hh in range(4):
                            h = 4 * g + hh
                            inst = e.tensor_scalar(out=den[kbase:kbase + 64, hh * 128:(hh + 1) * 128],
                                                   in0=ps[bd][kbase:kbase + 64, hh * 128:(hh + 1) * 128],
                                                   scalar1=esink[kbase:kbase + 64, h:h + 1], scalar2=None, op0=ALU.add)
                        return inst
                    P.op("dve", dn, reads=[("ps", bd), "esink"], writes=["T5"])
                    P.op("dve", lambda e, kbase=kbase: e.reciprocal(out=den[kbase:kbase + 64, :], in_=den[kbase:kbase + 64, :]),
                         reads=["T5"], writes=["T5"])

                    def yo(e, bo=bo, kbase=kbase, qc0=qc0, n=n):
                        return e.tensor_tensor(out=qT[kbase:kbase + 64, qc0:qc0 + 4, n * 128:(n + 1) * 128],
                                               in0=ps[bo][kbase:kbase + 64, :].rearrange("p (h q) -> p h q", h=4),
                                               in1=den[kbase:kbase + 64, :].rearrange("p (h q) -> p h q", h=4), op=ALU.mult)
                    P.op("dve", yo, reads=[("ps", bo), "T5"], writes=[("qT", qc0 + i) for i in range(4)])
            P.op("pool", lambda e: e.tensor_copy(out=kTh[:, :, 0:128], in_=kTh[:, :, TT:TT + 128]), reads=["kTh"], writes=["kTh"])
            P.op("pool", lambda e: e.tensor_copy(out=vtok[:, 0, :], in_=vtok[:, 4, :]), reads=["vtok"], writes=["vtok"])

        def rwkv(tile_idx, pre=False):
            hrhs = lambda kc: hT[:, kc, :]
            small = [(RW0 + 3072, 64, "mu_xw", AF.Tanh, 0, 24), (RW0 + 3136, 64, "mu_xa", AF.Copy, 1, 25),
                     (RW0 + 3200, 128, "mu_xg0", AF.Sigmoid, 2, 26), (RW0 + 3328, 32, "mu_xg1", AF.Sigmoid, 3, 27)]

            def ev_small(idx, b, w):
                c0, ww, mun, fn, slot, ccol = small[idx]
                token_shift_evac(b, ww, ccol, vcol(mun, 0, ww), xs_small[0:ww, slot, :], ("xs_small", slot), func=fn)
            proj("w_in", NKC, hrhs, ["hT"], [(s_[0], s_[1]) for s_ in small], ev_small)

            KSUB = int(os.environ.get("KSUB", "9"))
            if KSUB < 2:
                return
            for ch in range(8):
                dsts = [(xr, "xr"), (xk, "xk"), (xv, "xv")]

                def ev_rkv(idx, b, w, ch=ch):
                    token_shift_evac(b, 128, idx * 8 + ch, vcol("mu_rkv", idx * 8 + ch), dsts[idx][0][:, :], dsts[idx][1])
                for idx in range(3):
                    proj("w_in", NKC, hrhs, ["hT"], [(RW0 + idx * 1024 + ch * 128, 128)],
                         lambda i, b, w, idx=idx: ev_rkv(idx, b, w))
                T = Tt
                csl = slice(ch * 128, (ch + 1) * 128)
                b1 = nextps()
                P.op("pe", lambda e, b1=b1: e.matmul(ps[b1][:, :], lhsT=w2b[:, csl], rhs=xs_small[0:64, 0, :], start=True, stop=True),
                     reads=["w2b", ("xs_small", 0)], writes=[("ps", b1)])
                P.op("act", lambda e, b1=b1: e.activation(out=T[0][:, :], in_=ps[b1][:, :], func=AF.Sigmoid, bias=vcol("rwkv_w0", ch)),
                     reads=[("ps", b1), "vecs"], writes=["T0"])
                P.op("dve", lambda e: e.tensor_tensor_scan(out=T[1][:, :], data0=T[0][:, :], data1=T[0][:, :], initial=0.0,
                                                           op0=ALU.add, op1=ALU.bypass),
                     reads=["T0"], writes=["T1"])
                P.op("dve", lambda e: e.tensor_copy(out=T[2][:, 0:64], in_=T[1][:, 0:64]), reads=["T1"], writes=["T2"])

                def lwf(e):
                    base = T[1][:, 63:63 + 448].rearrange("p (c s) -> p c s", s=64)[:, :, 0:1].to_broadcast([128, 7, 64])
                    return e.tensor_tensor(out=T[2][:, 64:512].rearrange("p (c s) -> p c s", s=64),
                                           in0=T[1][:, 64:512].rearrange("p (c s) -> p c s", s=64), in1=base, op=ALU.subtract)
                P.op("dve", lwf, reads=["T1", "T2"], writes=["T2"])
                P.op("dve", lambda e: e.tensor_tensor(out=T[1][:, :], in0=T[2][:, :], in1=T[0][:, :], op=ALU.subtract),
                     reads=["T2", "T0", "T1"], writes=["T1"])

                def lrem(e):
                    last = T[2][:, :].rearrange("p (c s) -> p c s", s=64)[:, :, 63:64].to_broadcast([128, 8, 64])
                    return e.tensor_tensor(out=T[0][:, :].rearrange("p (c s) -> p c s", s=64), in0=last,
                                           in1=T[2][:, :].rearrange("p (c s) -> p c s", s=64), op=ALU.subtract)
                P.op("dve", lrem, reads=["T2", "T1", "T0"], writes=["T0"])
                P.op("act", lambda e: e.activation(out=Wtot[:, :], in_=T[2][:, :].rearrange("p (c s) -> p c s", s=64)[:, :, 63],
                                                   func=AF.Exp, scale=-CDEC), reads=["T2"], writes=["Wtot"])
                P.op("act", lambda e: e.activation(out=T[3][:, :], in_=T[2][:, :], func=AF.Exp, scale=-CDEC), reads=["T2"], writes=["T3"])
                P.op("act", lambda e: e.activation(out=T[2][:, :], in_=T[2][:, :], func=AF.Exp, scale=CDEC), reads=["T2", "T3", "Wtot"], writes=["T2"])
                P.op("act", lambda e: e.activation(out=T[1][:, :], in_=T[1][:, :], func=AF.Exp, scale=-CDEC), reads=["T1"], writes=["T1"])
                P.op("act", lambda e: e.activation(out=T[0][:, :], in_=T[0][:, :], func=AF.Exp, scale=-CDEC), reads=["T0"], writes=["T0"])
                P.op("dve", lambda e: e.tensor_scalar(out=T[4][:, :], in0=xk[:, :], scalar1=vcol("rwkv_k_k", ch), scalar2=None, op0=ALU.mult),
                     reads=["xk", "vecs"], writes=["T4"])
                P.op("act", lambda e: e.activation(out=Bt[0][:, :], in_=T[4][:, :], func=AF.Square), reads=["T4"], writes=["B0"])
                b2 = nextps()
                P.op("pe", lambda e, b2=b2: e.matmul(ps[b2][:, :], lhsT=cb("bones", 128), rhs=Bt[0][:, :], start=True, stop=True),
                     reads=["B0", "cst"], writes=[("ps", b2)])
                P.op("dve", lambda e, b2=b2: e.tensor_scalar(out=T[5][:, :], in0=ps[b2][:, :], scalar1=1e-24, scalar2=None, op0=ALU.max),
                     reads=[("ps", b2)], writes=["T5"])
                P.op("act", lambda e: e.activation(out=T[5][:, :], in_=T[5][:, :], func=AF.Ln), reads=["T5"], writes=["T5"])
                P.op("act", lambda e: e.activation(out=T[5][:, :], in_=T[5][:, :], func=AF.Exp, scale=-0.5), reads=["T5"], writes=["T5"])
                P.op("dve", lambda e: e.tensor_tensor(out=T[4][:, :], in0=T[4][:, :], in1=T[5][:, :], op=ALU.mult),
                     reads=["T4", "T5"], writes=["T4"])
                b3 = nextps()
                P.op("pe", lambda e, b3=b3: e.matmul(ps[b3][:, :], lhsT=a2b[:, csl], rhs=xs_small[0:64, 1, :], start=True, stop=True),
                     reads=["a2b", ("xs_small", 1)], writes=[("ps", b3)])
                P.op("act", lambda e, b3=b3: e.activation(out=T[5][:, :], in_=ps[b3][:, :], func=AF.Sigmoid, bias=vcol("rwkv_a0", ch)),
                     reads=[("ps", b3), "vecs", "T5", "T4"], writes=["T5"])
                P.op("dve", lambda e: e.tensor_scalar(out=T[6][:, :], in0=T[5][:, :], scalar1=vcol("rwkv_k_a", ch), scalar2=omka[:, ch:ch + 1],
                                                      op0=ALU.mult, op1=ALU.add), reads=["T5", "vecs", "omka"], writes=["T6"])
                P.op("dve", lambda e: e.tensor_tensor(out=T[6][:, :], in0=T[6][:, :], in1=xk[:, :], op=ALU.mult), reads=["T6", "xk"], writes=["T6"])
                P.op("dve", lambda e: e.tensor_tensor(out=T[5][:, :], in0=T[5][:, :], in1=T[4][:, :], op=ALU.mult), reads=["T5", "T4"], writes=["T5"])
                if not pre:
                    b4 = nextps()

                    def gmm(e, b4=b4):
                        e.matmul(ps[b4][:, :], lhsT=g2b[:, 0, csl], rhs=xs_small[:, 2, :], start=True, stop=False)
                        return e.matmul(ps[b4][:, :], lhsT=g2b[0:32, 1, csl], rhs=xs_small[0:32, 3, :], start=False, stop=True)
                    P.op("pe", gmm, reads=["g2b", "g2b1", ("xs_small", 2), ("xs_small", 3)], writes=[("ps", b4)])
                    P.op("act", lambda e, b4=b4: e.activation(out=gbuf[:, :], in_=ps[b4][:, :], func=AF.Copy), reads=[("ps", b4)], writes=["gbuf"])
                v3 = lambda t_: t_[:, :].rearrange("p (c s) -> p c s", s=64)
                P.op("dve", lambda e: e.scalar_tensor_tensor(out=AR[:, :, 0, :], in0=v3(T[4]), scalar=-1.0, in1=v3(T[1]), op0=ALU.mult, op1=ALU.mult),
                     reads=["T4", "T1"], writes=["AR"])
                P.op("pool", lambda e: e.tensor_tensor(out=AR[:, :, 1, :], in0=v3(xr), in1=v3(T[3]), op=ALU.mult), reads=["xr", "T3"], writes=["AR1"])
                P.op("dve", lambda e: e.tensor_tensor(out=BK[:, :, 0, :], in0=v3(T[5]), in1=v3(T[2]), op=ALU.mult), reads=["T5", "T2"], writes=["BK"])
                P.op("pool", lambda e: e.tensor_tensor(out=BK[:, :, 1, :], in0=v3(T[6]), in1=v3(T[2]), op=ALU.mult), reads=["T6", "T2"], writes=["BK1"])
                P.op("dve", lambda e: e.tensor_tensor(out=TM[:, :, 1, :], in0=v3(T[5]), in1=v3(T[0]), op=ALU.mult), reads=["T5", "T0"], writes=["TM1"])
                P.op("pool", lambda e: e.tensor_tensor(out=TM[:, :, 2, :], in0=v3(T[6]), in1=v3(T[0]), op=ALU.mult), reads=["T6", "T0"], writes=["TM2"])
                P.op("act", lambda e: e.activation(out=TM[:, :, 0, :], in_=v3(xv), func=AF.Copy), reads=["xv"], writes=["TM0"])
                selB = cstb[:, c_off["ident"] + 64:c_off["ident"] + 128]
                for (X, XB, kx, kxb) in ((AR, ARB, ("AR", "AR1"), "ARB"), (BK, BKB, ("BK", "BK1"), "BKB")):
                    for hf4 in range(2):
                        bx = nextps()
                        P.op("pe", lambda e, X=X, hf4=hf4, bx=bx: e.matmul(
                            ps[bx][0:64, :], lhsT=selB, rhs=X[:, hf4 * 4:(hf4 + 1) * 4, :, :].rearrange("p c a s -> p (c a s)"),
                            start=True, stop=True), reads=list(kx) + ["cst"], writes=[("ps", bx)])
                        if hf4 == 0:
                            P.op("act", lambda e, XB=XB, bx=bx: e.activation(out=XB[:, 0:4, :, :].rearrange("p c a s -> p (c a s)"),
                                                                          in_=ps[bx][0:64, :], func=AF.Copy),
                                 reads=[("ps", bx)], writes=[kxb])
                        else:
                            P.op("dve", lambda e, XB=XB, bx=bx: e.tensor_copy(out=XB[:, 4:8, :, :].rearrange("p c a s -> p (c a s)"),
                                                                           in_=ps[bx][0:64, :]),
                                 reads=[("ps", bx)], writes=[kxb + "h"])
                P.op("dve", lambda e: e.tensor_copy(out=Whl[:, 0, :], in_=Wtot[:, :]), reads=["Wtot"], writes=["Whl"])
                P.op("dve", lambda e: e.tensor_tensor(out=Whl[:, 1, :], in0=Wtot[:, :], in1=Whl[:, 0, :], op=ALU.subtract),
                     reads=["Wtot", "Whl"], writes=["Whl"])
                bw = nextps()

                def wmm(e, bw=bw):
                    e.matmul(ps[bw][0:64, 0:8], lhsT=selB, rhs=Whl[:, 0, :], start=True, stop=False)
                    return e.matmul(ps[bw][0:64, 0:8], lhsT=selB, rhs=Whl[:, 1, :], start=False, stop=True)
                P.op("pe", wmm, reads=["Whl", "cst"], writes=[("ps", bw)])
                P.op("dve", lambda e, bw=bw: e.tensor_copy(out=WtotB[:, :], in_=ps[bw][0:64, 0:8]), reads=[("ps", bw)], writes=["WtotB"])
                ARx = (AR, ARB)
                BKx = (BK, BKB)
                XKEYS = ["AR", "AR1", "BK", "BK1", "ARB", "ARBh", "BKB", "BKBh"]
                if not pre:
                    P.op("dve", lambda e: e.tensor_tensor(out=T[4][:, :], in0=xr[:, :], in1=T[6][:, :], op=ALU.mult), reads=["xr", "T6", "T4", "AR"], writes=["T4"])
                    P.op("dve", lambda e: e.tensor_scalar(out=Bt[0][:, :], in0=T[4][:, :], scalar1=vcol("rwkv_r_k", ch), scalar2=None, op0=ALU.mult),
                         reads=["T4", "vecs"], writes=["B0"])
                    b5 = nextps()
                    P.op("pe", lambda e, b5=b5: e.matmul(ps[b5][:, :], lhsT=cb("bones", 128), rhs=Bt[0][:, :], start=True, stop=True),
                         reads=["B0", "cst"], writes=[("ps", b5)])
                    P.op("dve", lambda e, b5=b5: e.tensor_tensor(out=T[4][:, :], in0=ps[b5][:, :], in1=xv[:, :], op=ALU.mult),
                         reads=[("ps", b5), "xv"], writes=["T4"])
                if KSUB < 3:
                    continue
                K3 = int(os.environ.get("K3", "7"))
                for c2 in range(8):
                    if not (K3 & 1):
                        break
                    bt_ = nextps()

                    def trf(e, c2=c2, bt_=bt_):
                        inst = None
                        for k3 in range(3):
                            inst = e.matmul(ps[bt_][0:64, k3 * 128:(k3 + 1) * 128], lhsT=TM[:, c2, k3, :], rhs=cb("ident", 128), start=True, stop=True)
                        return inst
                    P.op("pe", trf, reads=["TM0", "TM1", "TM2", "cst"], writes=[("ps", bt_)])
                    P.op("dve", lambda e, c2=c2, bt_=bt_: e.tensor_copy(out=TMt[:, c2, :, :].rearrange("p k f -> p (k f)"), in_=ps[bt_][0:64, 0:384]),
                         reads=[("ps", bt_)], writes=["TMt"])
                for (dstL, kdst, slot) in ((LT, "LT", 0), (LTk, "LTk", 1)):
                    for q4 in range(4):
                        if not (K3 & 2):
                            break
                        bl = nextps()

                        def lmm(e, q4=q4, bl=bl, slot=slot):
                            inst = None
                            for ii in range(4):
                                pi = q4 * 4 + ii
                                c, hd = pi // 2, pi % 2
                                inst = e.matmul(ps[bl][0:64, ii * 128:(ii + 1) * 128], lhsT=BKx[hd][0:64, c, slot, :],
                                                rhs=ARx[hd][0:64, c, :, :].rearrange("p a s -> p (a s)"), start=True, stop=True)
                            return inst
                        P.op("pe", lmm, reads=XKEYS, writes=[("ps", bl)])
                        if not (K3 & 4):
                            continue
                        P.op("dve", lambda e, q4=q4, bl=bl, dstL=dstL: e.tensor_tensor(
                            out=dstL[:, q4 * 4:(q4 + 1) * 4, :].rearrange("p a s -> p (a s)"),
                            in0=ps[bl][0:64, :], in1=cf("mlt", 512, rows=64), op=ALU.mult),
                            reads=[("ps", bl), "cstf"], writes=[kdst])
                if KSUB < 4:
                    continue
                f2 = lambda t_, hlf: t_[:, hlf * 8:(hlf + 1) * 8, :].rearrange("p a s -> p (a s)")
                for hlf in range(2):
                    bl = nextps()

                    def l0mm(e, hlf=hlf, bl=bl):
                        inst = None
                        for ii in range(8):
                            pi = hlf * 8 + ii
                            c, hd = pi // 2, pi % 2
                            inst = e.matmul(ps[bl][0:64, ii * 64:(ii + 1) * 64], lhsT=ARx[hd][0:64, c, 0, :], rhs=BKx[hd][0:64, c, 0, :], start=True, stop=True)
                        return inst
                    P.op("pe", l0mm, reads=XKEYS, writes=[("ps", bl)])
                    P.op("dve", lambda e, hlf=hlf, bl=bl: e.tensor_tensor(out=f2(inv["L0"], hlf), in0=ps[bl][0:64, :],
                                                                          in1=cf("mlow", 512, rows=64), op=ALU.mult),
                         reads=[("ps", bl), "cstf"], writes=["inv_L0"])
                P.op("pool", lambda e: e.tensor_copy(out=inv["N0"][:, :, :], in_=LT[:, :, 0:64]), reads=["LT"], writes=["inv_N0"])
                for hlf in range(2):
                    P.op("pool", lambda e, hlf=hlf: e.tensor_tensor(out=f2(inv["Q"], hlf), in0=f2(inv["N0"], hlf), in1=cb("id64", 512, rows=64), op=ALU.add),
                         reads=["inv_N0", "cst"], writes=["inv_Q"])
                    P.op("pool", lambda e, hlf=hlf: e.tensor_tensor(out=f2(inv["P"], hlf), in0=f2(inv["L0"], hlf), in1=cb("id64", 512, rows=64), op=ALU.add),
                         reads=["inv_L0", "cst"], writes=["inv_P"])
                Qm, Pm = inv["Q"], inv["P"]

                def mm8(lt, rh, hlf, bq):
                    def f(e):
                        inst = None
                        for ii in range(8):
                            pi = hlf * 8 + ii
                            inst = e.matmul(ps[bq][0:64, ii * 64:(ii + 1) * 64], lhsT=lt[:, pi, :], rhs=rh[:, pi, :], start=True, stop=True)
                        return inst
                    return f

                def do_sq(cur, nxt):
                    Lc, Nc, Ln_, Nn = inv["L" + cur], inv["N" + cur], inv["L" + nxt], inv["N" + nxt]
                    kLc, kNc, kLn, kNn = "inv_L" + cur, "inv_N" + cur, "inv_L" + nxt, "inv_N" + nxt
                    for (dst, kdst, lt, rh) in ((Ln_, kLn, Nc, Lc), (Nn, kNn, Lc, Nc)):
                        for hlf in range(2):
                            bq = nextps()
                            P.op("pe", mm8(lt, rh, hlf, bq), reads=[kLc, kNc], writes=[("ps", bq)])
                            if hlf == 0:
                                P.op("act", lambda e, hlf=hlf, bq=bq, dst=dst: e.activation(out=f2(dst, hlf), in_=ps[bq][0:64, :], func=AF.Copy),
                                     reads=[("ps", bq)], writes=[kdst])
                            else:
                                P.op("dve", lambda e, hlf=hlf, bq=bq, dst=dst: e.tensor_copy(out=f2(dst, hlf), in_=ps[bq][0:64, :]),
                                     reads=[("ps", bq)], writes=[kdst])

                def do_upd(nxt, last):
                    Ln_, Nn = inv["L" + nxt], inv["N" + nxt]
                    kLn, kNn = "inv_L" + nxt, "inv_N" + nxt
                    pend = []
                    for (dst, kdst, lt, rh, krh) in ((Qm, "inv_Q", Pm, Nn, kNn), (Pm, "inv_P", Qm, Ln_, kLn)):
                        if last and dst is Pm:
                            continue
                        for hlf in range(2):
                            bq = nextps()
                            P.op("pe", mm8(lt, rh, hlf, bq), reads=["inv_Q", "inv_P", krh], writes=[("ps", bq)])
                            pend.append((dst, kdst, hlf, bq))
                    for (dst, kdst, hlf, bq) in pend:
                        P.op("dve", lambda e, hlf=hlf, bq=bq, dst=dst: e.tensor_tensor(out=f2(dst, hlf), in0=ps[bq][0:64, :], in1=f2(dst, hlf), op=ALU.add),
                             reads=[("ps", bq), kdst], writes=[kdst])
                bufs = ["0", "1"]
                do_sq(bufs[0], bufs[1])
                for lvl in range(5):
                    cur, nxt = bufs[lvl % 2], bufs[(lvl + 1) % 2]
                    if lvl < 4:
                        do_sq(nxt, cur)
                    do_upd(nxt, lvl == 4)
                if KSUB < 5:
                    continue
                TTm = Qm
                kTT = "inv_Q"
                sk = ("S0T", ch)
                for c in range(8):
                    bz = nextps()

                    def zmm(e, c=c, bz=bz):
                        inst = None
                        for hd in range(2):
                            o = ps[bz][0:64, hd * 64:(hd + 1) * 64]
                            e.matmul(o, lhsT=ARx[hd][0:64, c, 0, :], rhs=S0Tb[:, ch, hd, :], start=True, stop=False)
                            inst = e.matmul(o, lhsT=LTk[:, c * 2 + hd, 0:64], rhs=TMt[:, c, 0, hd * 64:(hd + 1) * 64], start=False, stop=True)
                        return inst
                    P.op("pe", zmm, reads=XKEYS + [sk, "LTk", "TMt"] + INITK, writes=[("ps", bz)])
                    P.op("act", lambda e, bz=bz: e.activation(out=Zs[:, :, :].rearrange("p a s -> p (a s)"), in_=ps[bz][0:64, 0:128], func=AF.Copy),
                         reads=[("ps", bz)], writes=["Zs"])
                    bu = nextps()

                    def umm(e, c=c, bu=bu):
                        inst = None
                        for hd in range(2):
                            inst = e.matmul(ps[bu][0:64, hd * 64:(hd + 1) * 64], lhsT=TTm[:, c * 2 + hd, :], rhs=Zs[:, hd, :], start=True, stop=True)
                        return inst
                    P.op("pe", umm, reads=[kTT, "Zs"], writes=[("ps", bu)])
                    P.op("act", lambda e, bu=bu: e.activation(out=UV[:, :, :].rearrange("p a s -> p (a s)"), in_=ps[bu][0:64, 0:128], func=AF.Copy),
                         reads=[("ps", bu)], writes=["UV"])

                    def ymm(e, c=c):
                        inst = None
                        for hd in range(2):
                            rows = slice(hd * 64, hd * 64 + 64)
                            o = ps[YB][rows, c * 64:(c + 1) * 64]
                            e.matmul(o, lhsT=S0Tb[:, ch, hd, :], rhs=ARx[hd][0:64, c, 1, :], start=True, stop=False)
                            e.matmul(o, lhsT=UV[:, hd, :], rhs=LT[:, c * 2 + hd, 64:128], start=False, stop=False)
                            inst = e.matmul(o, lhsT=TMt[:, c, 0, hd * 64:(hd + 1) * 64], rhs=LTk[:, c * 2 + hd, 64:128], start=False, stop=True)
                        return inst
                    if not pre:
                        P.op("pe", ymm, reads=XKEYS + [sk, "UV", "LT", "LTk", "TMt"], writes=[("ps", YB)])
                    bs = nextps()

                    def smm(e, c=c, bs=bs):
                        inst = None
                        for hd in range(2):
                            o = ps[bs][0:64, hd * 64:(hd + 1) * 64]
                            e.matmul(o, lhsT=TMt[:, c, 1, hd * 64:(hd + 1) * 64], rhs=UV[:, hd, :], start=True, stop=False)
                            inst = e.matmul(o, lhsT=TMt[:, c, 2, hd * 64:(hd + 1) * 64], rhs=TMt[:, c, 0, hd * 64:(hd + 1) * 64], start=False, stop=True)
                        return inst
                    P.op("pe", smm, reads=["TMt", "UV"], writes=[("ps", bs)])

                    def supd(e, c=c, bs=bs):
                        e.scalar_tensor_tensor(out=S0T[:, ch, 0, :], in0=S0T[:, ch, 0, :], scalar=Wtot[0:64, c:c + 1], in1=ps[bs][0:64, 0:64],
                                               op0=ALU.mult, op1=ALU.add)
                        return e.scalar_tensor_tensor(out=S0T[:, ch, 1, :], in0=S0T[:, ch, 1, :], scalar=WtotB[:, c:c + 1], in1=ps[bs][0:64, 64:128],
                                                      op0=ALU.mult, op1=ALU.add)
                    P.op("dve", supd, reads=[("ps", bs), "Wtot", "WtotB"] + INITK, writes=[("S0Tf", ch)])
                    P.op("dve", lambda e: e.tensor_copy(out=S0Tb[:, ch, :, :].rearrange("p a s -> p (a s)"),
                                                        in_=S0T[:, ch, :, :].rearrange("p a s -> p (a s)")),
                         reads=[("S0Tf", ch)], writes=[sk])
                if KSUB < 6 or pre:
                    continue
                P.op("act", lambda e: e.activation(out=ysb[:, :], in_=ps[YB][:, :], func=AF.Copy), reads=[("ps", YB), "T3"], writes=["T3"])
                P.op("dve", lambda e: e.tensor_copy(out=sqb[:, :], in_=ysb[:, :]), reads=["T3"], writes=["sqb"])
                bm = nextps()
                P.op("pe", lambda e, bm=bm: e.matmul(ps[bm][:, :], lhsT=cb("bones", 128), rhs=sqb[:, :], start=True, stop=True),
                     reads=["sqb", "cst"], writes=[("ps", bm)])
                P.op("dve", lambda e, bm=bm: e.scalar_tensor_tensor(out=ysb[:, :], in0=ps[bm][:, :], scalar=-1.0 / 64, in1=ysb[:, :], op0=ALU.mult, op1=ALU.add),
                     reads=[("ps", bm), "T3"], writes=["T3"])
                P.op("act", lambda e: e.activation(out=sqb[:, :], in_=ysb[:, :], func=AF.Square), reads=["T3", "sqb"], writes=["sqb"])
                bv = nextps()
                P.op("pe", lambda e, bv=bv: e.matmul(ps[bv][:, :], lhsT=cb("bones", 128), rhs=sqb[:, :], start=True, stop=True),
                     reads=["sqb", "cst"], writes=[("ps", bv)])
                P.op("act", lambda e, bv=bv: e.activation(out=T[5][:, :], in_=ps[bv][:, :], func=AF.Ln, bias=64e-5, scale=1.0 / 64),
                     reads=[("ps", bv), "T5"], writes=["T5"])
                P.op("act", lambda e: e.activation(out=T[5][:, :], in_=T[5][:, :], func=AF.Exp, scale=-0.5), reads=["T5"], writes=["T5"])
                P.op("dve", lambda e: e.tensor_tensor(out=ysb[:, :], in0=ysb[:, :], in1=T[5][:, :], op=ALU.mult), reads=["T3", "T5"], writes=["T3"])
                P.op("dve", lambda e: e.tensor_scalar(out=ysb[:, :], in0=ysb[:, :], scalar1=vcol("rwkv_ln_w", ch), scalar2=vcol("rwkv_ln_b", ch),
                                                      op0=ALU.mult, op1=ALU.add), reads=["T3", "vecs"], writes=["T3"])
                P.op("dve", lambda e: e.tensor_tensor(out=ysb[:, :], in0=ysb[:, :], in1=T[4][:, :], op=ALU.add), reads=["T3", "T4"], writes=["T3"])
                P.op("dve", lambda e: e.tensor_tensor(out=yr[:, ch, :], in0=ysb[:, :], in1=gbuf[:, :], op=ALU.mult), reads=["T3", "gbuf"], writes=[("yr", ch)])

        def mixer(tile_idx, STG=9, pre=False):
            rmsnorm("mix_norm")
            attention(tile_idx, pre)
            if STG < 3:
                return
            rwkv(tile_idx, pre)
            if STG < 4 or pre:
                return
            for m in range(16):
                hold = {}

                def ev_pa(idx, b, w):
                    hold["pa"] = b

                def ev_pr(idx, b, w):
                    hold["pr"] = b

                def ev_ga(idx, b, w):
                    hold["ga"] = b

                def ev_gr(idx, b, w):
                    hold["gr"] = b
                proj("w_pa", 8, lambda kc: qT[:, kc, :], [("qT", i) for i in range(8)], [(m * 128, 128)], ev_pa)
                proj("w_pr", 8, lambda kc: yr[:, kc, :], [("yr", i) for i in range(8)], [(m * 128, 128)], ev_pr)
                proj("w_in", NKC, lambda kc: hT[:, kc, :], ["hT"], [(4896 + m * 128, 128)], ev_ga)
                proj("w_in", NKC, lambda kc: hT[:, kc, :], ["hT"], [(6944 + m * 128, 128)], ev_gr)
                pa, pr, ga, gr = hold["pa"], hold["pr"], hold["ga"], hold["gr"]
                P.op("act", lambda e, ga=ga: e.activation(out=Tt[0][:, :], in_=ps[ga][:, :], func=AF.Sigmoid), reads=[("ps", ga)], writes=["T0"])
                P.op("act", lambda e, gr=gr: e.activation(out=Tt[1][:, :], in_=ps[gr][:, :], func=AF.Sigmoid), reads=[("ps", gr)], writes=["T1"])
                P.op("dve", lambda e, pa=pa: e.tensor_tensor(out=Tt[0][:, :], in0=ps[pa][:, :], in1=Tt[0][:, :], op=ALU.mult), reads=[("ps", pa), "T0"], writes=["T0"])
                P.op("dve", lambda e, pr=pr: e.tensor_tensor(out=Tt[1][:, :], in0=ps[pr][:, :], in1=Tt[1][:, :], op=ALU.mult), reads=[("ps", pr), "T1"], writes=["T1"])
                P.op("dve", lambda e, m=m: e.tensor_tensor(out=act[:, m, :], in0=Tt[0][:, :], in1=Tt[1][:, :], op=ALU.add), reads=["T0", "T1"], writes=[("act", m)])

            def ev_out(idx, b, w):
                P.op("dve", lambda e: e.tensor_tensor(out=xT[:, idx, :], in0=ps[b][:, :], in1=xT[:, idx, :], op=ALU.add),
                     reads=[("ps", b), "xT"], writes=["xT"])
            proj("w_out", NKC, lambda kc: act[:, kc, :], [("act", j) for j in range(16)], [(m * 128, 128) for m in range(16)], ev_out)

        def cross():
            rmsnorm("cross_norm")

            def ev_q(idx, b, w):
                P.op("act", lambda e: e.activation(out=qT[:, idx, :], in_=ps[b][:, :], func=AF.Copy), reads=[("ps", b)], writes=[("qT", idx)])
            proj("w_cq", NKC, lambda kc: hT[:, kc, :], ["hT"], [(h * 128, 128) for h in range(4)], ev_q)
            sc = float(128 ** -0.5)
            for h in range(4):
                Eb = ET[h % 2]
                ekey = "T%d" % (h % 2)
                b0, b1 = nextps(), nextps()

                def smm(e, h=h, b0=b0, b1=b1):
                    e.matmul(ps[b0][:, :], lhsT=kmT[:, h, 0:128], rhs=qT[:, h, :], start=True, stop=True)
                    return e.matmul(ps[b1][:, :], lhsT=kmT[:, h, 128:256], rhs=qT[:, h, :], start=True, stop=True)
                P.op("pe", smm, reads=["kmT", ("qT", h)], writes=[("ps", b0), ("ps", b1)])
                P.op("act", lambda e, Eb=Eb, b0=b0: e.activation(out=Eb[:, 0, :], in_=ps[b0][:, :], func=AF.Exp, scale=sc), reads=[("ps", b0)], writes=[ekey])
                P.op("act", lambda e, Eb=Eb, b1=b1: e.activation(out=Eb[:, 1, :], in_=ps[b1][:, :], func=AF.Exp, scale=sc), reads=[("ps", b1)], writes=[ekey])
                bo, bd = nextps(), nextps()

                def omm(e, h=h, Eb=Eb, bo=bo, bd=bd):
                    e.matmul(ps[bo][:, :], lhsT=vm[:, 0, h * 128:(h + 1) * 128], rhs=Eb[:, 0, :], start=True, stop=False)
                    e.matmul(ps[bo][:, :], lhsT=vm[:, 1, h * 128:(h + 1) * 128], rhs=Eb[:, 1, :], start=False, stop=True)
                    e.matmul(ps[bd][:, :], lhsT=cb("ones", 128), rhs=Eb[:, 0, :], start=True, stop=False)
                    return e.matmul(ps[bd][:, :], lhsT=cb("ones", 128), rhs=Eb[:, 1, :], start=False, stop=True)
                P.op("pe", omm, reads=[ekey, "vm", "cst"], writes=[("ps", bo), ("ps", bd)])
                P.op("dve", lambda e, bd=bd: e.reciprocal(out=den[:, :], in_=ps[bd][:, :]), reads=[("ps", bd)], writes=["T5"])
                P.op("dve", lambda e, bo=bo, h=h: e.tensor_tensor(out=oT[:, h, :], in0=ps[bo][:, :], in1=den[:, :], op=ALU.mult),
                     reads=[("ps", bo), "T5"], writes=[("yr", h)])

            def ev_o(idx, b, w):
                P.op("dve", lambda e: e.tensor_tensor(out=xT[:, idx, :], in0=ps[b][:, :], in1=xT[:, idx, :], op=ALU.add),
                     reads=[("ps", b), "xT"], writes=["xT"])
            proj("w_co", 4, lambda kc: oT[:, kc, :], [("yr", h) for h in range(4)], [(m * 128, 128) for m in range(16)], ev_o)

        def final_norm_store(t):
            P.op("act", lambda e: e.activation(out=hT[:, :, :], in_=xT[:, :, :], func=AF.Square), reads=["xT"], writes=["hT"])
            b = nextps()

            def mm(e):
                inst = None
                for kc in range(NKC):
                    inst = e.matmul(ps[b][:, :], lhsT=cb("ones", 128), rhs=hT[:, kc, :], start=(kc == 0), stop=(kc == NKC - 1))
                return inst
            P.op("pe", mm, reads=["hT", "cst"], writes=[("ps", b)])
            P.op("act", lambda e: e.activation(out=rstd[:, :], in_=ps[b][:, :], func=AF.Ln, bias=1e-6, scale=1.0 / D), reads=[("ps", b)], writes=["rstd"])
            P.op("act", lambda e: e.activation(out=rstd[:, :], in_=rstd[:, :], func=AF.Exp, scale=-0.5), reads=["rstd"], writes=["rstd"])

            def sc(e):
                inst = None
                for kc in range(NKC):
                    inst = e.scalar_tensor_tensor(out=xT[:, kc, :], in0=xT[:, kc, :], scalar=vcol("final_norm", kc), in1=rstd[:, :],
                                                  op0=ALU.mult, op1=ALU.mult)
                return inst
            P.op("dve", sc, reads=["xT", "rstd", "vecs"], writes=["xT"])
            P.op("pool", lambda e: e.dma_start(out=out_d[:, t * TT:(t + 1) * TT].rearrange("(k p) n -> p k n", p=128), in_=xT[:, :, :]),
                 reads=["xT"], writes=["outd"], dma="out")

        import os
        STG = int(os.environ.get("KSTAGE", "9"))
        mem_kv()
        P.barrier()
        for t in range(ntiles):
            P.epoch = 1 + t // 4
            P.op("pool", lambda e, t=t: e.dma_start(out=xT[:, :, :], in_=xT_d[:, t * TT:(t + 1) * TT].rearrange("(k p) n -> p k n", p=128)),
                 writes=["xT"], dma="xin")
            if STG >= 1:
                ffn("ffn1", "ffn1_norm")
            P.barrier()
            pre = t < npre
            if STG >= 2:
                mixer(t, STG, pre)
            P.barrier()
            if pre:
                continue
            if STG >= 5:
                cross()
            if STG >= 6:
                ffn("ffn2", "ffn2_norm")
            final_norm_store(t - npre)
            P.barrier()
        P.finalize(nc, stack, None, ["out"])
    return nc


_CACHE = {}


def kernel(**inp):
    import os
    x = np.asarray(inp["x"], np.float32)
    B, T, _ = x.shape
    ntiles = T // TT
    npre = ntiles // 2
    TH = npre * TT
    V = build_vecs(inp)
    vecs = V.build()
    consts2 = [build_consts(h) for h in (0, 1)]
    c_off = consts2[0][1]
    key = (ntiles, vecs.shape[1])
    if key not in _CACHE:
        _CACHE[key] = build_program(ntiles, V.off, vecs.shape[1], c_off, consts2[0][0].shape[1], npre=npre)
    nc = _CACHE[key]
    f = lambda a: np.ascontiguousarray(np.asarray(a, np.float32))
    w_in = f(inp["w_in"][0]).copy()
    w_in[:, 0:1024] = w_in[:, Q_PERM]
    shared = {
        "vecs": vecs,
        "ffn1_wg": f(inp["ffn1_w_gate"][0]), "ffn1_wu": f(inp["ffn1_w_up"][0]), "ffn1_wd": f(inp["ffn1_w_down"][0]),
        "w_in": w_in, "w_pa": f(np.asarray(inp["w_proj_attn"][0])[Q_PERM, :]), "w_pr": f(inp["w_proj_rwkv"][0]),
        "w_out": f(inp["w_out"][0]), "w_cq": f(inp["w_cross_q"][0]), "w_ckv": f(inp["w_cross_kv"][0]),
        "w_co": f(inp["w_cross_o"][0]),
        "ffn2_wg": f(inp["ffn2_w_gate"][0]), "ffn2_wu": f(inp["ffn2_w_up"][0]), "ffn2_wd": f(inp["ffn2_w_down"][0]),
        "w2": f(inp["rwkv_w2"][0]), "a2": f(inp["rwkv_a2"][0]), "g2": f(inp["rwkv_g2"][0]),
    }
    mem = np.asarray(inp["mem"], np.float32)
    ncores = int(os.environ.get("KCORES", "8"))
    in_maps = []
    for c in range(ncores):
        b, half = (c // 2) % B, c % 2
        m = dict(shared)
        xt = np.zeros((D, T), np.float32)
        if half == 0:
            xt[:, TH:] = x[b, 0:TH].T
        else:
            xt[:, :] = x[b].T
        m["xT"] = xt
        m["memT"] = np.ascontiguousarray(mem[b].T)
        m["consts"] = consts2[half][0]
        in_maps.append(m)
    if os.environ.get("KTRACE"):
        res = run_bass_kernel_spmd(nc, in_maps, core_ids=list(range(ncores)), trace=True)
        print("EXEC_TIME_NS", res.exec_time_ns)
    else:
        res = run_bass_kernel_spmd(nc, in_maps, core_ids=list(range(ncores)))
    out = np.zeros((B, T, D), np.float32)
    for c in range(ncores):
        b, half = (c // 2) % B, c % 2
        out[b, half * TH:(half + 1) * TH] = res.results[c]["outT"].T
    return out
```

```python
import contextlib
import types
import numpy as np
import concourse.bass as bass
import concourse.mybir as mybir
from concourse.bass_utils import run_bass_kernel_spmd

F32 = mybir.dt.float32
BF16 = mybir.dt.bfloat16
AF = mybir.ActivationFunctionType
ALU = mybir.AluOpType

D = 2048; DFF = 5632; TT = 512; NKC = 16
SEQ = 8192; BATCH = 4; MEM = 256
INC = 8992
CDEC = 0.6065306597126334
import os as _os
SAME_ENGINE_SYNC = _os.environ.get("KSES", "1") == "1"

ENGS = ("pe", "act", "dve", "pool", "sp")


class Prog:
    def __init__(self):
        self.ops = []
        self.eng_ops = {e: [] for e in ENGS}
        self.last_w = {}
        self.readers = {}
        self.epoch = 0

    def op(self, eng, fn, reads=(), writes=(), dma=None):
        if fn is not None and fn.__closure__:
            cells = tuple(types.CellType(c.cell_contents) for c in fn.__closure__)
            fn = types.FunctionType(fn.__code__, fn.__globals__, fn.__name__, fn.__defaults__, cells)
        oid = len(self.ops)
        deps = set()
        for k in reads:
            w = self.last_w.get(k)
            if w is not None:
                deps.add(w)
        for k in writes:
            w = self.last_w.get(k)
            if w is not None:
                deps.add(w)
            for r in self.readers.get(k, ()):
                deps.add(r)
        for k in writes:
            self.last_w[k] = oid
            self.readers[k] = []
        ws = set(writes)
        for k in reads:
            if k not in ws:
                self.readers.setdefault(k, []).append(oid)
        deps.discard(oid)
        o = dict(id=oid, eng=eng, fn=fn, deps=deps, dma=dma, epoch=self.epoch, signal=(dma is not None))
        self.ops.append(o)
        self.eng_ops[eng].append(o)
        return oid

    def barrier(self, engs=("pe", "act", "dve", "pool")):
        lasts = {}
        for e in engs:
            for o in reversed(self.eng_ops[e]):
                if o["fn"] is not None and o["dma"] is None:
                    lasts[e] = o["id"]
                    break
        for e in engs:
            d = set(v for k, v in lasts.items() if k != e)
            o = dict(id=len(self.ops), eng=e, fn=None, deps=d, dma=None, epoch=self.epoch, signal=False)
            self.ops.append(o)
            self.eng_ops[e].append(o)

    def finalize(self, nc, stack, handles, final_dma_chans):
        ops = self.ops
        for o in ops:
            for d in o["deps"]:
                od = ops[d]
                if od["dma"] is not None:
                    continue
                if od["eng"] != o["eng"]:
                    od["signal"] = True
                elif SAME_ENGINE_SYNC and o["eng"] in ("act", "dve", "pool"):
                    od["signal"] = True
        sems = {}
        counts = {}

        def getsem(key):
            if key not in sems:
                sems[key] = stack.enter_context(nc.semaphore("s_%s_%s" % key))
                counts[key] = 0
            return sems[key]

        for o in ops:
            if not o["signal"]:
                continue
            if o["dma"] is not None:
                key = ("d" + str(o["dma"]), o["epoch"])
                getsem(key)
                counts[key] += 16
                o["sig"] = (key, counts[key], 16)
            else:
                key = (o["eng"], o["epoch"])
                getsem(key)
                counts[key] += 1
                o["sig"] = (key, counts[key], 1)
        self.nsems = len(sems)
        final_waits = []
        for ch in final_dma_chans:
            for key in sems:
                if key[0] == "d" + str(ch):
                    final_waits.append((key, counts[key]))

        def emit(engname, e):
            waited = {}
            for o in self.eng_ops[engname]:
                need = {}
                for d in o["deps"]:
                    od = ops[d]
                    if "sig" not in od:
                        continue
                    if od["dma"] is None and od["eng"] == engname and not (
                            SAME_ENGINE_SYNC and engname in ("act", "dve", "pool")):
                        continue
                    key, val, _ = od["sig"]
                    if val > need.get(key, 0):
                        need[key] = val
                for key, val in need.items():
                    if waited.get(key, 0) >= val:
                        continue
                    e.wait_ge(sems[key], val)
                    waited[key] = val
                if o["fn"] is None:
                    continue
                inst = o["fn"](e)
                if o["signal"]:
                    key, val, inc = o["sig"]
                    inst.then_inc(sems[key], inc)
            if engname == "sp":
                for key, val in final_waits:
                    e.wait_ge(sems[key], val)

        with nc.Block() as block:
            @block.tensor
            def _(e):
                emit("pe", e)

            @block.scalar
            def _(e):
                emit("act", e)

            @block.vector
            def _(e):
                emit("dve", e)

            @block.gpsimd
            def _(e):
                emit("pool", e)

            @block.sync
            def _(e):
                emit("sp", e)


Q_HEAD_ORDER = []
for _c in range(4):
    Q_HEAD_ORDER += [_c, 4 + _c]
for _c in range(4):
    Q_HEAD_ORDER += [8 + _c, 12 + _c]
Q_PERM = np.concatenate([np.arange(h * 64, (h + 1) * 64) for h in Q_HEAD_ORDER])


def _cols(v):
    v = np.asarray(v, np.float32).reshape(-1)
    n = (len(v) + 127) // 128 * 128
    p = np.zeros(n, np.float32)
    p[:len(v)] = v
    return p.reshape(-1, 128).T


class Vecs:
    def __init__(self):
        self.parts = []
        self.off = {}
        self.n = 0

    def add(self, name, arr):
        arr = np.asarray(arr, np.float32)
        assert arr.shape[0] == 128
        self.off[name] = self.n
        self.parts.append(arr)
        self.n += arr.shape[1]

    def build(self):
        return np.ascontiguousarray(np.concatenate(self.parts, axis=1))


def build_vecs(inp):
    V = Vecs()
    for nm in ("ffn1_norm", "mix_norm", "cross_norm", "ffn2_norm", "final_norm", "mem_norm"):
        V.add(nm, _cols(inp[nm]))
    mu = np.asarray(inp["rwkv_mu"], np.float32).reshape(-1)
    V.add("mu_rkv", _cols(mu[0:3072]))
    V.add("mu_xw", _cols(mu[3072:3136]))
    V.add("mu_xa", _cols(mu[3136:3200]))
    V.add("mu_xg0", _cols(mu[3200:3328]))
    V.add("mu_xg1", _cols(mu[3328:3360]))
    for nm in ("rwkv_w0", "rwkv_a0", "rwkv_k_k", "rwkv_k_a", "rwkv_r_k", "rwkv_ln_w", "rwkv_ln_b"):
        V.add(nm, _cols(inp[nm]))
    sk = np.asarray(inp["attn_sinks"], np.float32).reshape(-1)
    V.add("sinks", np.repeat(sk[None, :], 128, axis=0))
    return V


def build_consts(half=1):
    c = {}
    c["ident"] = np.eye(128, dtype=np.float32)
    c["ones"] = np.ones((128, 128), np.float32)
    bo = np.zeros((128, 128), np.float32)
    bo[:64, :64] = 1.0
    bo[64:, 64:] = 1.0
    c["bones"] = bo
    ki = np.arange(128)[:, None]
    qi = np.arange(128)[None, :]
    c["mcur"] = np.tile((qi >= ki).astype(np.float32), (1, 4))
    c["mprev"] = np.tile((qi < ki).astype(np.float32), (1, 4))
    c["flag"] = np.full((128, 1), float(half), np.float32)
    id64 = np.zeros((128, 512), np.float32)
    id64[:64] = np.tile(np.eye(64, dtype=np.float32), (1, 8))
    c["id64"] = id64
    s = np.arange(64)[:, None]
    t = np.arange(64)[None, :]
    strict = (t > s).astype(np.float32)
    incl = (t >= s).astype(np.float32)
    m = np.block([[strict, incl], [strict, incl]])
    c["mlt"] = np.tile(m, (1, 4))
    mlow = np.zeros((128, 512), np.float32)
    mlow[:64] = np.tile((s > t).astype(np.float32), (1, 8))
    c["mlow"] = mlow
    order = ["ident", "ones", "bones", "mcur", "mprev", "id64", "mlt", "mlow", "flag"]
    off = {}
    o = 0
    for k in order:
        off[k] = o
        o += c[k].shape[1]
    allc = np.concatenate([c[k] for k in order], axis=1)
    return np.ascontiguousarray(allc), off


NCB = 128 * 3 + 512 * 3


WNAMES = [
    ("ffn1_wg", D, DFF), ("ffn1_wu", D, DFF), ("ffn1_wd", DFF, D),
    ("w_in", D, INC), ("w_pa", 1024, D), ("w_pr", 1024, D), ("w_out", D, D),
    ("w_cq", D, 512), ("w_ckv", D, 1024), ("w_co", 512, D),
    ("ffn2_wg", D, DFF), ("ffn2_wu", D, DFF), ("ffn2_wd", DFF, D),
    ("w2", 64, 1024), ("a2", 64, 1024), ("g2", 160, 1024),
]


def build_program(ntiles, vec_off, nvec, c_off, ncst, debug=None, npre=0):
    nc = bass.Bass("TRN2", target_bir_lowering=False)
    ntok = ntiles * TT
    P = Prog()
    stack = contextlib.ExitStack()
    with stack:
        xT_d = nc.dram_tensor("xT", [D, ntok], F32, kind="ExternalInput").ap()
        memT_d = nc.dram_tensor("memT", [D, MEM], F32, kind="ExternalInput").ap()
        vecs_d = nc.dram_tensor("vecs", [128, nvec], F32, kind="ExternalInput").ap()
        cst_d = nc.dram_tensor("consts", [128, ncst], F32, kind="ExternalInput").ap()
        out_d = nc.dram_tensor("outT", [D, (ntiles - npre) * TT], F32, kind="ExternalOutput").ap()
        wf = {}
        wb = {}
        for nm, k, n in WNAMES:
            wf[nm] = nc.dram_tensor(nm, [k, n], F32, kind="ExternalInput").ap()
            wb[nm] = nc.dram_tensor(nm + "_bf", [k, n], BF16, kind="Internal").ap()
        dbg_d = {}
        if debug:
            for nm, shp in debug.items():
                dbg_d[nm] = nc.dram_tensor("dbg_" + nm, list(shp), F32, kind="ExternalOutput").ap()

        def sb(name, shape, dt):
            return stack.enter_context(nc.sbuf_tensor(name, list(shape), dt))

        xT = sb("xT_s", [128, NKC, TT], F32)
        hT = sb("hT_s", [128, NKC, TT], BF16)
        NWB = 3
        wbuf = [sb("wb%d" % i, [128, 8, 512], BF16) for i in range(NWB)]
        vecs = sb("vecs_s", [128, nvec], F32)
        cstf = sb("cstf", [128, ncst - NCB], F32)
        cstb = sb("cstb", [128, NCB], BF16)
        omka = sb("omka", [128, 8], F32)
        esink = sb("esink", [128, 16], F32)
        w2b = sb("w2b", [64, 1024], BF16)
        a2b = sb("a2b", [64, 1024], BF16)
        g2b = sb("g2b", [128, 2, 1024], BF16)
        kmT = sb("kmT", [128, 4, MEM], BF16)
        vm = sb("vm", [128, 2, 512], BF16)
        rstd = sb("rstd", [128, TT], F32)
        carry = sb("carry", [128, 32], F32)
        S0T = sb("S0T", [64, 8, 2, 64], F32)
        S0Tb = sb("S0Tb", [64, 8, 2, 64], BF16)
        ARB = sb("ARB", [64, 8, 2, 64], BF16)
        BKB = sb("BKB", [64, 8, 2, 64], BF16)
        WtotB = sb("WtotB", [64, 8], F32)
        Whl = sb("Whl", [128, 2, 8], BF16)
        kTh = sb("kTh", [128, 2, 128 + TT], BF16)
        vtok = sb("vtok", [128, 5, 256], BF16)
        NACT = 22
        act = sb("act", [128, NACT, TT], BF16)
        _al = act[:, 16:22, :].rearrange("p a b -> p (a b)")
        _alf = _al[:, 0:2056].bitcast(F32)
        Pst = [_alf[:, 0:TT + 1], _alf[:, TT + 1:2 * TT + 2]]
        gbuf = _al[:, 2056:2056 + 512]
        Bt = [sb("B0", [128, TT], BF16), sb("B1", [128, TT], BF16)]
        qT = sb("qT", [128, 8, TT], BF16)
        yr = sb("yr", [128, 8, TT], BF16)
        oT = yr
        xs_small = sb("xs_small", [128, 4, TT], BF16)
        xr = sb("xr", [128, TT], F32)
        xk = sb("xk", [128, TT], F32)
        xv = sb("xv", [128, TT], F32)
        Tt = [sb("T%d" % i, [128, TT], F32) for i in range(7)]
        den = Tt[5]
        ET = [Tt[i][:, :].bitcast(BF16).rearrange("p (a b) -> p a b", a=2) for i in range(2)]
        ysb = Tt[3]
        sqb = Bt[1]
        AR = sb("AR", [128, 8, 2, 64], BF16)
        BK = sb("BK", [128, 8, 2, 64], BF16)
        TM = sb("TM", [128, 8, 3, 64], BF16)
        TMt = sb("TMt", [64, 8, 3, 128], BF16)
        LT = sb("LT", [64, 16, 128], BF16)
        LTk = sb("LTk", [64, 16, 128], BF16)
        Wtot = sb("Wtot", [128, 8], F32)
        inv = {nm: sb("inv_" + nm, [64, 16, 64], BF16) for nm in ("L0", "N0", "L1", "N1", "Q", "P")}
        Zs = sb("Zs", [64, 2, 64], BF16)
        UV = sb("UV", [64, 2, 64], BF16)

        print("SBUF bytes remaining per partition:", nc.sbuf_bytes_remaining)
        ps = [stack.enter_context(nc.psum_tensor("ps%d" % i, [128, 512], F32)) for i in range(8)]
        psctr = [0]
        YB = 7

        def nextps():
            i = psctr[0] % 7
            psctr[0] += 1
            return i

        def vcol(name, j=0, rows=128):
            o = vec_off[name] + j
            return vecs[0:rows, o:o + 1]

        def cb(name, w, rows=128, c0=0):
            o = c_off[name] + c0
            return cstb[0:rows, o:o + w]

        def cf(name, w, rows=128, c0=0):
            o = c_off[name] + c0 - NCB
            return cstf[0:rows, o:o + w]

        for nm, k, n in WNAMES:
            step = 512
            for r0 in range(0, k, step):
                r1 = min(k, r0 + step)
                P.op("pool", lambda e, nm=nm, r0=r0, r1=r1: e.dma_start(out=wb[nm][r0:r1, :], in_=wf[nm][r0:r1, :]),
                     writes=[("wbf", nm)], dma="wc_" + nm)
        P.op("sp", lambda e: e.dma_start(out=vecs[:], in_=vecs_d[:, :]), writes=["vecs"], dma="misc1")
        xflat = xT[:, :, :].rearrange("p a b -> p (a b)")
        P.op("sp", lambda e: e.dma_start(out=xflat[:, 0:ncst], in_=cst_d[:, :]), writes=["xT"], dma="misc2")
        P.op("dve", lambda e: e.tensor_copy(out=cstb[:], in_=xflat[:, 0:NCB]), reads=["xT"], writes=["cst"])
        P.op("dve", lambda e: e.tensor_copy(out=cstf[:], in_=xflat[:, NCB:ncst]), reads=["xT"], writes=["cstf"])
        P.op("dve", lambda e: e.tensor_scalar(out=omka[:], in0=vecs[:, vec_off["rwkv_k_a"]:vec_off["rwkv_k_a"] + 8],
                                              scalar1=-1.0, scalar2=1.0, op0=ALU.mult, op1=ALU.add),
             reads=["vecs"], writes=["omka"])
        P.op("act", lambda e: e.activation(out=esink[:], in_=vecs[:, vec_off["sinks"]:vec_off["sinks"] + 16], func=AF.Exp),
             reads=["vecs"], writes=["esink"])
        P.op("sp", lambda e: e.dma_start(out=w2b[:], in_=wb["w2"][:, :]), reads=[("wbf", "w2")], writes=["w2b"], dma="misc3")
        P.op("sp", lambda e: e.dma_start(out=a2b[:], in_=wb["a2"][:, :]), reads=[("wbf", "a2")], writes=["a2b"], dma="misc4")
        P.op("sp", lambda e: e.dma_start(out=g2b[:, 0, :], in_=wb["g2"][0:128, :]), reads=[("wbf", "g2")], writes=["g2b"], dma="misc5")
        P.op("sp", lambda e: e.dma_start(out=g2b[0:32, 1, :], in_=wb["g2"][128:160, :]), reads=[("wbf", "g2")], writes=["g2b1"], dma="misc6")
        for t_ in (carry, S0T, S0Tb, kTh, vtok, AR):
            P.op("pool", lambda e, t_=t_: e.memset(t_[:], 0.0), writes=["init_" + t_.name])
        INITK = ["init_" + t_.name for t_ in (carry, S0T, S0Tb, kTh, vtok, AR)]

        wslot = [0]

        def load_panel(wname, kc0, nk, c0, ncol):
            s = wslot[0] % NWB
            wslot[0] += 1
            src = wb[wname][kc0 * 128:(kc0 + nk) * 128, c0:c0 + ncol].rearrange("(k p) n -> p k n", p=128)
            P.op("sp", lambda e, s=s, src=src, nk=nk, ncol=ncol: e.dma_start(out=wbuf[s][:, 0:nk, 0:ncol], in_=src),
                 reads=[("wbf", wname)], writes=[("wb", s)], dma="w%d" % s)
            return s

        def proj(wname, nkc_total, rhs_fn, rhs_keys, chunks, evac, pan_cols=512, ncols=TT, kc_base=0):
            groups = []
            cur = []
            for idx, (c0, w) in enumerate(chunks):
                if cur and (len(cur) == 4 or c0 + w - chunks[cur[0]][0] > pan_cols):
                    groups.append(cur)
                    cur = []
                cur.append(idx)
            if cur:
                groups.append(cur)
            for g in groups:
                gc0 = chunks[g[0]][0]
                gc1 = chunks[g[-1]][0] + chunks[g[-1]][1]
                banks = [nextps() for _ in g]
                for kp0 in range(0, nkc_total, 8):
                    nk = min(8, nkc_total - kp0)
                    s = load_panel(wname, kc_base + kp0, nk, gc0, gc1 - gc0)

                    def mmfn(e, s=s, kp0=kp0, nk=nk, g=g, banks=banks, gc0=gc0):
                        inst = None
                        for kk in range(nk):
                            kc = kp0 + kk
                            for idx, b in zip(g, banks):
                                c0, w = chunks[idx]
                                inst = e.matmul(ps[b][0:w, 0:ncols], lhsT=wbuf[s][:, kk, c0 - gc0:c0 - gc0 + w], rhs=rhs_fn(kc),
                                                start=(kc == 0), stop=(kc == nkc_total - 1))
                        return inst
                    P.op("pe", mmfn, reads=[("wb", s)] + list(rhs_keys), writes=[("ps", b) for b in banks])
                for idx, b in zip(g, banks):
                    evac(idx, b, chunks[idx][1])

        def rmsnorm(gname, src=None, src_key="xT", nrows=TT, dst=None, dst_key="hT"):
            src = xT if src is None else src
            dst = hT if dst is None else dst
            n = nrows
            P.op("act", lambda e: e.activation(out=dst[:, :, 0:n], in_=src[:, :, 0:n], func=AF.Square),
                 reads=[src_key], writes=[dst_key])
            b = nextps()

            def mm(e):
                inst = None
                for kc in range(NKC):
                    inst = e.matmul(ps[b][:, 0:n], lhsT=cb("ones", 128), rhs=dst[:, kc, 0:n], start=(kc == 0), stop=(kc == NKC - 1))
                return inst
            P.op("pe", mm, reads=[dst_key, "cst"], writes=[("ps", b)])
            P.op("act", lambda e: e.activation(out=rstd[:, 0:n], in_=ps[b][:, 0:n], func=AF.Ln, bias=1e-6, scale=1.0 / D),
                 reads=[("ps", b)], writes=["rstd"])
            P.op("act", lambda e: e.activation(out=rstd[:, 0:n], in_=rstd[:, 0:n], func=AF.Exp, scale=-0.5),
                 reads=["rstd"], writes=["rstd"])

            def sc(e):
                inst = None
                for kc in range(NKC):
                    inst = e.scalar_tensor_tensor(out=dst[:, kc, 0:n], in0=src[:, kc, 0:n], scalar=vcol(gname, kc), in1=rstd[:, 0:n],
                                                  op0=ALU.mult, op1=ALU.mult)
                return inst
            P.op("dve", sc, reads=[src_key, "rstd", "vecs"], writes=[dst_key])

        def ffn(pref, gname):
            rmsnorm(gname)
            hr = lambda kc: hT[:, kc, :]

            def ev_gate(idx, b, w):
                P.op("act", lambda e: e.activation(out=act[:, idx, :], in_=ps[b][:, :], func=AF.Silu),
                     reads=[("ps", b)], writes=[("act", idx)])

            def ev_up(idx, b, w):
                P.op("dve", lambda e: e.tensor_tensor(out=act[:, idx, :], in0=ps[b][:, :], in1=act[:, idx, :], op=ALU.mult),
                     reads=[("ps", b), ("act", idx)], writes=[("act", idx)])

            def ev_down(idx, b, w):
                P.op("dve", lambda e: e.scalar_tensor_tensor(out=xT[:, idx, :], in0=ps[b][:, :], scalar=0.5, in1=xT[:, idx, :],
                                                             op0=ALU.mult, op1=ALU.add),
                     reads=[("ps", b), "xT"], writes=["xT"])
            for hf in range(2):
                for g0 in range(0, NACT, 4):
                    n = min(4, NACT - g0)
                    sub = [((hf * NACT + g0 + i) * 128, 128) for i in range(n)]
                    proj(pref + "_wg", NKC, hr, ["hT"], sub, lambda i, b, w, g0=g0: ev_gate(g0 + i, b, w))
                    proj(pref + "_wu", NKC, hr, ["hT"], sub, lambda i, b, w, g0=g0: ev_up(g0 + i, b, w))
                proj(pref + "_wd", NACT, lambda kc: act[:, kc, :], [("act", j) for j in range(NACT)],
                     [(m * 128, 128) for m in range(16)], ev_down, kc_base=hf * NACT)

        def mem_kv():
            P.op("sp", lambda e: e.dma_start(out=xT[:, :, 0:MEM], in_=memT_d.rearrange("(k p) m -> p k m", p=128)),
                 writes=["xT"], dma="xin")
            rmsnorm("mem_norm", nrows=MEM)
            def ev_k(idx, b, w):
                P.op("act", lambda e: e.activation(out=kmT[:, idx, :], in_=ps[b][:, 0:MEM], func=AF.Copy),
                     reads=[("ps", b)], writes=["kmT"])
            proj("w_ckv", NKC, lambda kc: hT[:, kc, 0:MEM], ["hT"], [(h * 128, 128) for h in range(4)], ev_k, ncols=MEM)
            for mb in range(2):
                b = nextps()
                for kp0 in (0, 8):
                    s = load_panel("w_ckv", kp0, 8, 512, 512)

                    def mm(e, s=s, kp0=kp0, b=b, mb=mb):
                        inst = None
                        for kk in range(8):
                            kc = kp0 + kk
                            inst = e.matmul(ps[b][:, :], lhsT=hT[:, kc, mb * 128:(mb + 1) * 128], rhs=wbuf[s][:, kk, :],
                                            start=(kc == 0), stop=(kc == 15))
                        return inst
                    P.op("pe", mm, reads=[("wb", s), "hT"], writes=[("ps", b)])
                P.op("act", lambda e, b=b, mb=mb: e.activation(out=vm[:, mb, :], in_=ps[b][:, :], func=AF.Copy),
                     reads=[("ps", b)], writes=["vm"])

        RW0 = 1536

        def token_shift_evac(b, width, carry_col, mu_ap, dst_ap, dst_key, func=None, pidx=[0]):
            i = pidx[0] % 2
            pidx[0] += 1
            Pb = Pst[i]
            w = width
            P.op("pool", lambda e: e.tensor_copy(out=Pb[0:w, 0:1], in_=carry[0:w, carry_col:carry_col + 1]),
                 reads=["carry"] + INITK, writes=[("Pst", i)])
            P.op("act", lambda e: e.activation(out=Pb[0:w, 1:TT + 1], in_=ps[b][0:w, :], func=AF.Copy),
                 reads=[("ps", b)], writes=[("Pst", i)])
            P.op("pool", lambda e: e.tensor_copy(out=carry[0:w, carry_col:carry_col + 1], in_=Pb[0:w, TT:TT + 1]),
                 reads=[("Pst", i)], writes=["carry"])
            P.op("dve", lambda e: e.tensor_tensor(out=Tt[6][0:w, :], in0=Pb[0:w, 0:TT], in1=Pb[0:w, 1:TT + 1], op=ALU.subtract),
                 reads=[("Pst", i)], writes=["T6"])
            if func is None:
                P.op("dve", lambda e: e.scalar_tensor_tensor(out=dst_ap, in0=Tt[6][0:w, :], scalar=mu_ap, in1=Pb[0:w, 1:TT + 1],
                                                             op0=ALU.mult, op1=ALU.add),
                     reads=["T6", ("Pst", i), "vecs"], writes=[dst_key])
            else:
                P.op("dve", lambda e: e.scalar_tensor_tensor(out=Tt[6][0:w, :], in0=Tt[6][0:w, :], scalar=mu_ap, in1=Pb[0:w, 1:TT + 1],
                                                             op0=ALU.mult, op1=ALU.add),
                     reads=["T6", ("Pst", i), "vecs"], writes=["T6"])
                P.op("act", lambda e: e.activation(out=dst_ap, in_=Tt[6][0:w, :], func=func),
                     reads=["T6"], writes=[dst_key])

        def attention(tile_idx, pre=False):
            hrhs = lambda kc: hT[:, kc, :]
            if pre and tile_idx != npre - 1:
                return
            def ev_q(idx, b, w):
                P.op("act", lambda e: e.activation(out=qT[:, idx, :], in_=ps[b][:, :], func=AF.Copy),
                     reads=[("ps", b)], writes=[("qT", idx)])
            if not pre:
                proj("w_in", NKC, hrhs, ["hT"], [(c * 128, 128) for c in range(8)], ev_q)

            def ev_k(idx, b, w):
                P.op("act", lambda e: e.activation(out=kTh[:, idx, 128:128 + TT], in_=ps[b][:, :], func=AF.Copy),
                     reads=[("ps", b)] + INITK, writes=["kTh"])
            proj("w_in", NKC, hrhs, ["hT"], [(1024 + c * 128, 128) for c in range(2)], ev_k)
            for tb in range(4):
                b = nextps()
                for kp0 in (0, 8):
                    s = load_panel("w_in", kp0, 8, 1280, 256)

                    def mm(e, s=s, kp0=kp0, b=b, tb=tb):
                        inst = None
                        for kk in range(8):
                            kc = kp0 + kk
                            inst = e.matmul(ps[b][:, 0:256], lhsT=hT[:, kc, tb * 128:(tb + 1) * 128], rhs=wbuf[s][:, kk, 0:256],
                                            start=(kc == 0), stop=(kc == 15))
                        return inst
                    P.op("pe", mm, reads=[("wb", s), "hT"], writes=[("ps", b)])
                P.op("act", lambda e, b=b, tb=tb: e.activation(out=vtok[:, tb + 1, :], in_=ps[b][:, 0:256], func=AF.Copy),
                     reads=[("ps", b)] + INITK, writes=["vtok"])
            its = []
            for g in range(4 if not pre else 0):
                for n in range(4):
                    its.append(dict(g=g, n=n, kch=g // 2, kbase=(g % 2) * 64, qc0=(0 if g < 2 else 4),
                                    first=(npre == 0 and tile_idx == 0 and n == 0),
                                    firstown=(npre > 0 and tile_idx == npre and n == 0),
                                    ei=len(its) % 2))

            def stageA(it):
                g, n, kch, kbase, qc0, first, firstown = it["g"], it["n"], it["kch"], it["kbase"], it["qc0"], it["first"], it["firstown"]
                Eb = ET[it["ei"]]
                ekey = "T%d" % it["ei"]
                bprev, bcur = nextps(), nextps()
                qrhs = qT[kbase:kbase + 64, qc0:qc0 + 4, n * 128:(n + 1) * 128]

                def mm(e):
                    inst = None
                    if not first:
                        inst = e.matmul(ps[bprev][:, :], lhsT=kTh[kbase:kbase + 64, kch, n * 128:(n + 1) * 128], rhs=qrhs,
                                        start=True, stop=True)
                    inst = e.matmul(ps[bcur][:, :], lhsT=kTh[kbase:kbase + 64, kch, (n + 1) * 128:(n + 2) * 128], rhs=qrhs,
                                    start=True, stop=True)
                    return inst
                P.op("pe", mm, reads=["kTh"] + [("qT", qc0 + i) for i in range(4)], writes=[("ps", bprev), ("ps", bcur)])
                if not first:
                    P.op("act", lambda e: e.activation(out=Eb[:, 0, :], in_=ps[bprev][:, :], func=AF.Exp, scale=0.125),
                         reads=[("ps", bprev)], writes=[ekey])
                P.op("act", lambda e: e.activation(out=Eb[:, 1, :], in_=ps[bcur][:, :], func=AF.Exp, scale=0.125),
                     reads=[("ps", bcur)], writes=[ekey])
                if not first:
                    if firstown:
                        P.op("dve", lambda e: e.scalar_tensor_tensor(out=Eb[:, 0, :], in0=Eb[:, 0, :], scalar=cf("flag", 1), in1=cb("mprev", 512),
                                                                     op0=ALU.mult, op1=ALU.mult),
                             reads=[ekey, "cst", "cstf"], writes=[ekey])
                    else:
                        P.op("pool", lambda e: e.tensor_tensor(out=Eb[:, 0, :], in0=Eb[:, 0, :], in1=cb("mprev", 512), op=ALU.mult),
                             reads=[ekey, "cst"], writes=[ekey])
                P.op("pool", lambda e: e.tensor_tensor(out=Eb[:, 1, :], in0=Eb[:, 1, :], in1=cb("mcur", 512), op=ALU.mult),
                     reads=[ekey, "cst"], writes=[ekey])

            def stageB(it):
                g, n, kbase, qc0, first = it["g"], it["n"], it["kbase"], it["qc0"], it["first"]
                Eb = ET[it["ei"]]
                ekey = "T%d" % it["ei"]
                bo, bd = nextps(), nextps()

                def mm2(e):
                    inst = None
                    for hh in range(4):
                        osl = ps[bo][kbase:kbase + 64, hh * 128:(hh + 1) * 128]
                        dsl = ps[bd][kbase:kbase + 64, hh * 128:(hh + 1) * 128]
                        if not first:
                            e.matmul(osl, lhsT=vtok[:, n, g * 64:(g + 1) * 64], rhs=Eb[:, 0, hh * 128:(hh + 1) * 128], start=True, stop=False)
                        e.matmul(osl, lhsT=vtok[:, n + 1, g * 64:(g + 1) * 64], rhs=Eb[:, 1, hh * 128:(hh + 1) * 128], start=first, stop=True)
                        if not first:
                            e.matmul(dsl, lhsT=cb("ones", 64), rhs=Eb[:, 0, hh * 128:(hh + 1) * 128], start=True, stop=False)
                        inst = e.matmul(dsl, lhsT=cb("ones", 64), rhs=Eb[:, 1, hh * 128:(hh + 1) * 128], start=first, stop=True)
                    return inst
                P.op("pe", mm2, reads=[ekey, "vtok", "cst"], writes=[("ps", bo), ("ps", bd)])

                def dn(e):
                    inst = None
                    for hh in range(4):
                        h = 4 * g + hh
                        inst = e.tensor_scalar(out=den[kbase:kbase + 64, hh * 128:(hh + 1) * 128],
                                               in0=ps[bd][kbase:kbase + 64, hh * 128:(hh + 1) * 128],
                                               scalar1=esink[kbase:kbase + 64, h:h + 1], scalar2=None, op0=ALU.add)
                    return inst
                P.op("dve", dn, reads=[("ps", bd), "esink"], writes=["T5"])
                P.op("dve", lambda e: e.reciprocal(out=den[kbase:kbase + 64, :], in_=den[kbase:kbase + 64, :]),
                     reads=["T5"], writes=["T5"])

                def yo(e):
                    return e.tensor_tensor(out=qT[kbase:kbase + 64, qc0:qc0 + 4, n * 128:(n + 1) * 128],
                                           in0=ps[bo][kbase:kbase + 64, :].rearrange("p (h q) -> p h q", h=4),
                                           in1=den[kbase:kbase + 64, :].rearrange("p (h q) -> p h q", h=4), op=ALU.mult)
                P.op("dve", yo, reads=[("ps", bo), "T5"], writes=[("qT", qc0 + i) for i in range(4)])
            if its:
                stageA(its[0])
                for i in range(len(its)):
                    if i + 1 < len(its):
                        stageA(its[i + 1])
                    stageB(its[i])
            P.op("pool", lambda e: e.tensor_copy(out=kTh[:, :, 0:128], in_=kTh[:, :, TT:TT + 128]), reads=["kTh"], writes=["kTh"])
            P.op("pool", lambda e: e.tensor_copy(out=vtok[:, 0, :], in_=vtok[:, 4, :]), reads=["vtok"], writes=["vtok"])

        def rwkv(tile_idx, pre=False):
            hrhs = lambda kc: hT[:, kc, :]
            small = [(RW0 + 3072, 64, "mu_xw", AF.Tanh, 0, 24), (RW0 + 3136, 64, "mu_xa", AF.Copy, 1, 25),
                     (RW0 + 3200, 128, "mu_xg0", AF.Sigmoid, 2, 26), (RW0 + 3328, 32, "mu_xg1", AF.Sigmoid, 3, 27)]

            def ev_small(idx, b, w):
                c0, ww, mun, fn, slot, ccol = small[idx]
                token_shift_evac(b, ww, ccol, vcol(mun, 0, ww), xs_small[0:ww, slot, :], ("xs_small", slot), func=fn)
            proj("w_in", NKC, hrhs, ["hT"], [(s_[0], s_[1]) for s_ in small], ev_small)

            KSUB = int(os.environ.get("KSUB", "9"))
            if KSUB < 2:
                return
            for ch in range(8):
                dsts = [(xr, "xr"), (xk, "xk"), (xv, "xv")]

                def ev_rkv(idx, b, w, ch=ch):
                    token_shift_evac(b, 128, idx * 8 + ch, vcol("mu_rkv", idx * 8 + ch), dsts[idx][0][:, :], dsts[idx][1])
                skip_r = pre and tile_idx != npre - 1
                for idx in range(3):
                    if idx == 0 and skip_r:
                        continue
                    proj("w_in", NKC, hrhs, ["hT"], [(RW0 + idx * 1024 + ch * 128, 128)],
                         lambda i, b, w, idx=idx: ev_rkv(idx, b, w))
                T = Tt
                csl = slice(ch * 128, (ch + 1) * 128)
                b1 = nextps()
                P.op("pe", lambda e, b1=b1: e.matmul(ps[b1][:, :], lhsT=w2b[:, csl], rhs=xs_small[0:64, 0, :], start=True, stop=True),
                     reads=["w2b", ("xs_small", 0)], writes=[("ps", b1)])
                P.op("act", lambda e, b1=b1: e.activation(out=T[0][:, :], in_=ps[b1][:, :], func=AF.Sigmoid, bias=vcol("rwkv_w0", ch)),
                     reads=[("ps", b1), "vecs"], writes=["T0"])
                P.op("dve", lambda e: e.tensor_tensor_scan(out=T[1][:, :], data0=T[0][:, :], data1=T[0][:, :], initial=0.0,
                                                           op0=ALU.add, op1=ALU.bypass),
                     reads=["T0"], writes=["T1"])
                P.op("dve", lambda e: e.tensor_copy(out=T[2][:, 0:64], in_=T[1][:, 0:64]), reads=["T1"], writes=["T2"])

                def lwf(e):
                    base = T[1][:, 63:63 + 448].rearrange("p (c s) -> p c s", s=64)[:, :, 0:1].to_broadcast([128, 7, 64])
                    return e.tensor_tensor(out=T[2][:, 64:512].rearrange("p (c s) -> p c s", s=64),
                                           in0=T[1][:, 64:512].rearrange("p (c s) -> p c s", s=64), in1=base, op=ALU.subtract)
                P.op("dve", lwf, reads=["T1", "T2"], writes=["T2"])
                P.op("dve", lambda e: e.tensor_tensor(out=T[1][:, :], in0=T[2][:, :], in1=T[0][:, :], op=ALU.subtract),
                     reads=["T2", "T0", "T1"], writes=["T1"])

                def lrem(e):
                    last = T[2][:, :].rearrange("p (c s) -> p c s", s=64)[:, :, 63:64].to_broadcast([128, 8, 64])
                    return e.tensor_tensor(out=T[0][:, :].rearrange("p (c s) -> p c s", s=64), in0=last,
                                           in1=T[2][:, :].rearrange("p (c s) -> p c s", s=64), op=ALU.subtract)
                P.op("dve", lrem, reads=["T2", "T1", "T0"], writes=["T0"])
                P.op("act", lambda e: e.activation(out=Wtot[:, :], in_=T[2][:, :].rearrange("p (c s) -> p c s", s=64)[:, :, 63],
                                                   func=AF.Exp, scale=-CDEC), reads=["T2"], writes=["Wtot"])
                P.op("act", lambda e: e.activation(out=T[3][:, :], in_=T[2][:, :], func=AF.Exp, scale=-CDEC), reads=["T2"], writes=["T3"])
                P.op("act", lambda e: e.activation(out=T[2][:, :], in_=T[2][:, :], func=AF.Exp, scale=CDEC), reads=["T2", "T3", "Wtot"], writes=["T2"])
                P.op("act", lambda e: e.activation(out=T[1][:, :], in_=T[1][:, :], func=AF.Exp, scale=-CDEC), reads=["T1"], writes=["T1"])
                P.op("act", lambda e: e.activation(out=T[0][:, :], in_=T[0][:, :], func=AF.Exp, scale=-CDEC), reads=["T0"], writes=["T0"])
                P.op("dve", lambda e: e.tensor_scalar(out=T[4][:, :], in0=xk[:, :], scalar1=vcol("rwkv_k_k", ch), scalar2=None, op0=ALU.mult),
                     reads=["xk", "vecs"], writes=["T4"])
                P.op("act", lambda e: e.activation(out=Bt[0][:, :], in_=T[4][:, :], func=AF.Square), reads=["T4"], writes=["B0"])
                b2 = nextps()
                P.op("pe", lambda e, b2=b2: e.matmul(ps[b2][:, :], lhsT=cb("bones", 128), rhs=Bt[0][:, :], start=True, stop=True),
                     reads=["B0", "cst"], writes=[("ps", b2)])
                P.op("dve", lambda e, b2=b2: e.tensor_scalar(out=T[5][:, :], in0=ps[b2][:, :], scalar1=1e-24, scalar2=None, op0=ALU.max),
                     reads=[("ps", b2)], writes=["T5"])
                P.op("act", lambda e: e.activation(out=T[5][:, :], in_=T[5][:, :], func=AF.Ln), reads=["T5"], writes=["T5"])
                P.op("act", lambda e: e.activation(out=T[5][:, :], in_=T[5][:, :], func=AF.Exp, scale=-0.5), reads=["T5"], writes=["T5"])
                P.op("dve", lambda e: e.tensor_tensor(out=T[4][:, :], in0=T[4][:, :], in1=T[5][:, :], op=ALU.mult),
                     reads=["T4", "T5"], writes=["T4"])
                b3 = nextps()
                P.op("pe", lambda e, b3=b3: e.matmul(ps[b3][:, :], lhsT=a2b[:, csl], rhs=xs_small[0:64, 1, :], start=True, stop=True),
                     reads=["a2b", ("xs_small", 1)], writes=[("ps", b3)])
                P.op("act", lambda e, b3=b3: e.activation(out=T[5][:, :], in_=ps[b3][:, :], func=AF.Sigmoid, bias=vcol("rwkv_a0", ch)),
                     reads=[("ps", b3), "vecs", "T5", "T4"], writes=["T5"])
                P.op("dve", lambda e: e.tensor_scalar(out=T[6][:, :], in0=T[5][:, :], scalar1=vcol("rwkv_k_a", ch), scalar2=omka[:, ch:ch + 1],
                                                      op0=ALU.mult, op1=ALU.add), reads=["T5", "vecs", "omka"], writes=["T6"])
                P.op("dve", lambda e: e.tensor_tensor(out=T[6][:, :], in0=T[6][:, :], in1=xk[:, :], op=ALU.mult), reads=["T6", "xk"], writes=["T6"])
                P.op("dve", lambda e: e.tensor_tensor(out=T[5][:, :], in0=T[5][:, :], in1=T[4][:, :], op=ALU.mult), reads=["T5", "T4"], writes=["T5"])
                if not pre:
                    b4 = nextps()

                    def gmm(e, b4=b4):
                        e.matmul(ps[b4][:, :], lhsT=g2b[:, 0, csl], rhs=xs_small[:, 2, :], start=True, stop=False)
                        return e.matmul(ps[b4][:, :], lhsT=g2b[0:32, 1, csl], rhs=xs_small[0:32, 3, :], start=False, stop=True)
                    P.op("pe", gmm, reads=["g2b", "g2b1", ("xs_small", 2), ("xs_small", 3)], writes=[("ps", b4)])
                    P.op("act", lambda e, b4=b4: e.activation(out=gbuf[:, :], in_=ps[b4][:, :], func=AF.Copy), reads=[("ps", b4)], writes=["gbuf"])
                v3 = lambda t_: t_[:, :].rearrange("p (c s) -> p c s", s=64)
                P.op("dve", lambda e: e.scalar_tensor_tensor(out=AR[:, :, 0, :], in0=v3(T[4]), scalar=-1.0, in1=v3(T[1]), op0=ALU.mult, op1=ALU.mult),
                     reads=["T4", "T1"], writes=["AR"])
                if not skip_r:
                    P.op("pool", lambda e: e.tensor_tensor(out=AR[:, :, 1, :], in0=v3(xr), in1=v3(T[3]), op=ALU.mult), reads=["xr", "T3"], writes=["AR1"])
                P.op("dve", lambda e: e.tensor_tensor(out=BK[:, :, 0, :], in0=v3(T[5]), in1=v3(T[2]), op=ALU.mult), reads=["T5", "T2"], writes=["BK"])
                P.op("pool", lambda e: e.tensor_tensor(out=BK[:, :, 1, :], in0=v3(T[6]), in1=v3(T[2]), op=ALU.mult), reads=["T6", "T2"], writes=["BK1"])
                P.op("dve", lambda e: e.tensor_tensor(out=TM[:, :, 1, :], in0=v3(T[5]), in1=v3(T[0]), op=ALU.mult), reads=["T5", "T0"], writes=["TM1"])
                P.op("pool", lambda e: e.tensor_tensor(out=TM[:, :, 2, :], in0=v3(T[6]), in1=v3(T[0]), op=ALU.mult), reads=["T6", "T0"], writes=["TM2"])
                P.op("act", lambda e: e.activation(out=TM[:, :, 0, :], in_=v3(xv), func=AF.Copy), reads=["xv"], writes=["TM0"])
                selB = cstb[:, c_off["ident"] + 64:c_off["ident"] + 128]
                for (X, XB, kx, kxb) in ((AR, ARB, ("AR", "AR1"), "ARB"), (BK, BKB, ("BK", "BK1"), "BKB")):
                    for hf4 in range(2):
                        bx = nextps()
                        P.op("pe", lambda e, X=X, hf4=hf4, bx=bx: e.matmul(
                            ps[bx][0:64, :], lhsT=selB, rhs=X[:, hf4 * 4:(hf4 + 1) * 4, :, :].rearrange("p c a s -> p (c a s)"),
                            start=True, stop=True), reads=list(kx) + ["cst"], writes=[("ps", bx)])
                        if hf4 == 0:
                            P.op("act", lambda e, XB=XB, bx=bx: e.activation(out=XB[:, 0:4, :, :].rearrange("p c a s -> p (c a s)"),
                                                                          in_=ps[bx][0:64, :], func=AF.Copy),
                                 reads=[("ps", bx)], writes=[kxb])
                        else:
                            P.op("dve", lambda e, XB=XB, bx=bx: e.tensor_copy(out=XB[:, 4:8, :, :].rearrange("p c a s -> p (c a s)"),
                                                                           in_=ps[bx][0:64, :]),
                                 reads=[("ps", bx)], writes=[kxb + "h"])
                P.op("dve", lambda e: e.tensor_copy(out=Whl[:, 0, :], in_=Wtot[:, :]), reads=["Wtot"], writes=["Whl"])
                P.op("dve", lambda e: e.tensor_tensor(out=Whl[:, 1, :], in0=Wtot[:, :], in1=Whl[:, 0, :], op=ALU.subtract),
                     reads=["Wtot", "Whl"], writes=["Whl"])
                bw = nextps()

                def wmm(e, bw=bw):
                    e.matmul(ps[bw][0:64, 0:8], lhsT=selB, rhs=Whl[:, 0, :], start=True, stop=False)
                    return e.matmul(ps[bw][0:64, 0:8], lhsT=selB, rhs=Whl[:, 1, :], start=False, stop=True)
                P.op("pe", wmm, reads=["Whl", "cst"], writes=[("ps", bw)])
                P.op("dve", lambda e, bw=bw: e.tensor_copy(out=WtotB[:, :], in_=ps[bw][0:64, 0:8]), reads=[("ps", bw)], writes=["WtotB"])
                ARx = (AR, ARB)
                BKx = (BK, BKB)
                XKEYS = ["AR", "AR1", "BK", "BK1", "ARB", "ARBh", "BKB", "BKBh"]
                if not pre:
                    P.op("dve", lambda e: e.tensor_tensor(out=T[4][:, :], in0=xr[:, :], in1=T[6][:, :], op=ALU.mult), reads=["xr", "T6", "T4", "AR"], writes=["T4"])
                    P.op("dve", lambda e: e.tensor_scalar(out=Bt[0][:, :], in0=T[4][:, :], scalar1=vcol("rwkv_r_k", ch), scalar2=None, op0=ALU.mult),
                         reads=["T4", "vecs"], writes=["B0"])
                    b5 = nextps()
                    P.op("pe", lambda e, b5=b5: e.matmul(ps[b5][:, :], lhsT=cb("bones", 128), rhs=Bt[0][:, :], start=True, stop=True),
                         reads=["B0", "cst"], writes=[("ps", b5)])
                    P.op("dve", lambda e, b5=b5: e.tensor_tensor(out=T[4][:, :], in0=ps[b5][:, :], in1=xv[:, :], op=ALU.mult),
                         reads=[("ps", b5), "xv"], writes=["T4"])
                if KSUB < 3:
                    continue
                K3 = int(os.environ.get("K3", "7"))
                for c2 in range(8):
                    if not (K3 & 1):
                        break
                    bt_ = nextps()

                    def trf(e, c2=c2, bt_=bt_):
                        inst = None
                        for k3 in range(3):
                            inst = e.matmul(ps[bt_][0:64, k3 * 128:(k3 + 1) * 128], lhsT=TM[:, c2, k3, :], rhs=cb("ident", 128), start=True, stop=True)
                        return inst
                    P.op("pe", trf, reads=["TM0", "TM1", "TM2", "cst"], writes=[("ps", bt_)])
                    P.op("dve", lambda e, c2=c2, bt_=bt_: e.tensor_copy(out=TMt[:, c2, :, :].rearrange("p k f -> p (k f)"), in_=ps[bt_][0:64, 0:384]),
                         reads=[("ps", bt_)], writes=["TMt"])
                for (dstL, kdst, slot) in ((LT, "LT", 0), (LTk, "LTk", 1)):
                    for q4 in range(4):
                        if not (K3 & 2):
                            break
                        bl = nextps()

                        def lmm(e, q4=q4, bl=bl, slot=slot):
                            inst = None
                            for ii in range(4):
                                pi = q4 * 4 + ii
                                c, hd = pi // 2, pi % 2
                                inst = e.matmul(ps[bl][0:64, ii * 128:(ii + 1) * 128], lhsT=BKx[hd][0:64, c, slot, :],
                                                rhs=ARx[hd][0:64, c, :, :].rearrange("p a s -> p (a s)"), start=True, stop=True)
                            return inst
                        P.op("pe", lmm, reads=XKEYS, writes=[("ps", bl)])
                        if not (K3 & 4):
                            continue
                        P.op("dve", lambda e, q4=q4, bl=bl, dstL=dstL: e.tensor_tensor(
                            out=dstL[:, q4 * 4:(q4 + 1) * 4, :].rearrange("p a s -> p (a s)"),
                            in0=ps[bl][0:64, :], in1=cf("mlt", 512, rows=64), op=ALU.mult),
                            reads=[("ps", bl), "cstf"], writes=[kdst])
                if KSUB < 4:
                    continue
                f2 = lambda t_, hlf: t_[:, hlf * 8:(hlf + 1) * 8, :].rearrange("p a s -> p (a s)")
                for hlf in range(2):
                    bl = nextps()

                    def l0mm(e, hlf=hlf, bl=bl):
                        inst = None
                        for ii in range(8):
                            pi = hlf * 8 + ii
                            c, hd = pi // 2, pi % 2
                            inst = e.matmul(ps[bl][0:64, ii * 64:(ii + 1) * 64], lhsT=ARx[hd][0:64, c, 0, :], rhs=BKx[hd][0:64, c, 0, :], start=True, stop=True)
                        return inst
                    P.op("pe", l0mm, reads=XKEYS, writes=[("ps", bl)])
                    P.op("dve", lambda e, hlf=hlf, bl=bl: e.tensor_tensor(out=f2(inv["L0"], hlf), in0=ps[bl][0:64, :],
                                                                          in1=cf("mlow", 512, rows=64), op=ALU.mult),
                         reads=[("ps", bl), "cstf"], writes=["inv_L0"])
                P.op("pool", lambda e: e.tensor_copy(out=inv["N0"][:, :, :], in_=LT[:, :, 0:64]), reads=["LT"], writes=["inv_N0"])
                for hlf in range(2):
                    P.op("pool", lambda e, hlf=hlf: e.tensor_tensor(out=f2(inv["Q"], hlf), in0=f2(inv["N0"], hlf), in1=cb("id64", 512, rows=64), op=ALU.add),
                         reads=["inv_N0", "cst"], writes=["inv_Q"])
                    P.op("pool", lambda e, hlf=hlf: e.tensor_tensor(out=f2(inv["P"], hlf), in0=f2(inv["L0"], hlf), in1=cb("id64", 512, rows=64), op=ALU.add),
                         reads=["inv_L0", "cst"], writes=["inv_P"])
                Qm, Pm = inv["Q"], inv["P"]

                def mm8(lt, rh, hlf, bq):
                    def f(e):
                        inst = None
                        for ii in range(8):
                            pi = hlf * 8 + ii
                            inst = e.matmul(ps[bq][0:64, ii * 64:(ii + 1) * 64], lhsT=lt[:, pi, :], rhs=rh[:, pi, :], start=True, stop=True)
                        return inst
                    return f

                def do_sq(cur, nxt):
                    Lc, Nc, Ln_, Nn = inv["L" + cur], inv["N" + cur], inv["L" + nxt], inv["N" + nxt]
                    kLc, kNc, kLn, kNn = "inv_L" + cur, "inv_N" + cur, "inv_L" + nxt, "inv_N" + nxt
                    for (dst, kdst, lt, rh) in ((Ln_, kLn, Nc, Lc), (Nn, kNn, Lc, Nc)):
                        for hlf in range(2):
                            bq = nextps()
                            P.op("pe", mm8(lt, rh, hlf, bq), reads=[kLc, kNc], writes=[("ps", bq)])
                            if hlf == 0:
                                P.op("act", lambda e, hlf=hlf, bq=bq, dst=dst: e.activation(out=f2(dst, hlf), in_=ps[bq][0:64, :], func=AF.Copy),
                                     reads=[("ps", bq)], writes=[kdst])
                            else:
                                P.op("dve", lambda e, hlf=hlf, bq=bq, dst=dst: e.tensor_copy(out=f2(dst, hlf), in_=ps[bq][0:64, :]),
                                     reads=[("ps", bq)], writes=[kdst])

                def do_upd(nxt, last):
                    Ln_, Nn = inv["L" + nxt], inv["N" + nxt]
                    kLn, kNn = "inv_L" + nxt, "inv_N" + nxt
                    pend = []
                    for (dst, kdst, lt, rh, krh) in ((Qm, "inv_Q", Pm, Nn, kNn), (Pm, "inv_P", Qm, Ln_, kLn)):
                        if last and dst is Pm:
                            continue
                        for hlf in range(2):
                            bq = nextps()
                            P.op("pe", mm8(lt, rh, hlf, bq), reads=["inv_Q", "inv_P", krh], writes=[("ps", bq)])
                            pend.append((dst, kdst, hlf, bq))
                    for (dst, kdst, hlf, bq) in pend:
                        P.op("dve", lambda e, hlf=hlf, bq=bq, dst=dst: e.tensor_tensor(out=f2(dst, hlf), in0=ps[bq][0:64, :], in1=f2(dst, hlf), op=ALU.add),
                             reads=[("ps", bq), kdst], writes=[kdst])
                bufs = ["0", "1"]
                do_sq(bufs[0], bufs[1])
                for lvl in range(5):
                    cur, nxt = bufs[lvl % 2], bufs[(lvl + 1) % 2]
                    if lvl < 4:
                        do_sq(nxt, cur)
                    do_upd(nxt, lvl == 4)
                if KSUB < 5:
                    continue
                TTm = Qm
                kTT = "inv_Q"
                sk = ("S0T", ch)
                for c in range(8):
                    bz = nextps()

                    def zmm(e, c=c, bz=bz):
                        inst = None
                        for hd in range(2):
                            o = ps[bz][0:64, hd * 64:(hd + 1) * 64]
                            e.matmul(o, lhsT=ARx[hd][0:64, c, 0, :], rhs=S0Tb[:, ch, hd, :], start=True, stop=False)
                            inst = e.matmul(o, lhsT=LTk[:, c * 2 + hd, 0:64], rhs=TMt[:, c, 0, hd * 64:(hd + 1) * 64], start=False, stop=True)
                        return inst
                    P.op("pe", zmm, reads=XKEYS + [sk, "LTk", "TMt"] + INITK, writes=[("ps", bz)])
                    P.op("act", lambda e, bz=bz: e.activation(out=Zs[:, :, :].rearrange("p a s -> p (a s)"), in_=ps[bz][0:64, 0:128], func=AF.Copy),
                         reads=[("ps", bz)], writes=["Zs"])
                    bu = nextps()

                    def umm(e, c=c, bu=bu):
                        inst = None
                        for hd in range(2):
                            inst = e.matmul(ps[bu][0:64, hd * 64:(hd + 1) * 64], lhsT=TTm[:, c * 2 + hd, :], rhs=Zs[:, hd, :], start=True, stop=True)
                        return inst
                    P.op("pe", umm, reads=[kTT, "Zs"], writes=[("ps", bu)])
                    P.op("act", lambda e, bu=bu: e.activation(out=UV[:, :, :].rearrange("p a s -> p (a s)"), in_=ps[bu][0:64, 0:128], func=AF.Copy),
                         reads=[("ps", bu)], writes=["UV"])

                    def ymm(e, c=c):
                        inst = None
                        for hd in range(2):
                            rows = slice(hd * 64, hd * 64 + 64)
                            o = ps[YB][rows, c * 64:(c + 1) * 64]
                            e.matmul(o, lhsT=S0Tb[:, ch, hd, :], rhs=ARx[hd][0:64, c, 1, :], start=True, stop=False)
                            e.matmul(o, lhsT=UV[:, hd, :], rhs=LT[:, c * 2 + hd, 64:128], start=False, stop=False)
                            inst = e.matmul(o, lhsT=TMt[:, c, 0, hd * 64:(hd + 1) * 64], rhs=LTk[:, c * 2 + hd, 64:128], start=False, stop=True)
                        return inst
                    if not pre:
                        P.op("pe", ymm, reads=XKEYS + [sk, "UV", "LT", "LTk", "TMt"], writes=[("ps", YB)])
                    bs = nextps()

                    def smm(e, c=c, bs=bs):
                        inst = None
                        for hd in range(2):
                            o = ps[bs][0:64, hd * 64:(hd + 1) * 64]
                            e.matmul(o, lhsT=TMt[:, c, 1, hd * 64:(hd + 1) * 64], rhs=UV[:, hd, :], start=True, stop=False)
                            inst = e.matmul(o, lhsT=TMt[:, c, 2, hd * 64:(hd + 1) * 64], rhs=TMt[:, c, 0, hd * 64:(hd + 1) * 64], start=False, stop=True)
                        return inst
                    P.op("pe", smm, reads=["TMt", "UV"], writes=[("ps", bs)])

                    def supd(e, c=c, bs=bs):
                        e.scalar_tensor_tensor(out=S0T[:, ch, 0, :], in0=S0T[:, ch, 0, :], scalar=Wtot[0:64, c:c + 1], in1=ps[bs][0:64, 0:64],
                                               op0=ALU.mult, op1=ALU.add)
                        return e.scalar_tensor_tensor(out=S0T[:, ch, 1, :], in0=S0T[:, ch, 1, :], scalar=WtotB[:, c:c + 1], in1=ps[bs][0:64, 64:128],
                                                      op0=ALU.mult, op1=ALU.add)
                    P.op("dve", supd, reads=[("ps", bs), "Wtot", "WtotB"] + INITK, writes=[("S0Tf", ch)])
                    P.op("dve", lambda e: e.tensor_copy(out=S0Tb[:, ch, :, :].rearrange("p a s -> p (a s)"),
                                                        in_=S0T[:, ch, :, :].rearrange("p a s -> p (a s)")),
                         reads=[("S0Tf", ch)], writes=[sk])
                if KSUB < 6 or pre:
                    continue
                P.op("act", lambda e: e.activation(out=ysb[:, :], in_=ps[YB][:, :], func=AF.Copy), reads=[("ps", YB), "T3"], writes=["T3"])
                P.op("dve", lambda e: e.tensor_copy(out=sqb[:, :], in_=ysb[:, :]), reads=["T3"], writes=["sqb"])
                bm = nextps()
                P.op("pe", lambda e, bm=bm: e.matmul(ps[bm][:, :], lhsT=cb("bones", 128), rhs=sqb[:, :], start=True, stop=True),
                     reads=["sqb", "cst"], writes=[("ps", bm)])
                P.op("dve", lambda e, bm=bm: e.scalar_tensor_tensor(out=ysb[:, :], in0=ps[bm][:, :], scalar=-1.0 / 64, in1=ysb[:, :], op0=ALU.mult, op1=ALU.add),
                     reads=[("ps", bm), "T3"], writes=["T3"])
                P.op("act", lambda e: e.activation(out=sqb[:, :], in_=ysb[:, :], func=AF.Square), reads=["T3", "sqb"], writes=["sqb"])
                bv = nextps()
                P.op("pe", lambda e, bv=bv: e.matmul(ps[bv][:, :], lhsT=cb("bones", 128), rhs=sqb[:, :], start=True, stop=True),
                     reads=["sqb", "cst"], writes=[("ps", bv)])
                P.op("act", lambda e, bv=bv: e.activation(out=T[5][:, :], in_=ps[bv][:, :], func=AF.Ln, bias=64e-5, scale=1.0 / 64),
                     reads=[("ps", bv), "T5"], writes=["T5"])
                P.op("act", lambda e: e.activation(out=T[5][:, :], in_=T[5][:, :], func=AF.Exp, scale=-0.5), reads=["T5"], writes=["T5"])
                P.op("dve", lambda e: e.tensor_tensor(out=ysb[:, :], in0=ysb[:, :], in1=T[5][:, :], op=ALU.mult), reads=["T3", "T5"], writes=["T3"])
                P.op("dve", lambda e: e.tensor_scalar(out=ysb[:, :], in0=ysb[:, :], scalar1=vcol("rwkv_ln_w", ch), scalar2=vcol("rwkv_ln_b", ch),
                                                      op0=ALU.mult, op1=ALU.add), reads=["T3", "vecs"], writes=["T3"])
                P.op("dve", lambda e: e.tensor_tensor(out=ysb[:, :], in0=ysb[:, :], in1=T[4][:, :], op=ALU.add), reads=["T3", "T4"], writes=["T3"])
                P.op("dve", lambda e: e.tensor_tensor(out=yr[:, ch, :], in0=ysb[:, :], in1=gbuf[:, :], op=ALU.mult), reads=["T3", "gbuf"], writes=[("yr", ch)])

        def mixer(tile_idx, STG=9, pre=False):
            rmsnorm("mix_norm")
            attention(tile_idx, pre)
            if STG < 3:
                return
            rwkv(tile_idx, pre)
            if STG < 4 or pre:
                return
            for m in range(16):
                hold = {}

                def ev_pa(idx, b, w):
                    hold["pa"] = b

                def ev_pr(idx, b, w):
                    hold["pr"] = b

                def ev_ga(idx, b, w):
                    hold["ga"] = b

                def ev_gr(idx, b, w):
                    hold["gr"] = b
                proj("w_pa", 8, lambda kc: qT[:, kc, :], [("qT", i) for i in range(8)], [(m * 128, 128)], ev_pa)
                proj("w_pr", 8, lambda kc: yr[:, kc, :], [("yr", i) for i in range(8)], [(m * 128, 128)], ev_pr)
                proj("w_in", NKC, lambda kc: hT[:, kc, :], ["hT"], [(4896 + m * 128, 128)], ev_ga)
                proj("w_in", NKC, lambda kc: hT[:, kc, :], ["hT"], [(6944 + m * 128, 128)], ev_gr)
                pa, pr, ga, gr = hold["pa"], hold["pr"], hold["ga"], hold["gr"]
                P.op("act", lambda e, ga=ga: e.activation(out=Tt[0][:, :], in_=ps[ga][:, :], func=AF.Sigmoid), reads=[("ps", ga)], writes=["T0"])
                P.op("act", lambda e, gr=gr: e.activation(out=Tt[1][:, :], in_=ps[gr][:, :], func=AF.Sigmoid), reads=[("ps", gr)], writes=["T1"])
                P.op("dve", lambda e, pa=pa: e.tensor_tensor(out=Tt[0][:, :], in0=ps[pa][:, :], in1=Tt[0][:, :], op=ALU.mult), reads=[("ps", pa), "T0"], writes=["T0"])
                P.op("dve", lambda e, pr=pr: e.tensor_tensor(out=Tt[1][:, :], in0=ps[pr][:, :], in1=Tt[1][:, :], op=ALU.mult), reads=[("ps", pr), "T1"], writes=["T1"])
                P.op("dve", lambda e, m=m: e.tensor_tensor(out=act[:, m, :], in0=Tt[0][:, :], in1=Tt[1][:, :], op=ALU.add), reads=["T0", "T1"], writes=[("act", m)])

            def ev_out(idx, b, w):
                P.op("dve", lambda e: e.tensor_tensor(out=xT[:, idx, :], in0=ps[b][:, :], in1=xT[:, idx, :], op=ALU.add),
                     reads=[("ps", b), "xT"], writes=["xT"])
            proj("w_out", NKC, lambda kc: act[:, kc, :], [("act", j) for j in range(16)], [(m * 128, 128) for m in range(16)], ev_out)

        def cross():
            rmsnorm("cross_norm")

            def ev_q(idx, b, w):
                P.op("act", lambda e: e.activation(out=qT[:, idx, :], in_=ps[b][:, :], func=AF.Copy), reads=[("ps", b)], writes=[("qT", idx)])
            proj("w_cq", NKC, lambda kc: hT[:, kc, :], ["hT"], [(h * 128, 128) for h in range(4)], ev_q)
            sc = float(128 ** -0.5)
            for h in range(4):
                Eb = ET[h % 2]
                ekey = "T%d" % (h % 2)
                b0, b1 = nextps(), nextps()

                def smm(e, h=h, b0=b0, b1=b1):
                    e.matmul(ps[b0][:, :], lhsT=kmT[:, h, 0:128], rhs=qT[:, h, :], start=True, stop=True)
                    return e.matmul(ps[b1][:, :], lhsT=kmT[:, h, 128:256], rhs=qT[:, h, :], start=True, stop=True)
                P.op("pe", smm, reads=["kmT", ("qT", h)], writes=[("ps", b0), ("ps", b1)])
                P.op("act", lambda e, Eb=Eb, b0=b0: e.activation(out=Eb[:, 0, :], in_=ps[b0][:, :], func=AF.Exp, scale=sc), reads=[("ps", b0)], writes=[ekey])
                P.op("act", lambda e, Eb=Eb, b1=b1: e.activation(out=Eb[:, 1, :], in_=ps[b1][:, :], func=AF.Exp, scale=sc), reads=[("ps", b1)], writes=[ekey])
                bo, bd = nextps(), nextps()

                def omm(e, h=h, Eb=Eb, bo=bo, bd=bd):
                    e.matmul(ps[bo][:, :], lhsT=vm[:, 0, h * 128:(h + 1) * 128], rhs=Eb[:, 0, :], start=True, stop=False)
                    e.matmul(ps[bo][:, :], lhsT=vm[:, 1, h * 128:(h + 1) * 128], rhs=Eb[:, 1, :], start=False, stop=True)
                    e.matmul(ps[bd][:, :], lhsT=cb("ones", 128), rhs=Eb[:, 0, :], start=True, stop=False)
                    return e.matmul(ps[bd][:, :], lhsT=cb("ones", 128), rhs=Eb[:, 1, :], start=False, stop=True)
                P.op("pe", omm, reads=[ekey, "vm", "cst"], writes=[("ps", bo), ("ps", bd)])
                P.op("dve", lambda e, bd=bd: e.reciprocal(out=den[:, :], in_=ps[bd][:, :]), reads=[("ps", bd)], writes=["T5"])
                P.op("dve", lambda e, bo=bo, h=h: e.tensor_tensor(out=oT[:, h, :], in0=ps[bo][:, :], in1=den[:, :], op=ALU.mult),
                     reads=[("ps", bo), "T5"], writes=[("yr", h)])

            def ev_o(idx, b, w):
                P.op("dve", lambda e: e.tensor_tensor(out=xT[:, idx, :], in0=ps[b][:, :], in1=xT[:, idx, :], op=ALU.add),
                     reads=[("ps", b), "xT"], writes=["xT"])
            proj("w_co", 4, lambda kc: oT[:, kc, :], [("yr", h) for h in range(4)], [(m * 128, 128) for m in range(16)], ev_o)

        def final_norm_store(t):
            P.op("act", lambda e: e.activation(out=hT[:, :, :], in_=xT[:, :, :], func=AF.Square), reads=["xT"], writes=["hT"])
            b = nextps()

            def mm(e):
                inst = None
                for kc in range(NKC):
                    inst = e.matmul(ps[b][:, :], lhsT=cb("ones", 128), rhs=hT[:, kc, :], start=(kc == 0), stop=(kc == NKC - 1))
                return inst
            P.op("pe", mm, reads=["hT", "cst"], writes=[("ps", b)])
            P.op("act", lambda e: e.activation(out=rstd[:, :], in_=ps[b][:, :], func=AF.Ln, bias=1e-6, scale=1.0 / D), reads=[("ps", b)], writes=["rstd"])
            P.op("act", lambda e: e.activation(out=rstd[:, :], in_=rstd[:, :], func=AF.Exp, scale=-0.5), reads=["rstd"], writes=["rstd"])

            def sc(e):
                inst = None
                for kc in range(NKC):
                    inst = e.scalar_tensor_tensor(out=xT[:, kc, :], in0=xT[:, kc, :], scalar=vcol("final_norm", kc), in1=rstd[:, :],
                                                  op0=ALU.mult, op1=ALU.mult)
                return inst
            P.op("dve", sc, reads=["xT", "rstd", "vecs"], writes=["xT"])
            P.op("pool", lambda e: e.dma_start(out=out_d[:, t * TT:(t + 1) * TT].rearrange("(k p) n -> p k n", p=128), in_=xT[:, :, :]),
                 reads=["xT"], writes=["outd"], dma="out")

        import os
        STG = int(os.environ.get("KSTAGE", "9"))
        mem_kv()
        P.barrier()
        for t in range(ntiles):
            P.epoch = 1 + t // 4
            P.op("pool", lambda e, t=t: e.dma_start(out=xT[:, :, :], in_=xT_d[:, t * TT:(t + 1) * TT].rearrange("(k p) n -> p k n", p=128)),
                 writes=["xT"], dma="xin")
            if STG >= 1:
                ffn("ffn1", "ffn1_norm")
            P.barrier()
            pre = t < npre
            if STG >= 2:
                mixer(t, STG, pre)
            P.barrier()
            if pre:
                continue
            if STG >= 5:
                cross()
            if STG >= 6:
                ffn("ffn2", "ffn2_norm")
            final_norm_store(t - npre)
            P.barrier()
        P.finalize(nc, stack, None, ["out"])
    return nc


_CACHE = {}


def kernel(**inp):
    import os
    x = np.asarray(inp["x"], np.float32)
    B, T, _ = x.shape
    ntiles = T // TT
    npre = ntiles // 2
    TH = npre * TT
    V = build_vecs(inp)
    vecs = V.build()
    consts2 = [build_consts(h) for h in (0, 1)]
    c_off = consts2[0][1]
    key = (ntiles, vecs.shape[1])
    if key not in _CACHE:
        _CACHE[key] = build_program(ntiles, V.off, vecs.shape[1], c_off, consts2[0][0].shape[1], npre=npre)
    nc = _CACHE[key]
    f = lambda a: np.ascontiguousarray(np.asarray(a, np.float32))
    w_in = f(inp["w_in"][0]).copy()
    w_in[:, 0:1024] = w_in[:, Q_PERM]
    shared = {
        "vecs": vecs,
        "ffn1_wg": f(inp["ffn1_w_gate"][0]), "ffn1_wu": f(inp["ffn1_w_up"][0]), "ffn1_wd": f(inp["ffn1_w_down"][0]),
        "w_in": w_in, "w_pa": f(np.asarray(inp["w_proj_attn"][0])[Q_PERM, :]), "w_pr": f(inp["w_proj_rwkv"][0]),
        "w_out": f(inp["w_out"][0]), "w_cq": f(inp["w_cross_q"][0]), "w_ckv": f(inp["w_cross_kv"][0]),
        "w_co": f(inp["w_cross_o"][0]),
        "ffn2_wg": f(inp["ffn2_w_gate"][0]), "ffn2_wu": f(inp["ffn2_w_up"][0]), "ffn2_wd": f(inp["ffn2_w_down"][0]),
        "w2": f(inp["rwkv_w2"][0]), "a2": f(inp["rwkv_a2"][0]), "g2": f(inp["rwkv_g2"][0]),
    }
    mem = np.asarray(inp["mem"], np.float32)
    ncores = int(os.environ.get("KCORES", "8"))
    in_maps = []
    for c in range(ncores):
        b, half = (c // 2) % B, c % 2
        m = dict(shared)
        xt = np.zeros((D, T), np.float32)
        if half == 0:
            xt[:, TH:] = x[b, 0:TH].T
        else:
            xt[:, :] = x[b].T
        m["xT"] = xt
        m["memT"] = np.ascontiguousarray(mem[b].T)
        m["consts"] = consts2[half][0]
        in_maps.append(m)
    if os.environ.get("KTRACE"):
        res = run_bass_kernel_spmd(nc, in_maps, core_ids=list(range(ncores)), trace=True)
        print("EXEC_TIME_NS", res.exec_time_ns)
    else:
        res = run_bass_kernel_spmd(nc, in_maps, core_ids=list(range(ncores)))
    out = np.zeros((B, T, D), np.float32)
    for c in range(ncores):
        b, half = (c // 2) % B, c % 2
        out[b, half * TH:(half + 1) * TH] = res.results[c]["outT"].T
    return out
```
